# Optimizing a Trainium2 kernel written in Bass

```python
import numpy as np
import jax
import jax.numpy as jnp
from jax import lax

D_MODEL = 2048
BATCH = 4
SEQ = 4096
DEPTH = 2

HEAD_DIM = 128
N_HEADS = D_MODEL // HEAD_DIM
N_HEADS_FOX = N_HEADS // 2
N_HEADS_DIL = N_HEADS - N_HEADS_FOX
DILATED_PATTERNS = ((128, 1), (512, 4), (2048, 16))
N_HEADS_NSA = N_HEADS
N_KV_NSA = 4
CMP_BLOCK = 32
CMP_STRIDE = 16
SLC_BLOCK = 64
N_SELECT = 16
WINDOW_NSA = 512
Q_BLOCK = 128
NSA_Q_BLOCK = 32
ROPE_THETA = 500000.0
ROPE_DIM = HEAD_DIM // 4
D_FF = 4 * D_MODEL
EPS = 1e-6
NEG = -1e30
FORCED_SCORE = 1e9
SCALE = HEAD_DIM ** -0.5
N_EVEN = (DEPTH + 1) // 2
N_ODD = DEPTH // 2

FOX_W = N_HEADS_FOX * HEAD_DIM
DIL_W = N_HEADS_DIL * HEAD_DIM
EVEN_IN = 4 * FOX_W + N_HEADS_FOX + 3 * DIL_W
Q_NSA_W = N_HEADS_NSA * HEAD_DIM
KV_NSA_W = N_KV_NSA * HEAD_DIM
ODD_IN = Q_NSA_W + 6 * KV_NSA_W + 3 * N_HEADS_NSA

kernel_name = 'hybrid_fox_dilated_nsa_trunk'


def rms_norm(x, g):
    xf = x.astype(jnp.float32)
    y = xf * lax.rsqrt(jnp.mean(xf * xf, axis=-1, keepdims=True) + EPS)
    return (y * g.astype(jnp.float32)).astype(x.dtype)


def rope_tables(pos):
    inv = 1.0 / (ROPE_THETA ** (jnp.arange(0, ROPE_DIM, 2, dtype=jnp.float32) / ROPE_DIM))
    ang = pos.astype(jnp.float32)[..., None] * inv
    return jnp.cos(ang), jnp.sin(ang)


def partial_rope(x, cos, sin):
    half = ROPE_DIM // 2
    x1 = x[..., :half]
    x2 = x[..., half:ROPE_DIM]
    c = cos.astype(x.dtype)
    s = sin.astype(x.dtype)
    return jnp.concatenate([x1 * c - x2 * s, x2 * c + x1 * s, x[..., ROPE_DIM:]], axis=-1)


def masked_softmax(s, mask):
    s = jnp.where(mask, s.astype(jnp.float32), NEG)
    m = jnp.max(s, axis=-1, keepdims=True)
    e = jnp.where(mask, jnp.exp(s - m), 0.0)
    return e / jnp.maximum(jnp.sum(e, axis=-1, keepdims=True), 1e-30)


def sq_relu_mlp(h, w_up, w_down):
    return jnp.square(jax.nn.relu(h @ w_up)) @ w_down


def fox_attention(q, k, v, log_f):
    B, H, S, Dh = q.shape
    c = jnp.cumsum(log_f, axis=-1)
    nb = S // Q_BLOCK
    qb = jnp.moveaxis(q.reshape(B, H, nb, Q_BLOCK, Dh), 2, 0)
    cb = jnp.moveaxis(c.reshape(B, H, nb, Q_BLOCK), 2, 0)
    kpos = jnp.arange(S)

    def block(args):
        i, qi, ci = args
        s = jnp.einsum('bhqd,bhkd->bhqk', qi, k).astype(jnp.float32) * SCALE
        s = s + ci[..., :, None] - c[..., None, :]
        qpos = i * Q_BLOCK + jnp.arange(Q_BLOCK)
        p = masked_softmax(s, kpos[None, :] <= qpos[:, None])
        return jnp.einsum('bhqk,bhkd->bhqd', p, v)

    o = lax.map(block, (jnp.arange(nb), qb, cb))
    return jnp.moveaxis(o, 0, 2).reshape(B, H, S, Dh)


def dilated_pattern(q, k, v, dilation, steps):
    B, H, S, Dh = q.shape
    L = S // dilation
    qb = min(Q_BLOCK, L)
    nb = -(-L // qb)
    pad_r = nb * qb - L

    def to_sub(t):
        return t.reshape(B, H, L, dilation, Dh).transpose(0, 1, 3, 2, 4)

    pad_q = ((0, 0), (0, 0), (0, 0), (0, pad_r), (0, 0))
    pad_k = ((0, 0), (0, 0), (0, 0), (qb, pad_r), (0, 0))
    qs = jnp.pad(to_sub(q), pad_q).reshape(B, H, dilation, nb, qb, Dh)
    kp = jnp.pad(to_sub(k), pad_k).reshape(B, H, dilation, nb + 1, qb, Dh)
    vp = jnp.pad(to_sub(v), pad_k).reshape(B, H, dilation, nb + 1, qb, Dh)
    kw = jnp.concatenate([kp[:, :, :, :-1], kp[:, :, :, 1:]], axis=4)
    vw = jnp.concatenate([vp[:, :, :, :-1], vp[:, :, :, 1:]], axis=4)
    s = jnp.einsum('bhrnqd,bhrnkd->bhrnqk', qs, kw).astype(jnp.float32) * SCALE
    qm = (jnp.arange(nb) * qb)[:, None] + jnp.arange(qb)[None, :]
    km = (jnp.arange(nb) * qb - qb)[:, None] + jnp.arange(2 * qb)[None, :]
    dist = qm[:, :, None] - km[:, None, :]
    mask = (dist >= 0) & (dist <= steps) & (km[:, None, :] >= 0)
    s = jnp.where(mask, s, NEG)
    m = jnp.max(s, axis=-1)
    e = jnp.where(mask, jnp.exp(s - m[..., None]), 0.0)
    den = jnp.sum(e, axis=-1)
    num = jnp.einsum('bhrnqk,bhrnkd->bhrnqd', e, vw.astype(jnp.float32))

    def from_sub(t):
        rest = t.shape[5:]
        t = t.reshape((B, H, dilation, nb * qb) + rest)[:, :, :, :L]
        t = jnp.moveaxis(t, 2, 3)
        return t.reshape((B, H, S) + rest)

    return from_sub(m), from_sub(den), from_sub(num)


def dilated_attention(q, k, v):
    parts = [dilated_pattern(q, k, v, d, w // d) for (w, d) in DILATED_PATTERNS]
    m_all = jnp.max(jnp.stack([p[0] for p in parts]), axis=0)
    den = None
    num = None
    for m, l, n in parts:
        w = jnp.exp(m - m_all)
        den = w * l if den is None else den + w * l
        num = w[..., None] * n if num is None else num + w[..., None] * n
    return num / den[..., None]


def even_mixer(h, w_in, b_f, w_out, g_q_fox, g_k_fox, g_q_dil, g_k_dil, cos, sin):
    B, S, _ = h.shape
    proj = h @ w_in
    splits = np.cumsum([FOX_W, FOX_W, FOX_W, FOX_W, N_HEADS_FOX, DIL_W, DIL_W]).tolist()
    qa, ka, va, ga, fa, qd, kd, vd = jnp.split(proj, splits, axis=-1)

    def heads(t, n):
        return t.reshape(B, S, n, HEAD_DIM)

    def bhsd(t):
        return t.transpose(0, 2, 1, 3)

    qa = rms_norm(heads(qa, N_HEADS_FOX), g_q_fox)
    ka = rms_norm(heads(ka, N_HEADS_FOX), g_k_fox)
    log_f = jax.nn.log_sigmoid((fa + b_f).astype(jnp.float32))
    o_a = fox_attention(bhsd(qa), bhsd(ka), bhsd(heads(va, N_HEADS_FOX)), log_f.transpose(0, 2, 1))
    o_a = bhsd(o_a).astype(h.dtype) * jax.nn.sigmoid(heads(ga, N_HEADS_FOX))
    c3, s3 = cos[:, None, :], sin[:, None, :]
    qd = partial_rope(rms_norm(heads(qd, N_HEADS_DIL), g_q_dil), c3, s3)
    kd = partial_rope(rms_norm(heads(kd, N_HEADS_DIL), g_k_dil), c3, s3)
    o_b = dilated_attention(bhsd(qd), bhsd(kd), bhsd(heads(vd, N_HEADS_DIL)))
    o_b = bhsd(o_b).astype(h.dtype)
    o = jnp.concatenate([o_a.reshape(B, S, FOX_W), o_b.reshape(B, S, DIL_W)], axis=-1)
    return o @ w_out


def compress(x_tok, pe, w1, w2):
    S = x_tok.shape[2]
    n_cmp = (S - CMP_BLOCK) // CMP_STRIDE + 1
    idx = np.arange(n_cmp)[:, None] * CMP_STRIDE + np.arange(CMP_BLOCK)[None, :]
    blk = x_tok[:, :, idx] + pe
    flat = blk.reshape(blk.shape[:3] + (CMP_BLOCK * HEAD_DIM,))
    return jax.nn.gelu(flat @ w1) @ w2


def cmp_slc_overlap(n_cmp, n_slc):
    start = np.arange(n_cmp) * CMP_STRIDE
    js = np.arange(n_slc) * SLC_BLOCK
    ov = (start[:, None] < js[None, :] + SLC_BLOCK) & (start[:, None] + CMP_BLOCK > js[None, :])
    return ov.astype(np.float32)


def nsa_attention(q, kc, vc, k_slc, v_slc, k_win, v_win, gates):
    B, G, HPG, S, Dh = q.shape
    n_cmp = kc.shape[2]
    n_slc = S // SLC_BLOCK
    n_sel = min(N_SELECT, n_slc)
    qb = NSA_Q_BLOCK
    nb = S // qb
    cmp_end = jnp.arange(n_cmp) * CMP_STRIDE + CMP_BLOCK - 1
    overlap = jnp.asarray(cmp_slc_overlap(n_cmp, n_slc))
    ks_blocks = k_slc.reshape(B, G, n_slc, SLC_BLOCK, Dh)
    vs_blocks = v_slc.reshape(B, G, n_slc, SLC_BLOCK, Dh)
    kw_pad = jnp.pad(k_win, ((0, 0), (0, 0), (WINDOW_NSA, 0), (0, 0)))
    vw_pad = jnp.pad(v_win, ((0, 0), (0, 0), (WINDOW_NSA, 0), (0, 0)))
    b_ix = jnp.arange(B)[:, None, None, None]
    g_ix = jnp.arange(G)[None, :, None, None]
    blk_ix = jnp.arange(n_slc)
    q_st = jnp.moveaxis(q.reshape(B, G, HPG, nb, qb, Dh), 3, 0)
    g_st = jnp.moveaxis(gates.reshape(B, G, HPG, nb, qb, 3), 3, 0)

    def block(args):
        i, qi, gi = args
        qpos = i * qb + jnp.arange(qb)
        s_c = jnp.einsum('bghqd,bgnd->bghqn', qi, kc).astype(jnp.float32) * SCALE
        p_c = masked_softmax(s_c, cmp_end[None, :] <= qpos[:, None])
        o_c = jnp.einsum('bghqn,bgnd->bghqd', p_c, vc)
        imp = jnp.einsum('bghqn,nj->bgqj', p_c, overlap)
        cur = (qpos // SLC_BLOCK)[:, None]
        forced = (blk_ix == 0) | (blk_ix == cur) | (blk_ix == cur - 1)
        imp = jnp.where(forced, FORCED_SCORE, imp)
        imp = jnp.where(blk_ix <= cur, imp, NEG)
        top_val, top_idx = lax.top_k(imp, n_sel)
        k_sel = ks_blocks[b_ix, g_ix, top_idx].reshape(B, G, qb, n_sel * SLC_BLOCK, Dh)
        v_sel = vs_blocks[b_ix, g_ix, top_idx].reshape(B, G, qb, n_sel * SLC_BLOCK, Dh)
        tok = (top_idx[..., None] * SLC_BLOCK + jnp.arange(SLC_BLOCK)).reshape(B, G, qb, n_sel * SLC_BLOCK)
        ok = jnp.repeat(top_val > 0.5 * NEG, SLC_BLOCK, axis=-1) & (tok <= qpos[:, None])
        s_s = jnp.einsum('bghqd,bgqkd->bghqk', qi, k_sel).astype(jnp.float32) * SCALE
        p_s = masked_softmax(s_s, ok[:, :, None])
        o_s = jnp.einsum('bghqk,bgqkd->bghqd', p_s, v_sel)
        kw = lax.dynamic_slice_in_dim(kw_pad, i * qb, qb + WINDOW_NSA, axis=2)
        vw = lax.dynamic_slice_in_dim(vw_pad, i * qb, qb + WINDOW_NSA, axis=2)
        kpos = i * qb - WINDOW_NSA + jnp.arange(qb + WINDOW_NSA)
        dist = qpos[:, None] - kpos[None, :]
        mask_w = (dist >= 0) & (dist < WINDOW_NSA) & (kpos[None, :] >= 0)
        s_w = jnp.einsum('bghqd,bgkd->bghqk', qi, kw).astype(jnp.float32) * SCALE
        p_w = masked_softmax(s_w, mask_w)
        o_w = jnp.einsum('bghqk,bgkd->bghqd', p_w, vw)
        gi = gi.astype(jnp.float32)
        return gi[..., 0:1] * o_c + gi[..., 1:2] * o_s + gi[..., 2:3] * o_w

    o = lax.map(block, (jnp.arange(nb), q_st, g_st))
    return jnp.moveaxis(o, 0, 3).reshape(B, G, HPG, S, Dh)


def odd_mixer(h, w_in, w_out, k_pe, k_w1, k_w2, v_pe, v_w1, v_w2, g_q, g_kc, g_ks, g_kw, cos, sin):
    B, S, _ = h.shape
    G, HPG = N_KV_NSA, N_HEADS_NSA // N_KV_NSA
    proj = h @ w_in
    splits = np.cumsum([Q_NSA_W] + [KV_NSA_W] * 6).tolist()
    q, kc, vc, ks, vs, kw, vw, gt = jnp.split(proj, splits, axis=-1)
    c3, s3 = cos[:, None, :], sin[:, None, :]
    q = partial_rope(rms_norm(q.reshape(B, S, N_HEADS_NSA, HEAD_DIM), g_q), c3, s3)
    q = q.reshape(B, S, G, HPG, HEAD_DIM).transpose(0, 2, 3, 1, 4)

    def kv(t):
        return t.reshape(B, S, G, HEAD_DIM)

    def to_bg(t):
        return t.transpose(0, 2, 1, 3)

    ks = to_bg(partial_rope(rms_norm(kv(ks), g_ks), c3, s3))
    kw = to_bg(partial_rope(rms_norm(kv(kw), g_kw), c3, s3))
    vs = to_bg(kv(vs))
    vw = to_bg(kv(vw))
    n_cmp = (S - CMP_BLOCK) // CMP_STRIDE + 1
    cos_c, sin_c = rope_tables(jnp.arange(n_cmp) * CMP_STRIDE + CMP_BLOCK - 1)
    kc = partial_rope(rms_norm(compress(to_bg(kv(kc)), k_pe, k_w1, k_w2), g_kc), cos_c, sin_c)
    vc = compress(to_bg(kv(vc)), v_pe, v_w1, v_w2)
    gates = jax.nn.sigmoid(gt.reshape(B, S, G, HPG, 3)).transpose(0, 2, 3, 1, 4)
    o = nsa_attention(q, kc, vc, ks, vs, kw, vw, gates)
    o = o.transpose(0, 3, 1, 2, 4).reshape(B, S, Q_NSA_W).astype(h.dtype)
    return o @ w_out


def setup_inputs(seed: int = 0) -> dict:
    key = jax.random.key(seed)
    keys = iter(jax.random.split(key, 40))

    def nrm(shape, scale):
        return jax.random.normal(next(keys), shape, jnp.float32) * scale

    def gain(shape):
        return 1.0 + 0.02 * jax.random.normal(next(keys), shape, jnp.float32)

    D = D_MODEL
    return {
        'x': nrm((BATCH, SEQ, D), 1.0),
        'ln_mix_g': gain((DEPTH, D)),
        'ln_mlp_g': gain((DEPTH, D)),
        'w_mlp_up': nrm((DEPTH, D, D_FF), D ** -0.5),
        'w_mlp_down': nrm((DEPTH, D_FF, D), D_FF ** -0.5),
        'even_w_in': nrm((N_EVEN, D, EVEN_IN), D ** -0.5),
        'even_b_f': nrm((N_EVEN, N_HEADS_FOX), 0.1),
        'even_w_out': nrm((N_EVEN, FOX_W + DIL_W, D), (FOX_W + DIL_W) ** -0.5),
        'even_g_q_fox': gain((N_EVEN, HEAD_DIM)),
        'even_g_k_fox': gain((N_EVEN, HEAD_DIM)),
        'even_g_q_dil': gain((N_EVEN, HEAD_DIM)),
        'even_g_k_dil': gain((N_EVEN, HEAD_DIM)),
        'odd_w_in': nrm((N_ODD, D, ODD_IN), D ** -0.5),
        'odd_w_out': nrm((N_ODD, Q_NSA_W, D), Q_NSA_W ** -0.5),
        'odd_phi_k_pe': nrm((N_ODD, CMP_BLOCK, HEAD_DIM), 0.1),
        'odd_phi_k_w1': nrm((N_ODD, CMP_BLOCK * HEAD_DIM, HEAD_DIM), (CMP_BLOCK * HEAD_DIM) ** -0.5),
        'odd_phi_k_w2': nrm((N_ODD, HEAD_DIM, HEAD_DIM), HEAD_DIM ** -0.5),
        'odd_phi_v_pe': nrm((N_ODD, CMP_BLOCK, HEAD_DIM), 0.1),
        'odd_phi_v_w1': nrm((N_ODD, CMP_BLOCK * HEAD_DIM, HEAD_DIM), (CMP_BLOCK * HEAD_DIM) ** -0.5),
        'odd_phi_v_w2': nrm((N_ODD, HEAD_DIM, HEAD_DIM), HEAD_DIM ** -0.5),
        'odd_g_q': gain((N_ODD, HEAD_DIM)),
        'odd_g_kc': gain((N_ODD, HEAD_DIM)),
        'odd_g_ks': gain((N_ODD, HEAD_DIM)),
        'odd_g_kw': gain((N_ODD, HEAD_DIM)),
    }


def reference(x, ln_mix_g, ln_mlp_g, w_mlp_up, w_mlp_down, even_w_in, even_b_f, even_w_out,
              even_g_q_fox, even_g_k_fox, even_g_q_dil, even_g_k_dil, odd_w_in, odd_w_out,
              odd_phi_k_pe, odd_phi_k_w1, odd_phi_k_w2, odd_phi_v_pe, odd_phi_v_w1, odd_phi_v_w2,
              odd_g_q, odd_g_kc, odd_g_ks, odd_g_kw):
    S = x.shape[1]
    cos, sin = rope_tables(jnp.arange(S))
    h = x
    for layer in range(DEPTH):
        hn = rms_norm(h, ln_mix_g[layer])
        if layer % 2 == 0:
            i = layer // 2
            h = h + even_mixer(hn, even_w_in[i], even_b_f[i], even_w_out[i], even_g_q_fox[i],
                               even_g_k_fox[i], even_g_q_dil[i], even_g_k_dil[i], cos, sin)
        else:
            i = layer // 2
            h = h + odd_mixer(hn, odd_w_in[i], odd_w_out[i], odd_phi_k_pe[i], odd_phi_k_w1[i],
                              odd_phi_k_w2[i], odd_phi_v_pe[i], odd_phi_v_w1[i], odd_phi_v_w2[i],
                              odd_g_q[i], odd_g_kc[i], odd_g_ks[i], odd_g_kw[i], cos, sin)
        hn = rms_norm(h, ln_mlp_g[layer])
        h = h + sq_relu_mlp(hn, w_mlp_up[layer], w_mlp_down[layer])
    return h
```

```python
import numpy as np, time, sys, os
from contextlib import ExitStack
from concourse.bass_utils import run_bass_kernel_spmd
import numpy as np
import concourse.bass as bass
import concourse.mybir as mybir

F32 = mybir.dt.float32
BF16 = mybir.dt.bfloat16
AF = mybir.ActivationFunctionType
ALU = mybir.AluOpType
AX = mybir.AxisListType

ENGS = ['pe', 'act', 'dve', 'pool', 'sp']


class _Op:
    __slots__ = ('eng', 'fn', 'deps', 'dma', 'needed', 'sem', 'val', 'strict')


class Prog:
    def __init__(self, nc, n_slots=8, same_engine_sync=False):
        self.nc = nc
        self.same = same_engine_sync
        self.n_slots = n_slots
        self.sems = {}
        self.cnt = {e: 0 for e in ENGS}
        self.dma_cum = {}
        self.dma_last = {}
        self.dma_rr = {e: 0 for e in ENGS}
        self.waited = {e: {} for e in ENGS}
        self._stack = []
        for e in ['pe', 'act', 'dve', 'pool']:
            self.sems[e] = nc.alloc_semaphore(name=f"s_{e}")
        for q in ['sp', 'pool', 'act']:
            for s in range(n_slots):
                k = (q, s)
                self.sems[k] = nc.alloc_semaphore(name=f"d_{q}{s}")
                self.dma_cum[k] = 0
                self.dma_last[k] = None
        self.reset_stage()

    def reset_stage(self):
        self.ops = []
        self.last_w = {}
        self.readers = {}

    def _eng(self, e):
        nc = self.nc
        return {'pe': nc.tensor, 'act': nc.scalar, 'dve': nc.vector, 'pool': nc.gpsimd, 'sp': nc.sync}[e]

    def op(self, eng, fn, reads=(), writes=(), dma=False, strict=False):
        o = _Op()
        o.eng = eng; o.fn = fn; o.dma = dma; o.needed = False; o.sem = None; o.val = None; o.strict = strict
        deps = []
        for k in reads:
            w = self.last_w.get(k)
            if w is not None:
                deps.append(w)
        for k in writes:
            w = self.last_w.get(k)
            if w is not None:
                deps.append(w)
            deps.extend(self.readers.get(k, ()))
        if dma:
            slot = (eng, self.dma_rr[eng] % self.n_slots)
            self.dma_rr[eng] += 1
            prev = self.dma_last[slot]
            if prev is not None:
                deps.append(prev)
            self.dma_last[slot] = o
            self.dma_cum[slot] += 16
            o.sem = slot
            o.val = self.dma_cum[slot]
        o.deps = deps
        for k in reads:
            self.readers.setdefault(k, []).append(o)
        for k in writes:
            self.last_w[k] = o
            self.readers[k] = []
        self.ops.append(o)
        return o

    def dma(self, q, out, in_, reads=(), writes=(), **kw):
        return self.op(q, lambda e: e.dma_start(out=out, in_=in_, **kw), reads=reads, writes=writes, dma=True)

    def emit_stage(self, block_name=None, final_wait_all_dma=True):
        nc = self.nc
        ops = self.ops
        for o in ops:
            for d in o.deps:
                if d.dma:
                    continue
                if d.eng == o.eng and not o.dma and not self.same and not o.strict:
                    continue
                d.needed = True
        for o in ops:
            if not o.dma and o.needed:
                self.cnt[o.eng] += 1
                o.sem = o.eng
                o.val = self.cnt[o.eng]
        per = {e: [] for e in ENGS}
        for o in ops:
            per[o.eng].append(o)
        sems = self.sems
        waited = self.waited
        same = self.same
        dma_cum = self.dma_cum

        def emit_engine(ename, eng):
            wd = waited[ename]
            for o in per[ename]:
                need = {}
                for d in o.deps:
                    if d.val is None:
                        continue
                    if (not d.dma) and d.eng == ename and (not o.dma) and (not same) and (not o.strict):
                        continue
                    s = d.sem
                    if need.get(s, 0) < d.val:
                        need[s] = d.val
                for s, v in need.items():
                    if wd.get(s, 0) >= v:
                        continue
                    eng.wait_ge(sems[s], v)
                    wd[s] = v
                ins = o.fn(eng)
                if o.dma:
                    ins.then_inc(sems[o.sem], 16)
                elif o.needed:
                    ins.then_inc(sems[o.sem], 1)
            if final_wait_all_dma:
                for s, v in dma_cum.items():
                    if s[0] == ename and v > 0 and wd.get(s, 0) < v:
                        eng.wait_ge(sems[s], v)
                        wd[s] = v

        with nc.Block() as block:
            @block.tensor
            def _(e):
                emit_engine('pe', e)

            @block.scalar
            def _(e):
                emit_engine('act', e)

            @block.vector
            def _(e):
                emit_engine('dve', e)

            @block.gpsimd
            def _(e):
                emit_engine('pool', e)

            @block.sync
            def _(e):
                emit_engine('sp', e)
        for e in ENGS:
            for s in self.sems:
                if isinstance(s, str):
                    self.waited[e][s] = self.cnt[s]
                else:
                    self.waited[e][s] = self.dma_cum[s]
        self.reset_stage()

from contextlib import ExitStack

D = 2048; EPS = 1e-6; S = 4096
SCALE = 128 ** -0.5
NEGM = -1e30


def norm_stage(nc, P, es, sb, ps, xT, g_in, hnT, ones):
    xt = sb("xt", [128, 16, 256], F32)
    gt = sb("gt", [128, 16], F32)
    sq = [sb(f"nsq{i}", [128, 256], BF16) for i in range(2)]
    rstd = sb("nrstd", [128, 256], F32)
    P.dma('sp', gt[:], g_in[:, :], writes=[('g',)])
    pn = 6
    for tt in range(S // 256):
        tsl = slice(tt * 256, (tt + 1) * 256)
        P.dma('sp', xt[:], xT[:, tsl].rearrange("(kc p) n -> p kc n", p=128), writes=[('xt', dc) for dc in range(16)])
        for dc in range(16):
            si = dc % 2
            P.op('act', lambda e, dc=dc, si=si: e.activation(out=sq[si][:], in_=xt[:, dc, :], func=AF.Square),
                 reads=[('xt', dc)], writes=[('nsq', si)])
            P.op('pe', lambda e, dc=dc, si=si: e.matmul(ps[pn][:, 0:256], ones[:], sq[si][:], start=(dc == 0), stop=(dc == 15)),
                 reads=[('nsq', si), ('ones',)], writes=[('ps', pn)])
        P.op('act', lambda e: e.activation(out=rstd[:], in_=ps[pn][:, 0:256], func=AF.Sqrt, scale=1.0 / D, bias=EPS),
             reads=[('ps', pn)], writes=[('nrstd',)])
        P.op('dve', lambda e: e.reciprocal(out=rstd[:], in_=rstd[:]), reads=[('nrstd',)], writes=[('nrstd',)])
        for dc in range(16):
            P.op('dve', lambda e, dc=dc, tsl=tsl: e.scalar_tensor_tensor(out=hnT[:, dc, tsl], in0=xt[:, dc, :], scalar=gt[:, dc:dc + 1], in1=rstd[:], op0=ALU.mult, op1=ALU.mult),
                 reads=[('xt', dc), ('g',), ('nrstd',)], writes=[('hnT', tt // 2)])


class ProjCtx:
    pass


def proj_fm(nc, P, C, wsrc, ncols, specs, dst):
    nblk = ncols // 256
    state = C.state
    def load(blk, bi):
        P.dma('pool', C.wbl[bi][:], wsrc[:, blk * 256:(blk + 1) * 256].rearrange("(kc p) c -> p kc c", p=128), writes=[('wbl', bi)])
    for blk in range(nblk):
        bi = state['wrr'] % 2
        state['wrr'] += 1
        load(blk, bi)
        for sub in range(2):
            j = blk * 2 + sub
            sp = specs[j]
            for tt in range(8):
                tsl = slice(tt * 512, (tt + 1) * 512)
                pi = state['psrr'] % 6
                state['psrr'] += 1
                for kc in range(16):
                    P.op('pe', lambda e, kc=kc, pi=pi, sub=sub, bi=bi, tsl=tsl: e.matmul(C.ps[pi][:], C.wbl[bi][:, kc, sub * 128:(sub + 1) * 128], C.hnT[:, kc, tsl], start=(kc == 0), stop=(kc == 15)),
                         reads=[('wbl', bi), ('hnT', tt)], writes=[('ps', pi)])
                qi = state['qrr'] % 3
                state['qrr'] += 1
                qn = C.qn[qi]
                if sp['kind'] == 'raw':
                    P.op('act', lambda e, pi=pi, qn=qn: e.activation(out=qn[:], in_=C.ps[pi][:], func=AF.Copy),
                         reads=[('ps', pi)], writes=[('qn', qi)])
                else:
                    si = state['sqrr'] % 2
                    state['sqrr'] += 1
                    P.op('act', lambda e, pi=pi, si=si: e.activation(out=C.sq[si][:], in_=C.ps[pi][:], func=AF.Square),
                         reads=[('ps', pi)], writes=[('sq', si)])
                    P.op('pe', lambda e, si=si: e.matmul(C.ps[6][:], C.ones[:], C.sq[si][:], start=True, stop=True),
                         reads=[('sq', si), ('ones',)], writes=[('ps', 6)])
                    ri = state['rsrr'] % 2
                    state['rsrr'] += 1
                    P.op('act', lambda e, ri=ri: e.activation(out=C.rs[ri][:], in_=C.ps[6][:], func=AF.Sqrt, scale=1.0 / 128, bias=EPS),
                         reads=[('ps', 6)], writes=[('rs', ri)])
                    P.op('dve', lambda e, ri=ri: e.reciprocal(out=C.rs[ri][:], in_=C.rs[ri][:]), reads=[('rs', ri)], writes=[('rs', ri)])
                    gc = sp['gain']
                    P.op('dve', lambda e, pi=pi, qn=qn, ri=ri, gc=gc: e.scalar_tensor_tensor(out=qn[:], in0=C.ps[pi][:], scalar=C.gains[:, gc:gc + 1], in1=C.rs[ri][:], op0=ALU.mult, op1=ALU.mult),
                         reads=[('ps', pi), ('rs', ri), ('gains',)], writes=[('qn', qi)])
                    if sp.get('rope'):
                        ct, st = sp['tabs']
                        tofs = sp.get('tofs', 0)
                        tl = slice(tofs + tt * 512, tofs + (tt + 1) * 512)
                        P.op('pe', lambda e, qn=qn: e.matmul(C.ps[7][0:32, :], C.Pm[:], qn[0:32, :], start=True, stop=True),
                             reads=[('qn', qi), ('Pm',)], writes=[('ps', 7)])
                        P.op('dve', lambda e, tl=tl, st=st: e.tensor_tensor(out=C.r1[:], in0=C.ps[7][0:32, :], in1=st[:, tl], op=ALU.mult),
                             reads=[('ps', 7), ('tabs',)], writes=[('r1',)])
                        P.op('dve', lambda e, tl=tl, qn=qn, ct=ct: e.tensor_tensor(out=C.r2[:], in0=qn[0:32, :], in1=ct[:, tl], op=ALU.mult),
                             reads=[('qn', qi), ('tabs',)], writes=[('r2',)])
                        P.op('dve', lambda e, qn=qn: e.tensor_tensor(out=qn[0:32, :], in0=C.r1[:], in1=C.r2[:], op=ALU.add),
                             reads=[('r1',), ('r2',)], writes=[('qn', qi)])
                P.dma('sp', dst(j, tt), qn[:], reads=[('qn', qi)], writes=[('dst_fm', id(dst), j, tt)])


def proj_tm(nc, P, C, wsrc, ncols, dst):
    nblk = ncols // 256
    state = C.state
    for blk in range(nblk):
        bi = state['wrr'] % 2
        state['wrr'] += 1
        P.dma('pool', C.wbl[bi][:], wsrc[:, blk * 256:(blk + 1) * 256].rearrange("(kc p) c -> p kc c", p=128), writes=[('wbl', bi)])
        for tb4 in range(8):
            vi = state['vrr'] % 2
            state['vrr'] += 1
            for i in range(4):
                tb = tb4 * 4 + i
                pi = state['psrr'] % 6
                state['psrr'] += 1
                for kc in range(16):
                    P.op('pe', lambda e, kc=kc, pi=pi, bi=bi, tb=tb: e.matmul(C.ps[pi][:, 0:256], C.hnT[:, kc, tb * 128:(tb + 1) * 128], C.wbl[bi][:, kc, :], start=(kc == 0), stop=(kc == 15)),
                         reads=[('wbl', bi), ('hnT', tb // 4)], writes=[('ps', pi)])
                P.op('act', lambda e, pi=pi, vi=vi, i=i: e.activation(out=C.vst[vi][:, i, :], in_=C.ps[pi][:, 0:256], func=AF.Copy),
                     reads=[('ps', pi)], writes=[('vst', vi)])
            P.dma('sp', dst(blk, tb4), C.vst[vi][:], reads=[('vst', vi)], writes=[('dst_tm', blk, tb4)])


def attn_out_block(nc, P, C, Ops, qb4, gate_ap, og, key_og, gate_key=None):
    st = C.state
    oi = st['onrr'] % 2
    st['onrr'] += 1
    P.op('dve', lambda e, oi=oi: e.reciprocal(out=C.rden[oi][:], in_=Ops[:, 128:129]),
         reads=[('O', qb4)], writes=[('rden', oi)])
    P.op('dve', lambda e, oi=oi: e.tensor_scalar(out=C.on[oi][:], in0=Ops[:, 0:128], scalar1=C.rden[oi][:, 0:1], scalar2=None, op0=ALU.mult),
         reads=[('O', qb4), ('rden', oi)], writes=[('on', oi)], strict=True)
    ti = st['trr'] % 4
    st['trr'] += 1
    P.op('pe', lambda e, oi=oi, ti=ti: e.transpose(C.pst[:, ti * 128:(ti + 1) * 128], C.on[oi][:], C.ident[:]),
         reads=[('on', oi), ('ident',)], writes=[('pst', ti)])
    if gate_ap is not None:
        P.op('dve', lambda e, ti=ti: e.tensor_tensor(out=og[:, qb4 * 128:(qb4 + 1) * 128], in0=C.pst[:, ti * 128:(ti + 1) * 128], in1=gate_ap, op=ALU.mult),
             reads=[('pst', ti), gate_key], writes=[key_og])
    else:
        P.op('dve', lambda e, ti=ti: e.tensor_copy(out=og[:, qb4 * 128:(qb4 + 1) * 128], in_=C.pst[:, ti * 128:(ti + 1) * 128]),
             reads=[('pst', ti)], writes=[key_og])


def emit_mix0(nc, P, T_, NH=8):
    xT = T_['xT']; g_in = T_['g0']; Wfm = T_['Wfm0']; Wtm = T_['Wtm0']; Wf = T_['Wf0']; bfv = T_['bf0']
    gains_in = T_['gains0']; ropeC = T_['ropeC']; ropeS = T_['ropeS']; Pm_in = T_['Pm']; trim_in = T_['trim']
    ident_in = T_['ident']; Mt_in = T_['Mt']; oT = T_['o0T']
    debug_stage1_only = False
    qk_s = nc.dram_tensor("a_qk_s", [5 * NH, 128, S], BF16, kind="Internal").ap()
    v_s = nc.dram_tensor("a_v_s", [S, 2 * NH * 128], BF16, kind="Internal").ap()
    c_s = nc.dram_tensor("a_c_s", [NH, S], F32, kind="Internal").ap()
    nheads_fox = nheads_dil = NH

    with ExitStack() as es:
        def sb(name, shape, dt):
            return es.enter_context(nc.sbuf_tensor("a1_" + name, shape, dt))
        C = ProjCtx()
        C.state = dict(wrr=0, psrr=0, qrr=0, sqrr=0, rsrr=0, vrr=0)
        C.ps = [es.enter_context(nc.psum_tensor(f"a1ps{i}", [128, 512], F32)) for i in range(8)]
        C.hnT = sb("hnT", [128, 16, S], BF16)
        C.ones = sb("ones", [128, 128], BF16)
        C.wbl = [sb(f"wbl{i}", [128, 16, 256], BF16) for i in range(2)]
        C.Ct = sb("Ct", [32, S], BF16)
        C.St = sb("St", [32, S], BF16)
        C.Pm = sb("Pmt", [32, 32], BF16)
        C.gains = sb("gains", [128, 4], F32)
        C.sq = [sb(f"sq{i}", [128, 512], BF16) for i in range(2)]
        C.rs = [sb(f"rs{i}", [128, 512], F32) for i in range(2)]
        C.qn = [sb(f"qn{i}", [128, 512], BF16) for i in range(3)]
        C.r1 = sb("r1", [32, 512], F32)
        C.r2 = sb("r2", [32, 512], F32)
        C.vst = [sb(f"vst{i}", [128, 4, 256], BF16) for i in range(2)]
        wft = sb("wft", [128, 16, NH], BF16)
        bft = sb("bft", [NH, 1], F32)
        onesr = sb("onesr", [NH, 512], F32)
        fx = sb("fx", [NH, 512], F32)
        fa_ = sb("fa_", [NH, 512], F32)
        fm_ = sb("fm_", [NH, 512], F32)
        ct = [sb(f"ct{i}", [NH, 512], F32) for i in range(2)]

        P.op('pool', lambda e: e.memset(C.ones[:], 1.0), writes=[('ones',)])
        P.op('pool', lambda e: e.memset(onesr[:], 1.0), writes=[('onesr',)])
        P.dma('pool', C.Ct[:], ropeC[:, :], writes=[('tabs',)])
        P.dma('pool', C.St[:], ropeS[:, :], writes=[('tabs',)])
        P.dma('pool', C.Pm[:], Pm_in[:, :], writes=[('Pm',)])
        P.dma('sp', C.gains[:], gains_in[:, :], writes=[('gains',)])
        P.dma('pool', wft[:], Wf.rearrange("(kc p) c -> p kc c", p=128), writes=[('wft',)])
        P.dma('sp', bft[:], bfv[:, :], writes=[('bft',)])
        P.op('act', lambda e: e.mul(C.gains[:, 0:1], C.gains[:, 0:1], SCALE), reads=[('gains',)], writes=[('gains',)])
        P.op('act', lambda e: e.mul(C.gains[:, 2:3], C.gains[:, 2:3], SCALE), reads=[('gains',)], writes=[('gains',)])

        norm_stage(nc, P, es, sb, C.ps, xT, g_in, C.hnT, C.ones)

        for tt in range(8):
            tsl = slice(tt * 512, (tt + 1) * 512)
            for kc in range(16):
                P.op('pe', lambda e, kc=kc, tsl=tsl: e.matmul(C.ps[7][0:NH, :], wft[:, kc, :], C.hnT[:, kc, tsl], start=(kc == 0), stop=(kc == 15)),
                     reads=[('wft',), ('hnT', tt)], writes=[('ps', 7)])
            P.op('dve', lambda e: e.tensor_scalar(out=fx[:], in0=C.ps[7][0:NH, :], scalar1=bft[:, 0:1], scalar2=None, op0=ALU.add),
                 reads=[('ps', 7), ('bft',)], writes=[('fx',)])
            P.op('act', lambda e: e.activation(out=fa_[:], in_=fx[:], func=AF.Abs), reads=[('fx',)], writes=[('fa_',)])
            P.op('act', lambda e: e.activation(out=fa_[:], in_=fa_[:], func=AF.Exp, scale=-1.0), reads=[('fa_',)], writes=[('fa_',)])
            P.op('act', lambda e: e.activation(out=fa_[:], in_=fa_[:], func=AF.Ln, bias=1.0), reads=[('fa_',)], writes=[('fa_',)])
            P.op('dve', lambda e: e.tensor_scalar(out=fm_[:], in0=fx[:], scalar1=0.0, scalar2=None, op0=ALU.min),
                 reads=[('fx',)], writes=[('fm_',)])
            P.op('dve', lambda e: e.tensor_tensor(out=fm_[:], in0=fm_[:], in1=fa_[:], op=ALU.subtract),
                 reads=[('fm_',), ('fa_',)], writes=[('fm_',)])
            ci = tt % 2
            init = 0.0 if tt == 0 else ct[1 - ci][:, 511:512]
            P.op('dve', lambda e, ci=ci, init=init: e.tensor_tensor_scan(out=ct[ci][:], data0=onesr[:], data1=fm_[:], initial=init, op0=ALU.mult, op1=ALU.add),
                 reads=[('fm_',), ('onesr',), ('ct', 1 - ci)], writes=[('ct', ci)], strict=True)
            P.dma('sp', c_s[:, tsl], ct[ci][:], reads=[('ct', ci)], writes=[('c_s', tt)])

        specs = ([dict(kind='norm', gain=0)] * NH + [dict(kind='norm', gain=1)] * NH + [dict(kind='raw')] * NH +
                 [dict(kind='norm', gain=2, rope=True, tabs=(C.Ct, C.St))] * NH + [dict(kind='norm', gain=3, rope=True, tabs=(C.Ct, C.St))] * NH)
        proj_fm(nc, P, C, Wfm, 5 * NH * 128, specs, lambda j, tt: qk_s[j, :, tt * 512:(tt + 1) * 512])
        proj_tm(nc, P, C, Wtm, 2 * NH * 128, lambda blk, tb4: v_s[tb4 * 512:(tb4 + 1) * 512, blk * 256:(blk + 1) * 256].rearrange("(i p) c -> p i c", p=128))
        P.emit_stage()


    with ExitStack() as es:
        def sb(name, shape, dt):
            return es.enter_context(nc.sbuf_tensor("a2_" + name, shape, dt))
        C = ProjCtx()
        C.state = dict(onrr=0, trr=0, srr=0, trr2=0, prr=0, ogrr=0)
        C.ps = [es.enter_context(nc.psum_tensor(f"a2ps{i}", [128, 512], F32)) for i in range(7)]
        C.pst = es.enter_context(nc.psum_tensor("a2pst", [128, 1024], BF16))
        C.ident = sb("ident", [128, 128], BF16)
        trim = sb("trim", [128, 128], F32)
        Mt = sb("Mt", [128, 20, 512], BF16)
        C.rden = [sb(f"rden{i}", [128, 1], F32) for i in range(2)]
        C.on = [sb(f"on{i}", [128, 128], BF16) for i in range(2)]
        tt_ = [sb(f"tt{i}", [128, 512], F32) for i in range(2)]
        et = [sb(f"et{i}", [128, 512], BF16) for i in range(2)]
        pT = [sb(f"pT{i}", [128, 512], BF16) for i in range(3)]
        og = [sb(f"og{i}", [128, 512], F32) for i in range(2)]
        hb = []
        for i in range(2):
            h_ = ProjCtx()
            h_.qT = sb(f"hq{i}", [128, S], BF16)
            h_.kT = sb(f"hk{i}", [128, S], BF16)
            h_.gT = sb(f"hg{i}", [128, S], BF16)
            h_.V = sb(f"hv{i}", [128, 32, 136], BF16)
            h_.cqb = sb(f"hcq{i}", [128, S], F32)
            h_.ck = sb(f"hck{i}", [128, 32], F32)
            hb.append(h_)
        P.dma('pool', C.ident[:], ident_in[:, :], writes=[('ident',)])
        P.dma('sp', trim[:], trim_in[:, :], writes=[('trim',)])
        P.dma('pool', Mt[:], Mt_in.rearrange("p (o c) -> p o c", o=20), writes=[('Mt',)])
        for i in range(2):
            P.op('pool', lambda e, i=i: e.memset(hb[i].V[:, :, 128:136], 1.0), writes=[('hb_V1', i)])

        heads = [('fox', h) for h in range(nheads_fox)] + [('dil', h) for h in range(nheads_dil)]
        for hi, (kind, h) in enumerate(heads):
            bi = hi % 2
            H = hb[bi]
            kq, kk, kg, kv, kc_, kcq = [('hb', bi, n) for n in ('q', 'k', 'g', 'v', 'ck', 'cq')]
            if kind == 'fox':
                jq, jk, jg, vc0 = h, NH + h, 2 * NH + h, h * 128
            else:
                jq, jk, jg, vc0 = 3 * NH + h, 4 * NH + h, None, NH * 128 + h * 128
            P.dma('sp', H.qT[:], qk_s[jq, :, :], writes=[kq])
            P.dma('sp', H.kT[:], qk_s[jk, :, :], writes=[kk])
            P.dma('sp', H.V[:, :, 0:128], v_s[:, vc0:vc0 + 128].rearrange("(blk p) c -> p blk c", p=128), reads=[('hb_V1', bi)], writes=[kv])
            if kind == 'fox':
                P.dma('sp', H.gT[:], qk_s[jg, :, :], writes=[kg])
                P.op('act', lambda e, H=H: e.activation(out=H.gT[:], in_=H.gT[:], func=AF.Sigmoid), reads=[kg], writes=[kg])
                P.dma('sp', H.ck[:], c_s[h, :].rearrange("(blk p) -> p blk", p=128), reads=[], writes=[kc_], allow_slow_non_contiguous=True)
                P.op('pool', lambda e, H=H: e.tensor_scalar(out=H.ck[:], in0=H.ck[:], scalar1=-1.0, scalar2=None, op0=ALU.mult), reads=[kc_], writes=[kc_])
                P.dma('sp', H.cqb[:], c_s[h, :].partition_broadcast(128), writes=[kcq])
            import os
            for Qc in range(int(os.environ.get('NQC', '8'))):
                q0 = Qc * 512
                kb_lo = 0 if kind == 'fox' else max(0, 4 * Qc - 16)
                kb_hi = 4 * Qc + 3
                ogi = C.state['ogrr'] % 2
                C.state['ogrr'] += 1
                for kb in range(kb_lo, kb_hi + 1):
                    j = kb - 4 * Qc
                    qlo = max(0, j) * 128
                    si = C.state['srr'] % 2
                    C.state['srr'] += 1
                    P.op('pe', lambda e, H=H, kb=kb, si=si, qlo=qlo, q0=q0: e.matmul(C.ps[si][:, qlo:512], H.kT[:, kb * 128:(kb + 1) * 128], H.qT[:, q0 + qlo:q0 + 512], start=True, stop=True),
                         reads=[kq, kk], writes=[('ps', si)])
                    pi = C.state['prr'] % 3
                    C.state['prr'] += 1
                    if kind == 'fox':
                        ti = C.state['trr2'] % 2
                        C.state['trr2'] += 1
                        P.op('dve', lambda e, H=H, si=si, ti=ti, qlo=qlo, q0=q0: e.tensor_tensor(out=tt_[ti][:, qlo:512], in0=C.ps[si][:, qlo:512], in1=H.cqb[:, q0 + qlo:q0 + 512], op=ALU.add),
                             reads=[('ps', si), kcq], writes=[('tt', ti)])
                        if j >= 0:
                            P.op('dve', lambda e, ti=ti, qlo=qlo: e.tensor_tensor(out=tt_[ti][:, qlo:qlo + 128], in0=tt_[ti][:, qlo:qlo + 128], in1=trim[:], op=ALU.add),
                                 reads=[('tt', ti), ('trim',)], writes=[('tt', ti)])
                        P.op('act', lambda e, H=H, ti=ti, pi=pi, qlo=qlo, kb=kb: e.activation(out=pT[pi][:, qlo:512], in_=tt_[ti][:, qlo:512], func=AF.Exp, bias=H.ck[:, kb:kb + 1]),
                             reads=[('tt', ti), kc_], writes=[('pT', pi)])
                    else:
                        ei = C.state['trr2'] % 2
                        C.state['trr2'] += 1
                        oi_ = 4 * Qc - kb + 3
                        P.op('act', lambda e, si=si, ei=ei, qlo=qlo: e.activation(out=et[ei][:, qlo:512], in_=C.ps[si][:, qlo:512], func=AF.Exp),
                             reads=[('ps', si)], writes=[('et', ei)])
                        P.op('dve', lambda e, ei=ei, pi=pi, qlo=qlo, oi_=oi_: e.tensor_tensor(out=pT[pi][:, qlo:512], in0=et[ei][:, qlo:512], in1=Mt[:, oi_, qlo:512], op=ALU.mult),
                             reads=[('et', ei), ('Mt',)], writes=[('pT', pi)])
                    for qb4 in range(max(0, j), 4):
                        last_kb = 4 * Qc + qb4
                        P.op('pe', lambda e, H=H, pi=pi, qb4=qb4, kb=kb, kb_lo=kb_lo, last_kb=last_kb: e.matmul(C.ps[2 + qb4][:, 0:130], pT[pi][:, qb4 * 128:(qb4 + 1) * 128], H.V[:, kb, 0:130], start=(kb == kb_lo), stop=(kb == last_kb)),
                             reads=[('pT', pi), kv], writes=[('O', qb4)])
                for qb4 in range(4):
                    gate_ap = H.gT[:, q0 + qb4 * 128:q0 + (qb4 + 1) * 128] if kind == 'fox' else None
                    attn_out_block(nc, P, C, C.ps[2 + qb4], qb4, gate_ap, og[ogi], ('og', ogi), gate_key=kg)
                orow = (h if kind == 'fox' else NH + h) * 128
                P.dma('sp', oT[orow:orow + 128, q0:q0 + 512], og[ogi][:], reads=[('og', ogi)], writes=[('oT', hi, Qc)])
        P.emit_stage()


def host_consts():
    inv = 1.0 / (500000.0 ** (np.arange(0, 32, 2, dtype=np.float32) / 32))
    ang = np.arange(S, dtype=np.float32)[None, :] * inv[:, None]
    cos = np.cos(ang).astype(np.float32); sin = np.sin(ang).astype(np.float32)
    ropeC = np.concatenate([cos, cos], 0)
    ropeS = np.concatenate([-sin, sin], 0)
    Pm = np.zeros((32, 32), np.float32)
    for m in range(32):
        Pm[(m + 16) % 32, m] = 1.0
    p = np.arange(128)[:, None]; c = np.arange(128)[None, :]
    trim = np.where(p > c, NEGM, 0.0).astype(np.float32)
    ident = np.eye(128, dtype=np.float32)
    Mt = np.zeros((128, 20, 512), np.float32)
    for oi in range(20):
        o = 128 * (oi - 3)
        d = o + np.arange(512)[None, :] - np.arange(128)[:, None]
        m = ((d >= 0) & (d <= 128)).astype(np.float32) + ((d >= 0) & (d <= 512) & (d % 4 == 0)) + ((d >= 0) & (d <= 2048) & (d % 16 == 0))
        Mt[:, oi, :] = m
    return dict(ropeC=ropeC, ropeS=ropeS, Pm=Pm, trim=trim, ident=ident, Mt=Mt.reshape(128, 20 * 512))


def host_inputs_mix0(inp, b, half):
    w = inp['even_w_in'][0]
    hs = slice(half * 512, (half + 1) * 512)
    qa = w[:, 0:1024][:, hs]; ka = w[:, 1024:2048][:, hs]; va = w[:, 2048:3072][:, hs]; ga = w[:, 3072:4096][:, hs]
    fa = w[:, 4096:4104][:, half * 4:(half + 1) * 4]
    qd = w[:, 4104:5128][:, hs]; kd = w[:, 5128:6152][:, hs]; vd = w[:, 6152:7176][:, hs]
    m = dict(
        xT=np.ascontiguousarray(inp['x'][b].T),
        g=np.ascontiguousarray(inp['ln_mix_g'][0].reshape(16, 128).T),
        Wfm=np.ascontiguousarray(np.concatenate([qa, ka, ga, qd, kd], 1)),
        Wtm=np.ascontiguousarray(np.concatenate([va, vd], 1)),
        Wf=np.ascontiguousarray(fa),
        bf=np.ascontiguousarray(inp['even_b_f'][0][half * 4:(half + 1) * 4].reshape(4, 1)),
        gains=np.ascontiguousarray(np.stack([inp['even_g_q_fox'][0], inp['even_g_k_fox'][0], inp['even_g_q_dil'][0], inp['even_g_k_dil'][0]], 1)),
    )
    m.update(host_consts())
    return m


NB = -30000.0
GC = 1.5957691216057308


def emit_mix1(nc, P, T_, NG=4):
    xT = T_['h2T']; g_in = T_['g1']; Wfm = T_['Wfm1']; Wtm = T_['Wtm1']; gains_in = T_['gains1']
    ropeC = T_['ropeC']; ropeS = T_['ropeS']; Pm_in = T_['Pm']; ropeCc = T_['ropeCc']; ropeSc = T_['ropeSc']
    ident_in = T_['ident']
    w1k = T_['w1k']; w2k = T_['w2k']; peTk = T_['peTk']; w1v = T_['w1v']; w2v = T_['w2v']; peTv = T_['peTv']
    ovl_in = T_['ovl']; cmpM_in = T_['cmpM']; FT_in = T_['FT']; CT_in = T_['CT']; ExpT_in = T_['ExpT']; WB_in = T_['WB']
    trib_in = T_['trib']; oT = T_['o1T']
    dbg = False
    dbgo = nc.dram_tensor("b_dbgo", [128, 2048], F32, kind="Internal").ap()
    qk_s = nc.dram_tensor("b_qk_s", [8 * NG, 128, S], BF16, kind="Internal").ap()
    v_s = nc.dram_tensor("b_v_s", [S, 2 * NG * 128 + 256], BF16, kind="Internal").ap()
    kc_s = nc.dram_tensor("b_kc_s", [NG, 128, 256], BF16, kind="Internal").ap()
    vc_s = nc.dram_tensor("b_vc_s", [NG, 256, 128], BF16, kind="Internal").ap()

    with ExitStack() as es:
        def sb(name, shape, dt):
            return es.enter_context(nc.sbuf_tensor("b1_" + name, shape, dt))
        C = ProjCtx()
        C.state = dict(wrr=0, psrr=0, qrr=0, sqrr=0, rsrr=0, vrr=0)
        C.ps = [es.enter_context(nc.psum_tensor(f"b1ps{i}", [128, 512], F32)) for i in range(8)]
        C.hnT = sb("hnT", [128, 16, S], BF16)
        C.ones = sb("ones", [128, 128], BF16)
        C.wbl = [sb(f"wbl{i}", [128, 16, 256], BF16) for i in range(2)]
        C.Ct = sb("Ct", [32, S], BF16)
        C.St = sb("St", [32, S], BF16)
        C.Pm = sb("Pmt", [32, 32], BF16)
        C.gains = sb("gains", [128, 4], F32)
        C.sq = [sb(f"sq{i}", [128, 512], BF16) for i in range(2)]
        C.rs = [sb(f"rs{i}", [128, 512], F32) for i in range(2)]
        C.qn = [sb(f"qn{i}", [128, 512], BF16) for i in range(3)]
        C.r1 = sb("r1", [32, 512], F32)
        C.r2 = sb("r2", [32, 512], F32)
        C.vst = [sb(f"vst{i}", [128, 4, 256], BF16) for i in range(2)]
        P.op('pool', lambda e: e.memset(C.ones[:], 1.0), writes=[('ones',)])
        P.dma('pool', C.Ct[:], ropeC[:, :], writes=[('tabs',)])
        P.dma('pool', C.St[:], ropeS[:, :], writes=[('tabs',)])
        P.dma('pool', C.Pm[:], Pm_in[:, :], writes=[('Pm',)])
        P.dma('sp', C.gains[:], gains_in[:, :], writes=[('gains',)])
        P.op('act', lambda e: e.mul(C.gains[:, 0:1], C.gains[:, 0:1], SCALE), reads=[('gains',)], writes=[('gains',)])
        norm_stage(nc, P, es, sb, C.ps, xT, g_in, C.hnT, C.ones)
        rp = dict(rope=True, tabs=(C.Ct, C.St))
        specs = ([dict(kind='norm', gain=0, **rp)] * (4 * NG) + [dict(kind='norm', gain=1, **rp)] * NG +
                 [dict(kind='norm', gain=2, **rp)] * NG + [dict(kind='raw')] * (2 * NG))
        proj_fm(nc, P, C, Wfm, 8 * NG * 128, specs, lambda j, tt: qk_s[j, :, tt * 512:(tt + 1) * 512])
        proj_tm(nc, P, C, Wtm, 2 * NG * 128 + 256, lambda blk, tb4: v_s[tb4 * 512:(tb4 + 1) * 512, blk * 256:(blk + 1) * 256].rearrange("(i p) c -> p i c", p=128))
        P.emit_stage()

    with ExitStack() as es:
        def sb(name, shape, dt):
            return es.enter_context(nc.sbuf_tensor("bc_" + name, shape, dt))
        ps = [es.enter_context(nc.psum_tensor(f"bcps{i}", [128, 512], F32)) for i in range(4)]
        ones = sb("ones", [128, 128], BF16)
        Pm = sb("Pm", [32, 32], BF16)
        Cc = sb("Cc", [32, 256], BF16); Sc = sb("Sc", [32, 256], BF16)
        gains = sb("gains", [128, 4], F32)
        P.op('pool', lambda e: e.memset(ones[:], 1.0), writes=[('ones',)])
        P.dma('pool', Pm[:], Pm_in[:, :], writes=[('Pm',)])
        P.dma('pool', Cc[:], ropeCc[:, :], writes=[('tabc',)])
        P.dma('pool', Sc[:], ropeSc[:, :], writes=[('tabc',)])
        P.dma('sp', gains[:], gains_in[:, :], writes=[('gains',)])
        w1s = [sb(f"w1s{i}", [128, 32, 128], BF16) for i in range(2)]
        w2s = [sb(f"w2s{i}", [128, 128], BF16) for i in range(2)]
        pes = [sb(f"pes{i}", [128, 32], BF16) for i in range(2)]
        for i, (w1, w2, pe) in enumerate([(w1k, w2k, peTk), (w1v, w2v, peTv)]):
            P.dma('pool', w1s[i][:], w1.rearrange("(l p) o -> p l o", p=128), writes=[('w1s', i)])
            P.dma('pool', w2s[i][:], w2[:, :], writes=[('w2s', i)])
            P.dma('pool', pes[i][:], pe[:, :], writes=[('pes', i)])
        xc = sb("xc", [128, S], BF16)
        bz = sb("bz", [128, 1], F32)
        zs = sb("zs", [128, 256], F32); z2 = sb("z2", [128, 256], F32); sg = sb("sg", [128, 256], F32)
        G = sb("G", [128, 256], BF16)
        sq = sb("sq", [128, 256], BF16); rs = sb("rs", [128, 256], F32); kn = sb("kn", [128, 256], BF16)
        r1 = sb("r1", [32, 256], F32); r2 = sb("r2", [32, 256], F32)
        vct = sb("vct", [128, 2, 128], BF16)
        P.op('pool', lambda e: e.memset(G[:], 0.0), writes=[('G',)])
        P.op('pool', lambda e: e.memset(kn[:], 0.0), writes=[('kn',)])
        for g in range(NG):
            for i in range(2):
                P.dma('sp', xc[:], qk_s[6 * NG + NG * i + g, :, :], writes=[('xc',)])
                xv = xc[:].rearrange("p (i r) -> p i r", r=16)
                for l in range(32):
                    a, r = (0, l) if l < 16 else (1, l - 16)
                    P.op('pe', lambda e, l=l, a=a, r=r, i=i: e.matmul(ps[0][:, 0:255], w1s[i][:, l, :], xv[:, a:a + 255, r], start=(l == 0), stop=(l == 31)),
                         reads=[('w1s', i), ('xc',)], writes=[('ps', 0)])
                for l in range(32):
                    P.op('pe', lambda e, l=l, i=i: e.matmul(ps[1][:, 0:1], w1s[i][:, l, :], pes[i][:, l:l + 1], start=(l == 0), stop=(l == 31)),
                         reads=[('w1s', i), ('pes', i)], writes=[('ps', 1)])
                P.op('dve', lambda e: e.tensor_copy(out=bz[:], in_=ps[1][:, 0:1]), reads=[('ps', 1)], writes=[('bz',)])
                P.op('dve', lambda e: e.tensor_scalar(out=zs[:, 0:255], in0=ps[0][:, 0:255], scalar1=bz[:, 0:1], scalar2=None, op0=ALU.add),
                     reads=[('ps', 0), ('bz',)], writes=[('zs',)], strict=True)
                P.op('dve', lambda e: e.tensor_tensor(out=z2[:, 0:255], in0=zs[:, 0:255], in1=zs[:, 0:255], op=ALU.mult), reads=[('zs',)], writes=[('z2',)])
                P.op('dve', lambda e: e.tensor_scalar(out=z2[:, 0:255], in0=z2[:, 0:255], scalar1=0.044715, scalar2=1.0, op0=ALU.mult, op1=ALU.add), reads=[('z2',)], writes=[('z2',)])
                P.op('dve', lambda e: e.tensor_tensor(out=z2[:, 0:255], in0=z2[:, 0:255], in1=zs[:, 0:255], op=ALU.mult), reads=[('z2',), ('zs',)], writes=[('z2',)])
                P.op('act', lambda e: e.activation(out=sg[:, 0:255], in_=z2[:, 0:255], func=AF.Sigmoid, scale=GC), reads=[('z2',)], writes=[('sg',)])
                P.op('dve', lambda e: e.tensor_tensor(out=G[:, 0:255], in0=zs[:, 0:255], in1=sg[:, 0:255], op=ALU.mult), reads=[('zs',), ('sg',)], writes=[('G',)])
                if i == 0:
                    P.op('pe', lambda e: e.matmul(ps[2][:, 0:256], w2s[0][:], G[:], start=True, stop=True), reads=[('w2s', 0), ('G',)], writes=[('ps', 2)])
                    P.op('act', lambda e: e.activation(out=sq[:], in_=ps[2][:, 0:256], func=AF.Square), reads=[('ps', 2)], writes=[('sq',)])
                    P.op('pe', lambda e: e.matmul(ps[3][:, 0:256], ones[:], sq[:], start=True, stop=True), reads=[('sq',), ('ones',)], writes=[('ps', 3)])
                    P.op('act', lambda e: e.activation(out=rs[:], in_=ps[3][:, 0:256], func=AF.Sqrt, scale=1.0 / 128, bias=EPS), reads=[('ps', 3)], writes=[('rs',)])
                    P.op('dve', lambda e: e.reciprocal(out=rs[:], in_=rs[:]), reads=[('rs',)], writes=[('rs',)])
                    P.op('dve', lambda e: e.scalar_tensor_tensor(out=kn[:], in0=ps[2][:, 0:256], scalar=gains[:, 3:4], in1=rs[:], op0=ALU.mult, op1=ALU.mult),
                         reads=[('ps', 2), ('rs',), ('gains',)], writes=[('kn',)])
                    P.op('pe', lambda e: e.matmul(ps[3][0:32, 0:256], Pm[:], kn[0:32, :], start=True, stop=True), reads=[('kn',), ('Pm',)], writes=[('ps', 3)])
                    P.op('dve', lambda e: e.tensor_tensor(out=r1[:], in0=ps[3][0:32, 0:256], in1=Sc[:], op=ALU.mult), reads=[('ps', 3), ('tabc',)], writes=[('r1',)])
                    P.op('dve', lambda e: e.tensor_tensor(out=r2[:], in0=kn[0:32, :], in1=Cc[:], op=ALU.mult), reads=[('kn',), ('tabc',)], writes=[('r2',)])
                    P.op('dve', lambda e: e.tensor_tensor(out=kn[0:32, :], in0=r1[:], in1=r2[:], op=ALU.add), reads=[('r1',), ('r2',)], writes=[('kn',)])
                    P.dma('sp', kc_s[g, :, :], kn[:], reads=[('kn',)], writes=[('kc_s', g)])
                else:
                    for nb in range(2):
                        P.op('pe', lambda e, nb=nb: e.matmul(ps[2][:, nb * 128:(nb + 1) * 128], G[:, nb * 128:(nb + 1) * 128], w2s[1][:], start=True, stop=True),
                             reads=[('w2s', 1), ('G',)], writes=[('ps', 2)])
                    P.op('act', lambda e: e.activation(out=vct[:].rearrange("p a b -> p (a b)"), in_=ps[2][:, 0:256], func=AF.Copy), reads=[('ps', 2)], writes=[('vct',)])
                    P.dma('sp', vc_s[g].rearrange("(nb p) d -> p nb d", p=128), vct[:], reads=[('vct',)], writes=[('vc_s', g)])
        P.emit_stage()

    with ExitStack() as es:
        def sb(name, shape, dt):
            return es.enter_context(nc.sbuf_tensor("b2_" + name, shape, dt))
        st = dict(srr=0, prr=0, err=0, onrr=0, trr=0, ogrr=0, scl=0)
        ps = [es.enter_context(nc.psum_tensor(f"b2ps{i}", [128, 512], F32)) for i in range(7)]
        pst = es.enter_context(nc.psum_tensor("b2pst", [128, 1024], BF16))
        ident = sb("ident", [128, 128], BF16)
        trib = sb("trib", [128, 128], BF16)
        cmpM = sb("cmpM", [128, 2, S], BF16)
        ExpT = sb("ExpT", [64, 32, 128], BF16)
        WB = sb("WB", [128, 8, 512], BF16)
        P.dma('pool', ident[:], ident_in[:, :], writes=[('ident',)])
        P.dma('pool', trib[:], trib_in[:, :], writes=[('trib',)])
        for nb in range(2):
            for c in range(8):
                P.dma('pool', cmpM[:, nb, c * 512:(c + 1) * 512], cmpM_in[nb * 128:(nb + 1) * 128, c * 512:(c + 1) * 512], writes=[('cmpM', nb, c)])
        P.dma('pool', ExpT[:], ExpT_in.rearrange("j (k p) -> j k p", k=32), writes=[('ExpT',)])
        P.dma('pool', WB[:], WB_in.rearrange("p (o c) -> p o c", o=8), writes=[('WB',)])
        ksT = sb("ksT", [128, S], BF16); kwT = sb("kwT", [128, S], BF16)
        VS = sb("VS", [128, 32, 136], BF16); VW = sb("VW", [128, 32, 136], BF16)
        kcT = sb("kcT", [128, 256], BF16)
        VC = sb("VC", [128, 2, 200], BF16)
        gsg = sb("gsg", [128, 32, 12], F32)
        gtmp = sb("gtmp", [128, 32, 12], BF16)
        qT = [sb(f"qT{i}", [128, S], BF16) for i in range(4)]
        oacc = [sb(f"oacc{i}", [128, 4, 128], F32) for i in range(4)]
        impa = sb("impa", [128, 4, 64], F32)
        FTt = sb("FTt", [128, 4, 64], F32); CTt = sb("CTt", [128, 4, 64], F32)
        m8 = sb("m8", [128, 8], F32); m8b = sb("m8b", [128, 8], F32)
        wk = sb("wk", [128, 64], F32); s1 = sb("s1", [128, 64], F32); s2 = sb("s2", [128, 64], F32)
        selb = sb("selb", [128, 64], BF16)
        selT = sb("selT", [64, 512], BF16)
        et = [sb(f"et{i}", [128, 512], BF16) for i in range(2)]
        pT = [sb(f"pT{i}", [128, 512], BF16) for i in range(3)]
        og = [sb(f"og{i}", [128, 512], F32) for i in range(2)]
        on = [sb(f"on{i}", [128, 128], BF16) for i in range(2)]
        rden = [sb(f"rden{i}", [128, 1], F32) for i in range(4)]
        scl = [sb(f"scl{i}", [128, 1], F32) for i in range(4)]
        P.op('pool', lambda e: e.memset(VS[:, :, 128:136], 1.0), writes=[('VS1',)])
        P.op('pool', lambda e: e.memset(VW[:, :, 128:136], 1.0), writes=[('VW1',)])
        P.op('pool', lambda e: e.memset(VC[:, :, 128:136], 1.0), writes=[('VC1',)])
        for nb in range(2):
            P.dma('pool', VC[:, nb, 136:200], ovl_in[nb * 128:(nb + 1) * 128, :], writes=[('VCo', nb)])
        NQC = int(os.environ.get('NQC', '8'))
        dbt = sb('dbt', [128, 2048], F32)
        P.op('pool', lambda e: e.memset(dbt[:], 0.0), writes=[('dbt',)])
        dstate = {'done': os.environ.get('DBGD', '0') != '1'}
        P.same = os.environ.get('SAME', '0') == '1'

        def finish_branch(hh, Qc, br, first, ncol_den=128, imp=False, imp_first=False):
            SK = os.environ.get("SKIP", "")
            if "F" in SK:
                return
            if "I" in SK:
                imp = False
            for qb4 in range(4):
                Ops = ps[2 + qb4]
                blk = Qc * 4 + qb4
                ri = st['scl'] % 4
                st['scl'] += 1
                P.op('dve', lambda e, Ops=Ops, ri=ri: e.tensor_scalar(out=rden[ri][:], in0=Ops[:, 128:129], scalar1=1e-30, scalar2=None, op0=ALU.max),
                     reads=[('O', qb4)], writes=[('rden', ri)])
                P.op('dve', lambda e, ri=ri: e.reciprocal(out=rden[ri][:], in_=rden[ri][:]), reads=[('rden', ri)], writes=[('rden', ri)], strict=True)
                P.op('dve', lambda e, ri=ri, blk=blk, hh=hh, br=br: e.tensor_tensor(out=scl[ri][:], in0=rden[ri][:], in1=gsg[:, blk, hh * 3 + br:hh * 3 + br + 1], op=ALU.mult),
                     reads=[('rden', ri), ('gsg',)], writes=[('scl', ri)], strict=True)
                if imp:
                    if imp_first:
                        P.op('dve', lambda e, Ops=Ops, ri=ri, qb4=qb4: e.tensor_scalar(out=impa[:, qb4, :], in0=Ops[:, 136:200], scalar1=rden[ri][:, 0:1], scalar2=None, op0=ALU.mult),
                             reads=[('O', qb4), ('rden', ri)], writes=[('impa', qb4)], strict=True)
                    else:
                        P.op('dve', lambda e, Ops=Ops, ri=ri, qb4=qb4: e.scalar_tensor_tensor(out=impa[:, qb4, :], in0=Ops[:, 136:200], scalar=rden[ri][:, 0:1], in1=impa[:, qb4, :], op0=ALU.mult, op1=ALU.add),
                             reads=[('O', qb4), ('rden', ri), ('impa', qb4)], writes=[('impa', qb4)], strict=True)
                if first:
                    P.op('dve', lambda e, Ops=Ops, ri=ri, qb4=qb4, hh=hh: e.tensor_scalar(out=oacc[hh][:, qb4, :], in0=Ops[:, 0:128], scalar1=scl[ri][:, 0:1], scalar2=None, op0=ALU.mult),
                         reads=[('O', qb4), ('scl', ri)], writes=[('oacc', hh, qb4)], strict=True)
                else:
                    P.op('dve', lambda e, Ops=Ops, ri=ri, qb4=qb4, hh=hh: e.scalar_tensor_tensor(out=oacc[hh][:, qb4, :], in0=Ops[:, 0:128], scalar=scl[ri][:, 0:1], in1=oacc[hh][:, qb4, :], op0=ALU.mult, op1=ALU.add),
                         reads=[('O', qb4), ('scl', ri), ('oacc', hh, qb4)], writes=[('oacc', hh, qb4)], strict=True)

        for g in range(NG):
            P.dma('sp', ksT[:], qk_s[4 * NG + g, :, :], writes=[('ksT',)])
            P.dma('sp', kwT[:], qk_s[5 * NG + g, :, :], writes=[('kwT',)])
            P.dma('sp', VS[:, :, 0:128], v_s[:, g * 128:(g + 1) * 128].rearrange("(blk p) c -> p blk c", p=128), reads=[('VS1',)], writes=[('VS',)])
            P.dma('sp', VW[:, :, 0:128], v_s[:, NG * 128 + g * 128:NG * 128 + (g + 1) * 128].rearrange("(blk p) c -> p blk c", p=128), reads=[('VW1',)], writes=[('VW',)])
            P.dma('sp', kcT[:], kc_s[g, :, :], writes=[('kcT',)])
            P.dma('sp', VC[:, :, 0:128], vc_s[g].rearrange("(nb p) d -> p nb d", p=128), reads=[('VC1',)], writes=[('VC',)])
            P.dma('sp', gtmp[:], v_s[:, 2 * NG * 128 + g * 12:2 * NG * 128 + (g + 1) * 12].rearrange("(blk p) c -> p blk c", p=128), writes=[('gtmp',)])
            P.op('act', lambda e: e.activation(out=gsg[:], in_=gtmp[:], func=AF.Sigmoid), reads=[('gtmp',)], writes=[('gsg',)])
            for hh in range(4):
                P.dma('sp', qT[hh][:], qk_s[g * 4 + hh, :, :], writes=[('qT', hh)])
            def chunk(g, Qc):
                q0 = Qc * 512
                if 'T' not in os.environ.get('SKIP', ''):
                  P.dma('sp', FTt[:], FT_in[q0:q0 + 512, :].rearrange("(b p) j -> p b j", p=128), writes=[('FTt',)])
                if 'T' not in os.environ.get('SKIP', ''):
                  P.dma('sp', CTt[:], CT_in[q0:q0 + 512, :].rearrange("(b p) j -> p b j", p=128), writes=[('CTt',)])
                for hh in range(4):
                    nbs = [0] + ([1] if Qc >= 4 else [])
                    for nb in nbs:
                        si = st['srr'] % 2; st['srr'] += 1
                        P.op('pe', lambda e, nb=nb, si=si, hh=hh: e.matmul(ps[si][:], kcT[:, nb * 128:(nb + 1) * 128], qT[hh][:, q0:q0 + 512], start=True, stop=True),
                             reads=[('kcT',), ('qT', hh)], writes=[('ps', si)])
                        ei = st['err'] % 2; st['err'] += 1
                        pi = st['prr'] % 3; st['prr'] += 1
                        P.op('act', lambda e, si=si, ei=ei: e.activation(out=et[ei][:], in_=ps[si][:], func=AF.Exp), reads=[('ps', si)], writes=[('et', ei)])
                        P.op('dve', lambda e, ei=ei, pi=pi, nb=nb: e.tensor_tensor(out=pT[pi][:], in0=et[ei][:], in1=cmpM[:, nb, q0:q0 + 512], op=ALU.mult),
                             reads=[('et', ei), ('cmpM', nb, Qc)], writes=[('pT', pi)])
                        for qb4 in range(4):
                            P.op('pe', lambda e, pi=pi, qb4=qb4, nb=nb, nbs=nbs: e.matmul(ps[2 + qb4][:, 0:200], pT[pi][:, qb4 * 128:(qb4 + 1) * 128], VC[:, nb, :], start=(nb == 0), stop=(nb == nbs[-1])),
                                 reads=[('pT', pi), ('VC',), ('VCo', 0), ('VCo', 1), ('VC1',)], writes=[('O', qb4)])
                    if not dstate['done'] and hh == 0:
                        lastpi = (st['prr'] - 1) % 3
                        lastei = (st['err'] - 1) % 2
                        P.op('dve', lambda e, lastpi=lastpi: e.tensor_copy(out=dbt[:, 0:512], in_=pT[lastpi][:]), reads=[('pT', lastpi)], writes=[('dbt',)])
                        P.op('dve', lambda e, lastei=lastei: e.tensor_copy(out=dbt[:, 512:1024], in_=et[lastei][:]), reads=[('et', lastei)], writes=[('dbt',)])
                        P.op('dve', lambda e: e.tensor_copy(out=dbt[:, 1024:1224], in_=ps[2][:, 0:200]), reads=[('O', 0)], writes=[('dbt',)])
                        P.op('dve', lambda e, q0=q0: e.tensor_copy(out=dbt[:, 1536:2048], in_=cmpM[:, 0, q0:q0 + 512]), reads=[('cmpM', 0, Qc)], writes=[('dbt',)])
                    finish_branch(hh, Qc, 0, True, imp=True, imp_first=(hh == 0))
                    if not dstate['done'] and hh == 0:
                        dstate['done'] = True
                        lr = (st['scl'] - 4) % 4
                        P.op('dve', lambda e, lr=lr: e.tensor_copy(out=dbt[:, 1224:1225], in_=rden[lr][:]), reads=[('rden', lr)], writes=[('dbt',)])
                        P.op('dve', lambda e, lr=lr: e.tensor_copy(out=dbt[:, 1225:1226], in_=scl[lr][:]), reads=[('scl', lr)], writes=[('dbt',)])
                        P.op('dve', lambda e: e.tensor_copy(out=dbt[:, 1226:1354], in_=oacc[0][:, 0, :]), reads=[('oacc', 0, 0)], writes=[('dbt',)])
                        P.dma('sp', dbgo[:, :], dbt[:], reads=[('dbt',)], writes=[('dbgo',)])
                SKIP = os.environ.get('SKIP', '')
                if 'B' in SKIP:
                    P.op('pool', lambda e: e.memset(selT[:], 0.0), writes=[('selT',)])
                for qb4 in range(4 if 'B' not in SKIP else 0):
                    P.op('dve', lambda e, qb4=qb4: e.tensor_tensor(out=wk[:], in0=impa[:, qb4, :], in1=FTt[:, qb4, :], op=ALU.max), reads=[('impa', qb4), ('FTt',)], writes=[('wk',)])
                    P.op('dve', lambda e, qb4=qb4: e.tensor_tensor(out=wk[:], in0=wk[:], in1=CTt[:, qb4, :], op=ALU.min), reads=[('wk',), ('CTt',)], writes=[('wk',)])
                    P.op('dve', lambda e: e.max(out=m8[:], in_=wk[:]), reads=[('wk',)], writes=[('m8',)], strict=True)
                    P.op('dve', lambda e: e.match_replace(out=s1[:], in_to_replace=m8[:], in_values=wk[:], imm_value=-3.0e38), reads=[('wk',), ('m8',)], writes=[('s1',)], strict=True)
                    P.op('dve', lambda e: e.max(out=m8b[:], in_=s1[:]), reads=[('s1',)], writes=[('m8b',)], strict=True)
                    P.op('dve', lambda e: e.tensor_scalar(out=s1[:], in0=wk[:], scalar1=m8b[:, 7:8], scalar2=None, op0=ALU.is_ge), reads=[('wk',), ('m8b',)], writes=[('s1',)], strict=True)
                    P.op('dve', lambda e: e.tensor_scalar(out=s2[:], in0=wk[:], scalar1=-5.0e29, scalar2=None, op0=ALU.is_gt), reads=[('wk',)], writes=[('s2',)])
                    P.op('dve', lambda e: e.tensor_tensor(out=s1[:], in0=s1[:], in1=s2[:], op=ALU.mult), reads=[('s1',), ('s2',)], writes=[('s1',)])
                    P.op('dve', lambda e: e.tensor_scalar(out=selb[:], in0=s1[:], scalar1=-NB, scalar2=NB, op0=ALU.mult, op1=ALU.add), reads=[('s1',)], writes=[('selb',)])
                    ti = st['trr'] % 4; st['trr'] += 1
                    P.op('pe', lambda e, ti=ti: e.transpose(pst[0:64, ti * 128:(ti + 1) * 128], selb[:], ident[:]), reads=[('selb',), ('ident',)], writes=[('pst', ti)])
                    P.op('dve', lambda e, ti=ti, qb4=qb4: e.tensor_copy(out=selT[:, qb4 * 128:(qb4 + 1) * 128], in_=pst[0:64, ti * 128:(ti + 1) * 128]), reads=[('pst', ti)], writes=[('selT',)])
                for hh in range(4):
                    for br in [b_ for b_ in (1, 2) if not ((b_ == 1 and 'S' in SKIP) or (b_ == 2 and 'W' in SKIP))]:
                        kb_lo = 0 if br == 1 else max(0, 4 * Qc - 4)
                        kb_hi = 4 * Qc + 3
                        KT, VV, kkey, vkey, v1 = (ksT, VS, ('ksT',), ('VS',), ('VS1',)) if br == 1 else (kwT, VW, ('kwT',), ('VW',), ('VW1',))
                        for kb in range(kb_lo, kb_hi + 1):
                            j = kb - 4 * Qc
                            qlo = max(0, j) * 128
                            si = st['srr'] % 2; st['srr'] += 1
                            P.op('pe', lambda e, KT=KT, kb=kb, si=si, qlo=qlo, hh=hh: e.matmul(ps[si][:, qlo:512], KT[:, kb * 128:(kb + 1) * 128], qT[hh][:, q0 + qlo:q0 + 512], start=True, stop=False),
                                 reads=[kkey, ('qT', hh)], writes=[('ps', si)])
                            if br == 1:
                                P.op('pe', lambda e, kb=kb, si=si, qlo=qlo, j=j: e.matmul(ps[si][:, qlo:512], ExpT[:, kb, :], selT[:, qlo:512], start=False, stop=(j < 0)),
                                     reads=[('ExpT',), ('selT',)], writes=[('ps', si)])
                                if j >= 0:
                                    P.op('pe', lambda e, si=si, qlo=qlo: e.matmul(ps[si][:, qlo:qlo + 128], ident[:], trib[:], start=False, stop=True),
                                         reads=[('ident',), ('trib',)], writes=[('ps', si)])
                            else:
                                oi_ = 4 * Qc - kb + 3
                                P.op('pe', lambda e, si=si, qlo=qlo, oi_=oi_: e.matmul(ps[si][:, qlo:512], ident[:], WB[:, oi_, qlo:512], start=False, stop=True),
                                     reads=[('ident',), ('WB',)], writes=[('ps', si)])
                            pi = st['prr'] % 3; st['prr'] += 1
                            P.op('act', lambda e, si=si, pi=pi, qlo=qlo: e.activation(out=pT[pi][:, qlo:512], in_=ps[si][:, qlo:512], func=AF.Exp), reads=[('ps', si)], writes=[('pT', pi)])
                            for qb4 in range(max(0, j), 4):
                                last_kb = 4 * Qc + qb4
                                P.op('pe', lambda e, VV=VV, pi=pi, qb4=qb4, kb=kb, kb_lo=kb_lo, last_kb=last_kb: e.matmul(ps[2 + qb4][:, 0:130], pT[pi][:, qb4 * 128:(qb4 + 1) * 128], VV[:, kb, 0:130], start=(kb == kb_lo), stop=(kb == last_kb)),
                                     reads=[('pT', pi), vkey, v1], writes=[('O', qb4)])
                        finish_branch(hh, Qc, br, False)
                    ogi = st['ogrr'] % 2; st['ogrr'] += 1
                    for qb4 in range(4 if 'O' not in SKIP else 0):
                        oi = st['onrr'] % 2; st['onrr'] += 1
                        P.op('dve', lambda e, oi=oi, hh=hh, qb4=qb4: e.tensor_copy(out=on[oi][:], in_=oacc[hh][:, qb4, :]), reads=[('oacc', hh, qb4)], writes=[('on', oi)])
                        if 'P' in SKIP:
                            continue
                        ti = st['trr'] % 4; st['trr'] += 1
                        P.op('pe', lambda e, oi=oi, ti=ti: e.transpose(pst[:, ti * 128:(ti + 1) * 128], on[oi][:], ident[:]), reads=[('on', oi), ('ident',)], writes=[('pst', ti)])
                        if 'G' in SKIP:
                            continue
                        P.op('dve', lambda e, ti=ti, qb4=qb4, ogi=ogi: e.tensor_copy(out=og[ogi][:, qb4 * 128:(qb4 + 1) * 128], in_=pst[:, ti * 128:(ti + 1) * 128]), reads=[('pst', ti)], writes=[('og', ogi)])
                    orow = (g * 4 + hh) * 128
                    if 'D' not in SKIP:
                      P.dma('sp', oT[orow:orow + 128, q0:q0 + 512], og[ogi][:], reads=[('og', ogi)], writes=[('oT', g, hh, Qc)])
            for Qc in range(NQC):
                chunk(g, Qc)
        P.emit_stage()


def host_consts1():
    c0 = host_consts()
    out = dict(ropeC=c0['ropeC'], ropeS=c0['ropeS'], Pm=c0['Pm'], ident=c0['ident'])
    inv = 1.0 / (500000.0 ** (np.arange(0, 32, 2, dtype=np.float32) / 32))
    posc = (np.arange(256) * 16 + 31).astype(np.float32)
    ang = posc[None, :] * inv[:, None]
    cos = np.cos(ang).astype(np.float32); sin = np.sin(ang).astype(np.float32)
    out['ropeCc'] = np.concatenate([cos, cos], 0); out['ropeSc'] = np.concatenate([-sin, sin], 0)
    n = np.arange(256)
    start = n * 16
    js = np.arange(64) * 64
    ov = ((start[:, None] < js[None, :] + 64) & (start[:, None] + 32 > js[None, :])).astype(np.float32)
    ov[255] = 0
    out['ovl'] = ov
    q = np.arange(S)
    cm = ((16 * n + 31)[:, None] <= q[None, :]).astype(np.float32)
    cm[255] = 0
    out['cmpM'] = cm
    cur = (q // 64)[:, None]
    jj = np.arange(64)[None, :]
    forced = (jj == 0) | (jj == cur) | (jj == cur - 1)
    out['FT'] = np.where(forced, 1e9, 0.0).astype(np.float32)
    out['CT'] = np.where(jj <= cur, 3.0e38, -1e30).astype(np.float32)
    E = np.zeros((64, 32, 128), np.float32)
    for kb in range(32):
        for p in range(128):
            E[2 * kb + p // 64, kb, p] = 1.0
    out['ExpT'] = E.reshape(64, 32 * 128)
    WBt = np.zeros((128, 8, 512), np.float32)
    for oi in range(8):
        o = 128 * (oi - 3)
        d = o + np.arange(512)[None, :] - np.arange(128)[:, None]
        WBt[:, oi, :] = np.where((d >= 0) & (d < 512), 0.0, NB)
    out['WB'] = WBt.reshape(128, 8 * 512)
    p = np.arange(128)[:, None]; c = np.arange(128)[None, :]
    out['trib'] = np.where(p > c, NB, 0.0).astype(np.float32)
    return out


def host_inputs_mix1(inp, xT_b, half):
    w = inp['odd_w_in'][0]
    q = w[:, 0:2048][:, half * 1024:(half + 1) * 1024]
    def grp(i):
        blk = w[:, 2048 + i * 512:2048 + (i + 1) * 512]
        return blk[:, half * 256:(half + 1) * 256]
    kc, vc, ks, vs, kw, vw = [grp(i) for i in range(6)]
    gt = w[:, 2048 + 3072:2048 + 3072 + 48][:, half * 24:(half + 1) * 24]
    gpad = np.zeros((D, 256), np.float32); gpad[:, :24] = gt
    m = dict(
        xT=xT_b,
        g=np.ascontiguousarray(inp['ln_mix_g'][1].reshape(16, 128).T),
        Wfm=np.ascontiguousarray(np.concatenate([q, ks, kw, kc, vc], 1)),
        Wtm=np.ascontiguousarray(np.concatenate([vs, vw, gpad], 1)),
        gains=np.ascontiguousarray(np.stack([inp['odd_g_q'][0], inp['odd_g_ks'][0], inp['odd_g_kw'][0], inp['odd_g_kc'][0]], 1)),
        w1k=inp['odd_phi_k_w1'][0], w2k=inp['odd_phi_k_w2'][0], peTk=np.ascontiguousarray(inp['odd_phi_k_pe'][0].T),
        w1v=inp['odd_phi_v_w1'][0], w2v=inp['odd_phi_v_w2'][0], peTv=np.ascontiguousarray(inp['odd_phi_v_pe'][0].T),
    )
    m.update(host_consts1())
    return m


F = 8192


def emit_phaseC(nc, P, tag, xT, oT, w_out, g_in, w_up, w_down, hT, T, TT=512):
    NT = T // TT
    with ExitStack() as es:
        def sb(name, shape, dt):
            return es.enter_context(nc.sbuf_tensor(tag + name, shape, dt))
        wb = [sb(f"wb{i}", [128, 8192], BF16) for i in range(3)]
        ot = sb("ot", [128, 16, TT], BF16)
        ht = sb("ht", [128, 16, TT], F32)
        hn = sb("hn", [128, 16, TT], BF16)
        ut = sb("ut", [128, 32, TT], BF16)
        sq = [sb(f"sq{i}", [128, TT], BF16) for i in range(2)]
        rt = [sb(f"rt{i}", [128, TT], F32) for i in range(2)]
        rstd = sb("rstd", [128, TT], F32)
        ones = sb("ones", [128, 128], BF16)
        gt = sb("gt", [128, 16], F32)
        ps = [es.enter_context(nc.psum_tensor(tag + f"ps{i}", [128, 512], F32)) for i in range(8)]

        P.op('pool', lambda e: e.memset(ones[:], 1.0), writes=[('ones',)])
        P.dma('sp', gt[:], g_in[:, :], writes=[('g',)])

        jobs = []
        state = {'psrr': 0, 'sqrr': 0, 'rtrr': 0}

        def nextps():
            i = state['psrr'] % 7
            state['psrr'] += 1
            return i

        for t in range(NT):
            tsl = slice(t * TT, (t + 1) * TT)

            def tile_begin(t=t, tsl=tsl):
                P.dma('pool', ot[:], oT[:, tsl].rearrange("(kc p) n -> p kc n", p=128),
                      writes=[('ot',)])
                P.dma('sp', ht[:], xT[:, tsl].rearrange("(kc p) n -> p kc n", p=128),
                      writes=[('ht', dc) for dc in range(16)])

            for blk in range(4):
                def load(bi, blk=blk):
                    v = wb[bi][:].rearrange("p (kc c) -> p kc c", kc=16)
                    P.dma('pool', v, w_out[:, blk * 512:(blk + 1) * 512].rearrange("(kc p) c -> p kc c", p=128),
                          writes=[('wb', bi)])

                def comp(bi, blk=blk, t=t, first=(blk == 0), tb=tile_begin):
                    if first:
                        tb()
                    v = wb[bi][:].rearrange("p (kc c) -> p kc c", kc=16)
                    for dcl in range(4):
                        dc = blk * 4 + dcl
                        pi = nextps()
                        for kc in range(16):
                            P.op('pe', lambda e, kc=kc, pi=pi, dcl=dcl: e.matmul(ps[pi][:], v[:, kc, dcl * 128:(dcl + 1) * 128], ot[:, kc, :], start=(kc == 0), stop=(kc == 15)),
                                 reads=[('wb', bi), ('ot',)], writes=[('ps', pi)])
                        P.op('dve', lambda e, pi=pi, dc=dc: e.tensor_tensor(out=ht[:, dc, :], in0=ps[pi][:], in1=ht[:, dc, :], op=ALU.add),
                             reads=[('ps', pi), ('ht', dc)], writes=[('ht', dc)])
                jobs.append((load, comp))

            def norm():
                pn = 7
                for dc in range(16):
                    si = state['sqrr'] % 2
                    state['sqrr'] += 1
                    P.op('act', lambda e, dc=dc, si=si: e.activation(out=sq[si][:], in_=ht[:, dc, :], func=AF.Square),
                         reads=[('ht', dc)], writes=[('sq', si)])
                    P.op('pe', lambda e, dc=dc, si=si: e.matmul(ps[pn][:], ones[:], sq[si][:], start=(dc == 0), stop=(dc == 15)),
                         reads=[('sq', si), ('ones',)], writes=[('ps', pn)])
                P.op('act', lambda e: e.activation(out=rstd[:], in_=ps[pn][:], func=AF.Sqrt, scale=1.0 / D, bias=EPS),
                     reads=[('ps', pn)], writes=[('rstd',)])
                P.op('dve', lambda e: e.reciprocal(out=rstd[:], in_=rstd[:]), reads=[('rstd',)], writes=[('rstd',)])
                for dc in range(16):
                    P.op('dve', lambda e, dc=dc: e.scalar_tensor_tensor(out=hn[:, dc, :], in0=ht[:, dc, :], scalar=gt[:, dc:dc + 1], in1=rstd[:], op0=ALU.mult, op1=ALU.mult),
                         reads=[('ht', dc), ('g',), ('rstd',)], writes=[('hn', dc)])

            for hf in range(2):
                for blk in range(8):
                    c0 = hf * 4096 + blk * 512

                    def load(bi, c0=c0):
                        v = wb[bi][:].rearrange("p (kc c) -> p kc c", kc=16)
                        P.dma('pool', v, w_up[:, c0:c0 + 512].rearrange("(kc p) c -> p kc c", p=128), writes=[('wb', bi)])

                    def comp(bi, blk=blk, hf=hf, donorm=(hf == 0 and blk == 0), nf=norm):
                        if donorm:
                            nf()
                        v = wb[bi][:].rearrange("p (kc c) -> p kc c", kc=16)
                        for fl in range(4):
                            fc = blk * 4 + fl
                            pi = nextps()
                            for kc in range(16):
                                P.op('pe', lambda e, kc=kc, pi=pi, fl=fl: e.matmul(ps[pi][:], v[:, kc, fl * 128:(fl + 1) * 128], hn[:, kc, :], start=(kc == 0), stop=(kc == 15)),
                                     reads=[('wb', bi), ('hn', kc)], writes=[('ps', pi)])
                            ri = state['rtrr'] % 2
                            state['rtrr'] += 1
                            P.op('act', lambda e, pi=pi, ri=ri: e.activation(out=rt[ri][:], in_=ps[pi][:], func=AF.Relu),
                                 reads=[('ps', pi)], writes=[('rt', ri)])
                            P.op('dve', lambda e, ri=ri, fc=fc: e.tensor_tensor(out=ut[:, fc, :], in0=rt[ri][:], in1=rt[ri][:], op=ALU.mult),
                                 reads=[('rt', ri)], writes=[('ut', fc)])
                    jobs.append((load, comp))
                for blk in range(8):
                    r0 = hf * 4096

                    def load(bi, blk=blk, r0=r0):
                        v = wb[bi][:].rearrange("p (fc c) -> p fc c", fc=32)
                        P.dma('pool', v, w_down[r0:r0 + 4096, blk * 256:(blk + 1) * 256].rearrange("(fc p) c -> p fc c", p=128), writes=[('wb', bi)])

                    def comp(bi, blk=blk, hf=hf, t=t, tsl=tsl):
                        v = wb[bi][:].rearrange("p (fc c) -> p fc c", fc=32)
                        for dcl in range(2):
                            dc = blk * 2 + dcl
                            pi = nextps()
                            for fc in range(32):
                                P.op('pe', lambda e, fc=fc, pi=pi, dcl=dcl: e.matmul(ps[pi][:], v[:, fc, dcl * 128:(dcl + 1) * 128], ut[:, fc, :], start=(fc == 0), stop=(fc == 31)),
                                     reads=[('wb', bi), ('ut', fc)], writes=[('ps', pi)])
                            P.op('dve', lambda e, pi=pi, dc=dc: e.tensor_tensor(out=ht[:, dc, :], in0=ps[pi][:], in1=ht[:, dc, :], op=ALU.add),
                                 reads=[('ps', pi), ('ht', dc)], writes=[('ht', dc)])
                        if hf == 1 and blk == 7:
                            P.dma('sp', hT[:, tsl].rearrange("(kc p) n -> p kc n", p=128), ht[:],
                                  reads=[('ht', dc) for dc in range(16)], writes=[('hTout', t)])
                    jobs.append((load, comp))

        nj = len(jobs)
        for i in range(min(2, nj)):
            jobs[i][0](i % 3)
        for i in range(nj):
            if i + 2 < nj:
                jobs[i + 2][0]((i + 2) % 3)
            jobs[i][1](i % 3)
        P.emit_stage()


IN_SPECS = [
    ("xT", [D, S]), ("g0", [128, 16]), ("Wfm0", [D, 5120]), ("Wtm0", [D, 2048]), ("Wf0", [D, 8]), ("bf0", [8, 1]),
    ("gains0", [128, 4]), ("ropeC", [32, S]), ("ropeS", [32, S]), ("Pm", [32, 32]), ("trim", [128, 128]),
    ("ident", [128, 128]), ("Mt", [128, 20 * 512]),
    ("w_out0", [D, D]), ("gm0", [128, 16]), ("w_up0", [D, F]), ("w_down0", [F, D]),
    ("g1", [128, 16]), ("Wfm1", [D, 4096]), ("Wtm1", [D, 1280]), ("gains1", [128, 4]),
    ("ropeCc", [32, 256]), ("ropeSc", [32, 256]),
    ("w1k", [4096, 128]), ("w2k", [128, 128]), ("peTk", [128, 32]), ("w1v", [4096, 128]), ("w2v", [128, 128]), ("peTv", [128, 32]),
    ("ovl", [256, 64]), ("cmpM", [256, S]), ("FT", [S, 64]), ("CT", [S, 64]), ("ExpT", [64, 32 * 128]), ("WB", [128, 8 * 512]),
    ("trib", [128, 128]),
    ("w_out1", [D, D]), ("gm1", [128, 16]), ("w_up1", [D, F]), ("w_down1", [F, D]),
]


def build_fused():
    nc = bass.Bass("TRN2", target_bir_lowering=False)
    T_ = {}
    for name, shape in IN_SPECS:
        T_[name] = nc.dram_tensor(name, shape, F32, kind="ExternalInput").ap()
    T_['o0T'] = nc.dram_tensor("o0T", [D, S], F32, kind="Internal").ap()
    T_['h2T'] = nc.dram_tensor("h2T", [D, S], F32, kind="Internal").ap()
    T_['o1T'] = nc.dram_tensor("o1T", [D, S], F32, kind="Internal").ap()
    hT = nc.dram_tensor("hT", [D, S], F32, kind="ExternalOutput").ap()
    P = Prog(nc)
    emit_mix0(nc, P, T_, NH=8)
    emit_phaseC(nc, P, "c0_", T_['xT'], T_['o0T'], T_['w_out0'], T_['gm0'], T_['w_up0'], T_['w_down0'], T_['h2T'], S)
    emit_mix1(nc, P, T_, NG=4)
    emit_phaseC(nc, P, "c1_", T_['h2T'], T_['o1T'], T_['w_out1'], T_['gm1'], T_['w_up1'], T_['w_down1'], hT, S)
    return nc


def host_shared_inputs(inp):
    def gl(v):
        return np.ascontiguousarray(v.reshape(16, 128).T)
    w = inp['even_w_in'][0]
    m = dict(
        g0=gl(inp['ln_mix_g'][0]),
        Wfm0=np.ascontiguousarray(np.concatenate([w[:, 0:1024], w[:, 1024:2048], w[:, 3072:4096], w[:, 4104:5128], w[:, 5128:6152]], 1)),
        Wtm0=np.ascontiguousarray(np.concatenate([w[:, 2048:3072], w[:, 6152:7176]], 1)),
        Wf0=np.ascontiguousarray(w[:, 4096:4104]),
        bf0=np.ascontiguousarray(inp['even_b_f'][0].reshape(8, 1)),
        gains0=np.ascontiguousarray(np.stack([inp['even_g_q_fox'][0], inp['even_g_k_fox'][0], inp['even_g_q_dil'][0], inp['even_g_k_dil'][0]], 1)),
        w_out0=inp['even_w_out'][0], gm0=gl(inp['ln_mlp_g'][0]), w_up0=inp['w_mlp_up'][0], w_down0=inp['w_mlp_down'][0],
    )
    w = inp['odd_w_in'][0]
    def grp(i):
        return w[:, 2048 + i * 512:2048 + (i + 1) * 512]
    kc, vc, ks, vs, kw, vw = [grp(i) for i in range(6)]
    gpad = np.zeros((D, 256), np.float32); gpad[:, :48] = w[:, 5120:5168]
    m.update(
        g1=gl(inp['ln_mix_g'][1]),
        Wfm1=np.ascontiguousarray(np.concatenate([w[:, 0:2048], ks, kw, kc, vc], 1)),
        Wtm1=np.ascontiguousarray(np.concatenate([vs, vw, gpad], 1)),
        gains1=np.ascontiguousarray(np.stack([inp['odd_g_q'][0], inp['odd_g_ks'][0], inp['odd_g_kw'][0], inp['odd_g_kc'][0]], 1)),
        w1k=inp['odd_phi_k_w1'][0], w2k=inp['odd_phi_k_w2'][0], peTk=np.ascontiguousarray(inp['odd_phi_k_pe'][0].T),
        w1v=inp['odd_phi_v_w1'][0], w2v=inp['odd_phi_v_w2'][0], peTv=np.ascontiguousarray(inp['odd_phi_v_pe'][0].T),
        w_out1=inp['odd_w_out'][0], gm1=gl(inp['ln_mlp_g'][1]), w_up1=inp['w_mlp_up'][1], w_down1=inp['w_mlp_down'][1],
    )
    m.update(host_consts())
    m.update(host_consts1())
    return m


def kernel(**inp):
    inp = {k: np.asarray(v) for k, v in inp.items()}
    x = inp['x']
    B = x.shape[0]
    cores = list(range(8))
    shared = host_shared_inputs(inp)
    nc = build_fused()
    maps = []
    for c in cores:
        m = dict(shared)
        m['xT'] = np.ascontiguousarray(x[c // 2].T)
        maps.append({name: np.ascontiguousarray(m[name], dtype=np.float32) for name, _ in IN_SPECS})
    res = run_bass_kernel_spmd(nc, maps, core_ids=cores).results
    return np.stack([np.ascontiguousarray(np.asarray(res[2 * b]["hT"]).T) for b in range(B)], axis=0).astype(np.float32)
```

```python
import numpy as np, time, sys, os
from contextlib import ExitStack
from concourse.bass_utils import run_bass_kernel_spmd
import numpy as np
import concourse.bass as bass
import concourse.mybir as mybir

F32 = mybir.dt.float32
BF16 = mybir.dt.bfloat16
AF = mybir.ActivationFunctionType
ALU = mybir.AluOpType
AX = mybir.AxisListType

ENGS = ['pe', 'act', 'dve', 'pool', 'sp']


class _Op:
    __slots__ = ('eng', 'fn', 'deps', 'dma', 'needed', 'sem', 'val', 'strict')


class Prog:
    def __init__(self, nc, n_slots=8, same_engine_sync=False):
        self.nc = nc
        self.same = same_engine_sync
        self.n_slots = n_slots
        self.sems = {}
        self.cnt = {e: 0 for e in ENGS}
        self.dma_cum = {}
        self.dma_last = {}
        self.dma_rr = {e: 0 for e in ENGS}
        self.waited = {e: {} for e in ENGS}
        self._stack = []
        for e in ['pe', 'act', 'dve', 'pool']:
            self.sems[e] = nc.alloc_semaphore(name=f"s_{e}")
        for q in ['sp', 'pool', 'act']:
            for s in range(n_slots):
                k = (q, s)
                self.sems[k] = nc.alloc_semaphore(name=f"d_{q}{s}")
                self.dma_cum[k] = 0
                self.dma_last[k] = None
        self.reset_stage()

    def reset_stage(self):
        self.ops = []
        self.last_w = {}
        self.readers = {}

    def _eng(self, e):
        nc = self.nc
        return {'pe': nc.tensor, 'act': nc.scalar, 'dve': nc.vector, 'pool': nc.gpsimd, 'sp': nc.sync}[e]

    def op(self, eng, fn, reads=(), writes=(), dma=False, strict=False):
        o = _Op()
        o.eng = eng; o.fn = fn; o.dma = dma; o.needed = False; o.sem = None; o.val = None; o.strict = strict
        deps = []
        for k in reads:
            w = self.last_w.get(k)
            if w is not None:
                deps.append(w)
        for k in writes:
            w = self.last_w.get(k)
            if w is not None:
                deps.append(w)
            deps.extend(self.readers.get(k, ()))
        if dma:
            slot = (eng, self.dma_rr[eng] % self.n_slots)
            self.dma_rr[eng] += 1
            prev = self.dma_last[slot]
            if prev is not None:
                deps.append(prev)
            self.dma_last[slot] = o
            self.dma_cum[slot] += 16
            o.sem = slot
            o.val = self.dma_cum[slot]
        o.deps = deps
        for k in reads:
            self.readers.setdefault(k, []).append(o)
        for k in writes:
            self.last_w[k] = o
            self.readers[k] = []
        self.ops.append(o)
        return o

    def dma(self, q, out, in_, reads=(), writes=(), **kw):
        return self.op(q, lambda e: e.dma_start(out=out, in_=in_, **kw), reads=reads, writes=writes, dma=True)

    def emit_stage(self, block_name=None, final_wait_all_dma=True):
        nc = self.nc
        ops = self.ops
        for o in ops:
            for d in o.deps:
                if d.dma:
                    continue
                if d.eng == o.eng and not o.dma and not self.same and not o.strict:
                    continue
                d.needed = True
        for o in ops:
            if not o.dma and o.needed:
                self.cnt[o.eng] += 1
                o.sem = o.eng
                o.val = self.cnt[o.eng]
        per = {e: [] for e in ENGS}
        for o in ops:
            per[o.eng].append(o)
        sems = self.sems
        waited = self.waited
        same = self.same
        dma_cum = self.dma_cum

        def emit_engine(ename, eng):
            wd = waited[ename]
            for o in per[ename]:
                need = {}
                for d in o.deps:
                    if d.val is None:
                        continue
                    if (not d.dma) and d.eng == ename and (not o.dma) and (not same) and (not o.strict):
                        continue
                    s = d.sem
                    if need.get(s, 0) < d.val:
                        need[s] = d.val
                for s, v in need.items():
                    if wd.get(s, 0) >= v:
                        continue
                    eng.wait_ge(sems[s], v)
                    wd[s] = v
                ins = o.fn(eng)
                if o.dma:
                    ins.then_inc(sems[o.sem], 16)
                elif o.needed:
                    ins.then_inc(sems[o.sem], 1)
            if final_wait_all_dma:
                for s, v in dma_cum.items():
                    if s[0] == ename and v > 0 and wd.get(s, 0) < v:
                        eng.wait_ge(sems[s], v)
                        wd[s] = v

        with nc.Block() as block:
            @block.tensor
            def _(e):
                emit_engine('pe', e)

            @block.scalar
            def _(e):
                emit_engine('act', e)

            @block.vector
            def _(e):
                emit_engine('dve', e)

            @block.gpsimd
            def _(e):
                emit_engine('pool', e)

            @block.sync
            def _(e):
                emit_engine('sp', e)
        for e in ENGS:
            for s in self.sems:
                if isinstance(s, str):
                    self.waited[e][s] = self.cnt[s]
                else:
                    self.waited[e][s] = self.dma_cum[s]
        self.reset_stage()

from contextlib import ExitStack

D = 2048; EPS = 1e-6; S = 4096
SCALE = 128 ** -0.5
NEGM = -1e30


def norm_stage(nc, P, es, sb, ps, xT, g_in, hnT, ones):
    xt = sb("xt", [128, 16, 256], F32)
    gt = sb("gt", [128, 16], F32)
    sq = [sb(f"nsq{i}", [128, 256], BF16) for i in range(2)]
    rstd = sb("nrstd", [128, 256], F32)
    P.dma('sp', gt[:], g_in[:, :], writes=[('g',)])
    pn = 6
    for tt in range(S // 256):
        tsl = slice(tt * 256, (tt + 1) * 256)
        P.dma('sp', xt[:], xT[:, tsl].rearrange("(kc p) n -> p kc n", p=128), writes=[('xt', dc) for dc in range(16)])
        for dc in range(16):
            si = dc % 2
            P.op('act', lambda e, dc=dc, si=si: e.activation(out=sq[si][:], in_=xt[:, dc, :], func=AF.Square),
                 reads=[('xt', dc)], writes=[('nsq', si)])
            P.op('pe', lambda e, dc=dc, si=si: e.matmul(ps[pn][:, 0:256], ones[:], sq[si][:], start=(dc == 0), stop=(dc == 15)),
                 reads=[('nsq', si), ('ones',)], writes=[('ps', pn)])
        P.op('act', lambda e: e.activation(out=rstd[:], in_=ps[pn][:, 0:256], func=AF.Sqrt, scale=1.0 / D, bias=EPS),
             reads=[('ps', pn)], writes=[('nrstd',)])
        P.op('dve', lambda e: e.reciprocal(out=rstd[:], in_=rstd[:]), reads=[('nrstd',)], writes=[('nrstd',)])
        for dc in range(16):
            P.op('dve', lambda e, dc=dc, tsl=tsl: e.scalar_tensor_tensor(out=hnT[:, dc, tsl], in0=xt[:, dc, :], scalar=gt[:, dc:dc + 1], in1=rstd[:], op0=ALU.mult, op1=ALU.mult),
                 reads=[('xt', dc), ('g',), ('nrstd',)], writes=[('hnT', tt // 2)])


class ProjCtx:
    pass


def run_pipeline(tasks, depth=2):
    n = len(tasks)
    for i in range(min(depth, n)):
        tasks[i].pre()
        tasks[i].a()
    for t in range(n):
        if t + depth < n:
            tasks[t + depth].pre()
            tasks[t + depth].a()
        tasks[t].b()
        tasks[t].c()
        tasks[t].post()


def proj_fm(nc, P, C, wsrc, ncols, specs, dst):
    nblk = ncols // 256
    state = C.state
    def load(blk, bi):
        P.dma('pool', C.wbl[bi][:], wsrc[:, blk * 256:(blk + 1) * 256].rearrange("(kc p) c -> p kc c", p=128), writes=[('wbl', bi)])
    for blk in range(nblk):
        bi = state['wrr'] % 2
        state['wrr'] += 1
        load(blk, bi)
        for sub in range(2):
            j = blk * 2 + sub
            sp = specs[j]
            for tt in range(8):
                tsl = slice(tt * 512, (tt + 1) * 512)
                pi = state['psrr'] % 6
                state['psrr'] += 1
                for kc in range(16):
                    P.op('pe', lambda e, kc=kc, pi=pi, sub=sub, bi=bi, tsl=tsl: e.matmul(C.ps[pi][:], C.wbl[bi][:, kc, sub * 128:(sub + 1) * 128], C.hnT[:, kc, tsl], start=(kc == 0), stop=(kc == 15)),
                         reads=[('wbl', bi), ('hnT', tt)], writes=[('ps', pi)])
                qi = state['qrr'] % 3
                state['qrr'] += 1
                qn = C.qn[qi]
                if sp['kind'] == 'raw':
                    P.op('act', lambda e, pi=pi, qn=qn: e.activation(out=qn[:], in_=C.ps[pi][:], func=AF.Copy),
                         reads=[('ps', pi)], writes=[('qn', qi)])
                else:
                    si = state['sqrr'] % 2
                    state['sqrr'] += 1
                    P.op('act', lambda e, pi=pi, si=si: e.activation(out=C.sq[si][:], in_=C.ps[pi][:], func=AF.Square),
                         reads=[('ps', pi)], writes=[('sq', si)])
                    P.op('pe', lambda e, si=si: e.matmul(C.ps[6][:], C.ones[:], C.sq[si][:], start=True, stop=True),
                         reads=[('sq', si), ('ones',)], writes=[('ps', 6)])
                    ri = state['rsrr'] % 2
                    state['rsrr'] += 1
                    P.op('act', lambda e, ri=ri: e.activation(out=C.rs[ri][:], in_=C.ps[6][:], func=AF.Sqrt, scale=1.0 / 128, bias=EPS),
                         reads=[('ps', 6)], writes=[('rs', ri)])
                    P.op('dve', lambda e, ri=ri: e.reciprocal(out=C.rs[ri][:], in_=C.rs[ri][:]), reads=[('rs', ri)], writes=[('rs', ri)])
                    gc = sp['gain']
                    P.op('dve', lambda e, pi=pi, qn=qn, ri=ri, gc=gc: e.scalar_tensor_tensor(out=qn[:], in0=C.ps[pi][:], scalar=C.gains[:, gc:gc + 1], in1=C.rs[ri][:], op0=ALU.mult, op1=ALU.mult),
                         reads=[('ps', pi), ('rs', ri), ('gains',)], writes=[('qn', qi)])
                    if sp.get('rope'):
                        ct, st = sp['tabs']
                        tofs = sp.get('tofs', 0)
                        tl = slice(tofs + tt * 512, tofs + (tt + 1) * 512)
                        P.op('pe', lambda e, qn=qn: e.matmul(C.ps[7][0:32, :], C.Pm[:], qn[0:32, :], start=True, stop=True),
                             reads=[('qn', qi), ('Pm',)], writes=[('ps', 7)])
                        P.op('dve', lambda e, tl=tl, st=st: e.tensor_tensor(out=C.r1[:], in0=C.ps[7][0:32, :], in1=st[:, tl], op=ALU.mult),
                             reads=[('ps', 7), ('tabs',)], writes=[('r1',)])
                        P.op('dve', lambda e, tl=tl, qn=qn, ct=ct: e.tensor_tensor(out=C.r2[:], in0=qn[0:32, :], in1=ct[:, tl], op=ALU.mult),
                             reads=[('qn', qi), ('tabs',)], writes=[('r2',)])
                        P.op('dve', lambda e, qn=qn: e.tensor_tensor(out=qn[0:32, :], in0=C.r1[:], in1=C.r2[:], op=ALU.add),
                             reads=[('r1',), ('r2',)], writes=[('qn', qi)])
                P.dma('sp', dst(j, tt), qn[:], reads=[('qn', qi)], writes=[('dst_fm', id(dst), j, tt)])


def proj_tm(nc, P, C, wsrc, ncols, dst):
    nblk = ncols // 256
    state = C.state
    for blk in range(nblk):
        bi = state['wrr'] % 2
        state['wrr'] += 1
        P.dma('pool', C.wbl[bi][:], wsrc[:, blk * 256:(blk + 1) * 256].rearrange("(kc p) c -> p kc c", p=128), writes=[('wbl', bi)])
        for tb4 in range(8):
            vi = state['vrr'] % 2
            state['vrr'] += 1
            for i in range(4):
                tb = tb4 * 4 + i
                pi = state['psrr'] % 6
                state['psrr'] += 1
                for kc in range(16):
                    P.op('pe', lambda e, kc=kc, pi=pi, bi=bi, tb=tb: e.matmul(C.ps[pi][:, 0:256], C.hnT[:, kc, tb * 128:(tb + 1) * 128], C.wbl[bi][:, kc, :], start=(kc == 0), stop=(kc == 15)),
                         reads=[('wbl', bi), ('hnT', tb // 4)], writes=[('ps', pi)])
                P.op('act', lambda e, pi=pi, vi=vi, i=i: e.activation(out=C.vst[vi][:, i, :], in_=C.ps[pi][:, 0:256], func=AF.Copy),
                     reads=[('ps', pi)], writes=[('vst', vi)])
            P.dma('sp', dst(blk, tb4), C.vst[vi][:], reads=[('vst', vi)], writes=[('dst_tm', blk, tb4)])


def attn_out_block(nc, P, C, Ops, qb4, gate_ap, og, key_og, gate_key=None):
    st = C.state
    oi = st['onrr'] % 2
    st['onrr'] += 1
    P.op('dve', lambda e, oi=oi: e.reciprocal(out=C.rden[oi][:], in_=Ops[:, 128:129]),
         reads=[('O', qb4)], writes=[('rden', oi)])
    P.op('dve', lambda e, oi=oi: e.tensor_scalar(out=C.on[oi][:], in0=Ops[:, 0:128], scalar1=C.rden[oi][:, 0:1], scalar2=None, op0=ALU.mult),
         reads=[('O', qb4), ('rden', oi)], writes=[('on', oi)], strict=True)
    ti = st['trr'] % 4
    st['trr'] += 1
    P.op('pe', lambda e, oi=oi, ti=ti: e.transpose(C.pst[:, ti * 128:(ti + 1) * 128], C.on[oi][:], C.ident[:]),
         reads=[('on', oi), ('ident',)], writes=[('pst', ti)])
    if gate_ap is not None:
        P.op('dve', lambda e, ti=ti: e.tensor_tensor(out=og[:, qb4 * 128:(qb4 + 1) * 128], in0=C.pst[:, ti * 128:(ti + 1) * 128], in1=gate_ap, op=ALU.mult),
             reads=[('pst', ti), gate_key], writes=[key_og])
    else:
        P.op('dve', lambda e, ti=ti: e.tensor_copy(out=og[:, qb4 * 128:(qb4 + 1) * 128], in_=C.pst[:, ti * 128:(ti + 1) * 128]),
             reads=[('pst', ti)], writes=[key_og])


def emit_mix0(nc, P, T_, NH=8):
    xT = T_['xT']; g_in = T_['g0']; Wfm = T_['Wfm0']; Wtm = T_['Wtm0']; Wf = T_['Wf0']; bfv = T_['bf0']
    gains_in = T_['gains0']; ropeC = T_['ropeC']; ropeS = T_['ropeS']; Pm_in = T_['Pm']; trim_in = T_['trim']
    ident_in = T_['ident']; Mt_in = T_['Mt']; oT = T_['o0T']
    debug_stage1_only = False
    qk_s = nc.dram_tensor("a_qk_s", [5 * NH, 128, S], BF16, kind="Internal").ap()
    v_s = nc.dram_tensor("a_v_s", [S, 2 * NH * 128], BF16, kind="Internal").ap()
    c_s = nc.dram_tensor("a_c_s", [NH, S], F32, kind="Internal").ap()
    nheads_fox = nheads_dil = NH

    with ExitStack() as es:
        def sb(name, shape, dt):
            return es.enter_context(nc.sbuf_tensor("a1_" + name, shape, dt))
        C = ProjCtx()
        C.state = dict(wrr=0, psrr=0, qrr=0, sqrr=0, rsrr=0, vrr=0)
        C.ps = [es.enter_context(nc.psum_tensor(f"a1ps{i}", [128, 512], F32)) for i in range(8)]
        C.hnT = sb("hnT", [128, 16, S], BF16)
        C.ones = sb("ones", [128, 128], BF16)
        C.wbl = [sb(f"wbl{i}", [128, 16, 256], BF16) for i in range(2)]
        C.Ct = sb("Ct", [32, S], BF16)
        C.St = sb("St", [32, S], BF16)
        C.Pm = sb("Pmt", [32, 32], BF16)
        C.gains = sb("gains", [128, 4], F32)
        C.sq = [sb(f"sq{i}", [128, 512], BF16) for i in range(2)]
        C.rs = [sb(f"rs{i}", [128, 512], F32) for i in range(2)]
        C.qn = [sb(f"qn{i}", [128, 512], BF16) for i in range(3)]
        C.r1 = sb("r1", [32, 512], F32)
        C.r2 = sb("r2", [32, 512], F32)
        C.vst = [sb(f"vst{i}", [128, 4, 256], BF16) for i in range(2)]
        wft = sb("wft", [128, 16, NH], BF16)
        bft = sb("bft", [NH, 1], F32)
        onesr = sb("onesr", [NH, 512], F32)
        fx = sb("fx", [NH, 512], F32)
        fa_ = sb("fa_", [NH, 512], F32)
        fm_ = sb("fm_", [NH, 512], F32)
        ct = [sb(f"ct{i}", [NH, 512], F32) for i in range(2)]

        P.op('pool', lambda e: e.memset(C.ones[:], 1.0), writes=[('ones',)])
        P.op('pool', lambda e: e.memset(onesr[:], 1.0), writes=[('onesr',)])
        P.dma('pool', C.Ct[:], ropeC[:, :], writes=[('tabs',)])
        P.dma('pool', C.St[:], ropeS[:, :], writes=[('tabs',)])
        P.dma('pool', C.Pm[:], Pm_in[:, :], writes=[('Pm',)])
        P.dma('sp', C.gains[:], gains_in[:, :], writes=[('gains',)])
        P.dma('pool', wft[:], Wf.rearrange("(kc p) c -> p kc c", p=128), writes=[('wft',)])
        P.dma('sp', bft[:], bfv[:, :], writes=[('bft',)])
        P.op('act', lambda e: e.mul(C.gains[:, 0:1], C.gains[:, 0:1], SCALE), reads=[('gains',)], writes=[('gains',)])
        P.op('act', lambda e: e.mul(C.gains[:, 2:3], C.gains[:, 2:3], SCALE), reads=[('gains',)], writes=[('gains',)])

        norm_stage(nc, P, es, sb, C.ps, xT, g_in, C.hnT, C.ones)

        for tt in range(8):
            tsl = slice(tt * 512, (tt + 1) * 512)
            for kc in range(16):
                P.op('pe', lambda e, kc=kc, tsl=tsl: e.matmul(C.ps[7][0:NH, :], wft[:, kc, :], C.hnT[:, kc, tsl], start=(kc == 0), stop=(kc == 15)),
                     reads=[('wft',), ('hnT', tt)], writes=[('ps', 7)])
            P.op('dve', lambda e: e.tensor_scalar(out=fx[:], in0=C.ps[7][0:NH, :], scalar1=bft[:, 0:1], scalar2=None, op0=ALU.add),
                 reads=[('ps', 7), ('bft',)], writes=[('fx',)])
            P.op('act', lambda e: e.activation(out=fa_[:], in_=fx[:], func=AF.Abs), reads=[('fx',)], writes=[('fa_',)])
            P.op('act', lambda e: e.activation(out=fa_[:], in_=fa_[:], func=AF.Exp, scale=-1.0), reads=[('fa_',)], writes=[('fa_',)])
            P.op('act', lambda e: e.activation(out=fa_[:], in_=fa_[:], func=AF.Ln, bias=1.0), reads=[('fa_',)], writes=[('fa_',)])
            P.op('dve', lambda e: e.tensor_scalar(out=fm_[:], in0=fx[:], scalar1=0.0, scalar2=None, op0=ALU.min),
                 reads=[('fx',)], writes=[('fm_',)])
            P.op('dve', lambda e: e.tensor_tensor(out=fm_[:], in0=fm_[:], in1=fa_[:], op=ALU.subtract),
                 reads=[('fm_',), ('fa_',)], writes=[('fm_',)])
            ci = tt % 2
            init = 0.0 if tt == 0 else ct[1 - ci][:, 511:512]
            P.op('dve', lambda e, ci=ci, init=init: e.tensor_tensor_scan(out=ct[ci][:], data0=onesr[:], data1=fm_[:], initial=init, op0=ALU.mult, op1=ALU.add),
                 reads=[('fm_',), ('onesr',), ('ct', 1 - ci)], writes=[('ct', ci)], strict=True)
            P.dma('sp', c_s[:, tsl], ct[ci][:], reads=[('ct', ci)], writes=[('c_s', tt)])

        specs = ([dict(kind='norm', gain=0)] * NH + [dict(kind='norm', gain=1)] * NH + [dict(kind='raw')] * NH +
                 [dict(kind='norm', gain=2, rope=True, tabs=(C.Ct, C.St))] * NH + [dict(kind='norm', gain=3, rope=True, tabs=(C.Ct, C.St))] * NH)
        proj_fm(nc, P, C, Wfm, 5 * NH * 128, specs, lambda j, tt: qk_s[j, :, tt * 512:(tt + 1) * 512])
        proj_tm(nc, P, C, Wtm, 2 * NH * 128, lambda blk, tb4: v_s[tb4 * 512:(tb4 + 1) * 512, blk * 256:(blk + 1) * 256].rearrange("(i p) c -> p i c", p=128))
        P.emit_stage()


    with ExitStack() as es:
        def sb(name, shape, dt):
            return es.enter_context(nc.sbuf_tensor("a2_" + name, shape, dt))
        C = ProjCtx()
        C.state = dict(onrr=0, trr=0, srr=0, trr2=0, prr=0, ogrr=0)
        C.ps = [es.enter_context(nc.psum_tensor(f"a2ps{i}", [128, 512], F32)) for i in range(7)]
        C.pst = es.enter_context(nc.psum_tensor("a2pst", [128, 1024], BF16))
        C.ident = sb("ident", [128, 128], BF16)
        trim = sb("trim", [128, 128], F32)
        Mt = sb("Mt", [128, 20, 512], BF16)
        C.rden = [sb(f"rden{i}", [128, 1], F32) for i in range(2)]
        C.on = [sb(f"on{i}", [128, 128], BF16) for i in range(2)]
        tt_ = [sb(f"tt{i}", [128, 512], F32) for i in range(2)]
        et = [sb(f"et{i}", [128, 512], BF16) for i in range(2)]
        pT = [sb(f"pT{i}", [128, 512], BF16) for i in range(3)]
        og = [sb(f"og{i}", [128, 512], F32) for i in range(2)]
        hb = []
        for i in range(2):
            h_ = ProjCtx()
            h_.qT = sb(f"hq{i}", [128, S], BF16)
            h_.kT = sb(f"hk{i}", [128, S], BF16)
            h_.gT = sb(f"hg{i}", [128, S], BF16)
            h_.V = sb(f"hv{i}", [128, 32, 136], BF16)
            h_.cqb = sb(f"hcq{i}", [128, S], F32)
            h_.ck = sb(f"hck{i}", [128, 32], F32)
            hb.append(h_)
        P.dma('pool', C.ident[:], ident_in[:, :], writes=[('ident',)])
        P.dma('sp', trim[:], trim_in[:, :], writes=[('trim',)])
        P.dma('pool', Mt[:], Mt_in.rearrange("p (o c) -> p o c", o=20), writes=[('Mt',)])
        for i in range(2):
            P.op('pool', lambda e, i=i: e.memset(hb[i].V[:, :, 128:136], 1.0), writes=[('hb_V1', i)])

        heads = [('fox', h) for h in range(nheads_fox)] + [('dil', h) for h in range(nheads_dil)]
        S3 = [0, 1, 6]
        NQC0 = int(os.environ.get('NQC', '8'))

        def make_pre(hi, kind, h):
            bi = hi % 2
            H = hb[bi]
            kq, kk, kg, kv, kc_, kcq = [('hb', bi, n) for n in ('q', 'k', 'g', 'v', 'ck', 'cq')]
            if kind == 'fox':
                jq, jk, jg, vc0 = h, NH + h, 2 * NH + h, h * 128
            else:
                jq, jk, jg, vc0 = 3 * NH + h, 4 * NH + h, None, NH * 128 + h * 128

            def pre():
                P.dma('sp', H.qT[:], qk_s[jq, :, :], writes=[kq])
                P.dma('sp', H.kT[:], qk_s[jk, :, :], writes=[kk])
                P.dma('sp', H.V[:, :, 0:128], v_s[:, vc0:vc0 + 128].rearrange("(blk p) c -> p blk c", p=128), reads=[('hb_V1', bi)], writes=[kv])
                if kind == 'fox':
                    P.dma('sp', H.gT[:], qk_s[jg, :, :], writes=[kg])
                    P.op('act', lambda e: e.activation(out=H.gT[:], in_=H.gT[:], func=AF.Sigmoid), reads=[kg], writes=[kg])
                    P.dma('sp', H.ck[:], c_s[h, :].rearrange("(blk p) -> p blk", p=128), reads=[], writes=[kc_], allow_slow_non_contiguous=True)
                    P.op('pool', lambda e: e.tensor_scalar(out=H.ck[:], in0=H.ck[:], scalar1=-1.0, scalar2=None, op0=ALU.mult), reads=[kc_], writes=[kc_])
                    P.dma('sp', H.cqb[:], c_s[h, :].partition_broadcast(128), writes=[kcq])
            return pre

        def make_tile(hi, kind, h, Qc, kb, pre, last_of_chunk, ogi):
            bi = hi % 2
            H = hb[bi]
            kq, kk, kg, kv, kc_, kcq = [('hb', bi, n) for n in ('q', 'k', 'g', 'v', 'ck', 'cq')]
            q0 = Qc * 512
            kb_lo = 0 if kind == 'fox' else max(0, 4 * Qc - 16)
            j = kb - 4 * Qc
            qlo = max(0, j) * 128
            t = ProjCtx()

            def a():
                si = S3[C.state['srr'] % 3]
                C.state['srr'] += 1
                t.si = si
                P.op('pe', lambda e: e.matmul(C.ps[si][:, qlo:512], H.kT[:, kb * 128:(kb + 1) * 128], H.qT[:, q0 + qlo:q0 + 512], start=True, stop=True),
                     reads=[kq, kk], writes=[('ps', si)])

            def b():
                si = t.si
                pi = C.state['prr'] % 3
                C.state['prr'] += 1
                t.pi = pi
                if kind == 'fox':
                    ti = C.state['trr2'] % 2
                    C.state['trr2'] += 1
                    P.op('dve', lambda e: e.tensor_tensor(out=tt_[ti][:, qlo:512], in0=C.ps[si][:, qlo:512], in1=H.cqb[:, q0 + qlo:q0 + 512], op=ALU.add),
                         reads=[('ps', si), kcq], writes=[('tt', ti)])
                    if j >= 0:
                        P.op('dve', lambda e: e.tensor_tensor(out=tt_[ti][:, qlo:qlo + 128], in0=tt_[ti][:, qlo:qlo + 128], in1=trim[:], op=ALU.add),
                             reads=[('tt', ti), ('trim',)], writes=[('tt', ti)])
                    P.op('act', lambda e: e.activation(out=pT[pi][:, qlo:512], in_=tt_[ti][:, qlo:512], func=AF.Exp, bias=H.ck[:, kb:kb + 1]),
                         reads=[('tt', ti), kc_], writes=[('pT', pi)])
                else:
                    ei = C.state['trr2'] % 2
                    C.state['trr2'] += 1
                    oi_ = 4 * Qc - kb + 3
                    P.op('act', lambda e: e.activation(out=et[ei][:, qlo:512], in_=C.ps[si][:, qlo:512], func=AF.Exp),
                         reads=[('ps', si)], writes=[('et', ei)])
                    P.op('dve', lambda e: e.tensor_tensor(out=pT[pi][:, qlo:512], in0=et[ei][:, qlo:512], in1=Mt[:, oi_, qlo:512], op=ALU.mult),
                         reads=[('et', ei), ('Mt',)], writes=[('pT', pi)])

            def c():
                pi = t.pi
                for qb4 in range(max(0, j), 4):
                    last_kb = 4 * Qc + qb4
                    P.op('pe', lambda e, qb4=qb4, last_kb=last_kb: e.matmul(C.ps[2 + qb4][:, 0:130], pT[pi][:, qb4 * 128:(qb4 + 1) * 128], H.V[:, kb, 0:130], start=(kb == kb_lo), stop=(kb == last_kb)),
                         reads=[('pT', pi), kv], writes=[('O', qb4)])

            def post():
                if not last_of_chunk:
                    return
                for qb4 in range(4):
                    gate_ap = H.gT[:, q0 + qb4 * 128:q0 + (qb4 + 1) * 128] if kind == 'fox' else None
                    attn_out_block(nc, P, C, C.ps[2 + qb4], qb4, gate_ap, og[ogi], ('og', ogi), gate_key=kg)
                orow = (h if kind == 'fox' else NH + h) * 128
                P.dma('sp', oT[orow:orow + 128, q0:q0 + 512], og[ogi][:], reads=[('og', ogi)], writes=[('oT', hi, Qc)])

            t.pre = pre if pre is not None else (lambda: None)
            t.a, t.b, t.c, t.post = a, b, c, post
            return t

        tasks = []
        pres = [make_pre(hi, kind, h) for hi, (kind, h) in enumerate(heads)]
        for hi, (kind, h) in enumerate(heads):
            nt_head = 0
            for Qc in range(NQC0):
                kb_lo = 0 if kind == 'fox' else max(0, 4 * Qc - 16)
                kb_hi = 4 * Qc + 3
                ogi = C.state['ogrr'] % 2
                C.state['ogrr'] += 1
                for kb in range(kb_lo, kb_hi + 1):
                    pre = None
                    if hi == 0 and nt_head == 0:
                        pre = pres[0]
                    if nt_head == 2 and hi + 1 < len(heads):
                        pre = pres[hi + 1]
                    tasks.append(make_tile(hi, kind, h, Qc, kb, pre, kb == kb_hi, ogi))
                    nt_head += 1
        run_pipeline(tasks, depth=2)
        P.emit_stage()


def host_consts():
    inv = 1.0 / (500000.0 ** (np.arange(0, 32, 2, dtype=np.float32) / 32))
    ang = np.arange(S, dtype=np.float32)[None, :] * inv[:, None]
    cos = np.cos(ang).astype(np.float32); sin = np.sin(ang).astype(np.float32)
    ropeC = np.concatenate([cos, cos], 0)
    ropeS = np.concatenate([-sin, sin], 0)
    Pm = np.zeros((32, 32), np.float32)
    for m in range(32):
        Pm[(m + 16) % 32, m] = 1.0
    p = np.arange(128)[:, None]; c = np.arange(128)[None, :]
    trim = np.where(p > c, NEGM, 0.0).astype(np.float32)
    ident = np.eye(128, dtype=np.float32)
    Mt = np.zeros((128, 20, 512), np.float32)
    for oi in range(20):
        o = 128 * (oi - 3)
        d = o + np.arange(512)[None, :] - np.arange(128)[:, None]
        m = ((d >= 0) & (d <= 128)).astype(np.float32) + ((d >= 0) & (d <= 512) & (d % 4 == 0)) + ((d >= 0) & (d <= 2048) & (d % 16 == 0))
        Mt[:, oi, :] = m
    return dict(ropeC=ropeC, ropeS=ropeS, Pm=Pm, trim=trim, ident=ident, Mt=Mt.reshape(128, 20 * 512))


def host_inputs_mix0(inp, b, half):
    w = inp['even_w_in'][0]
    hs = slice(half * 512, (half + 1) * 512)
    qa = w[:, 0:1024][:, hs]; ka = w[:, 1024:2048][:, hs]; va = w[:, 2048:3072][:, hs]; ga = w[:, 3072:4096][:, hs]
    fa = w[:, 4096:4104][:, half * 4:(half + 1) * 4]
    qd = w[:, 4104:5128][:, hs]; kd = w[:, 5128:6152][:, hs]; vd = w[:, 6152:7176][:, hs]
    m = dict(
        xT=np.ascontiguousarray(inp['x'][b].T),
        g=np.ascontiguousarray(inp['ln_mix_g'][0].reshape(16, 128).T),
        Wfm=np.ascontiguousarray(np.concatenate([qa, ka, ga, qd, kd], 1)),
        Wtm=np.ascontiguousarray(np.concatenate([va, vd], 1)),
        Wf=np.ascontiguousarray(fa),
        bf=np.ascontiguousarray(inp['even_b_f'][0][half * 4:(half + 1) * 4].reshape(4, 1)),
        gains=np.ascontiguousarray(np.stack([inp['even_g_q_fox'][0], inp['even_g_k_fox'][0], inp['even_g_q_dil'][0], inp['even_g_k_dil'][0]], 1)),
    )
    m.update(host_consts())
    return m


NB = -30000.0
GC = 1.5957691216057308


def emit_mix1(nc, P, T_, NG=4):
    xT = T_['h2T']; g_in = T_['g1']; Wfm = T_['Wfm1']; Wtm = T_['Wtm1']; gains_in = T_['gains1']
    ropeC = T_['ropeC']; ropeS = T_['ropeS']; Pm_in = T_['Pm']; ropeCc = T_['ropeCc']; ropeSc = T_['ropeSc']
    ident_in = T_['ident']
    w1k = T_['w1k']; w2k = T_['w2k']; peTk = T_['peTk']; w1v = T_['w1v']; w2v = T_['w2v']; peTv = T_['peTv']
    ovl_in = T_['ovl']; cmpM_in = T_['cmpM']; FT_in = T_['FT']; CT_in = T_['CT']; ExpT_in = T_['ExpT']; WB_in = T_['WB']
    trib_in = T_['trib']; oT = T_['o1T']
    dbg = False
    dbgo = nc.dram_tensor("b_dbgo", [128, 2048], F32, kind="Internal").ap()
    qk_s = nc.dram_tensor("b_qk_s", [8 * NG, 128, S], BF16, kind="Internal").ap()
    v_s = nc.dram_tensor("b_v_s", [S, 2 * NG * 128 + 256], BF16, kind="Internal").ap()
    kc_s = nc.dram_tensor("b_kc_s", [NG, 128, 256], BF16, kind="Internal").ap()
    vc_s = nc.dram_tensor("b_vc_s", [NG, 256, 128], BF16, kind="Internal").ap()

    with ExitStack() as es:
        def sb(name, shape, dt):
            return es.enter_context(nc.sbuf_tensor("b1_" + name, shape, dt))
        C = ProjCtx()
        C.state = dict(wrr=0, psrr=0, qrr=0, sqrr=0, rsrr=0, vrr=0)
        C.ps = [es.enter_context(nc.psum_tensor(f"b1ps{i}", [128, 512], F32)) for i in range(8)]
        C.hnT = sb("hnT", [128, 16, S], BF16)
        C.ones = sb("ones", [128, 128], BF16)
        C.wbl = [sb(f"wbl{i}", [128, 16, 256], BF16) for i in range(2)]
        C.Ct = sb("Ct", [32, S], BF16)
        C.St = sb("St", [32, S], BF16)
        C.Pm = sb("Pmt", [32, 32], BF16)
        C.gains = sb("gains", [128, 4], F32)
        C.sq = [sb(f"sq{i}", [128, 512], BF16) for i in range(2)]
        C.rs = [sb(f"rs{i}", [128, 512], F32) for i in range(2)]
        C.qn = [sb(f"qn{i}", [128, 512], BF16) for i in range(3)]
        C.r1 = sb("r1", [32, 512], F32)
        C.r2 = sb("r2", [32, 512], F32)
        C.vst = [sb(f"vst{i}", [128, 4, 256], BF16) for i in range(2)]
        P.op('pool', lambda e: e.memset(C.ones[:], 1.0), writes=[('ones',)])
        P.dma('pool', C.Ct[:], ropeC[:, :], writes=[('tabs',)])
        P.dma('pool', C.St[:], ropeS[:, :], writes=[('tabs',)])
        P.dma('pool', C.Pm[:], Pm_in[:, :], writes=[('Pm',)])
        P.dma('sp', C.gains[:], gains_in[:, :], writes=[('gains',)])
        P.op('act', lambda e: e.mul(C.gains[:, 0:1], C.gains[:, 0:1], SCALE), reads=[('gains',)], writes=[('gains',)])
        norm_stage(nc, P, es, sb, C.ps, xT, g_in, C.hnT, C.ones)
        rp = dict(rope=True, tabs=(C.Ct, C.St))
        specs = ([dict(kind='norm', gain=0, **rp)] * (4 * NG) + [dict(kind='norm', gain=1, **rp)] * NG +
                 [dict(kind='norm', gain=2, **rp)] * NG + [dict(kind='raw')] * (2 * NG))
        proj_fm(nc, P, C, Wfm, 8 * NG * 128, specs, lambda j, tt: qk_s[j, :, tt * 512:(tt + 1) * 512])
        proj_tm(nc, P, C, Wtm, 2 * NG * 128 + 256, lambda blk, tb4: v_s[tb4 * 512:(tb4 + 1) * 512, blk * 256:(blk + 1) * 256].rearrange("(i p) c -> p i c", p=128))
        P.emit_stage()

    with ExitStack() as es:
        def sb(name, shape, dt):
            return es.enter_context(nc.sbuf_tensor("bc_" + name, shape, dt))
        ps = [es.enter_context(nc.psum_tensor(f"bcps{i}", [128, 512], F32)) for i in range(4)]
        ones = sb("ones", [128, 128], BF16)
        Pm = sb("Pm", [32, 32], BF16)
        Cc = sb("Cc", [32, 256], BF16); Sc = sb("Sc", [32, 256], BF16)
        gains = sb("gains", [128, 4], F32)
        P.op('pool', lambda e: e.memset(ones[:], 1.0), writes=[('ones',)])
        P.dma('pool', Pm[:], Pm_in[:, :], writes=[('Pm',)])
        P.dma('pool', Cc[:], ropeCc[:, :], writes=[('tabc',)])
        P.dma('pool', Sc[:], ropeSc[:, :], writes=[('tabc',)])
        P.dma('sp', gains[:], gains_in[:, :], writes=[('gains',)])
        w1s = [sb(f"w1s{i}", [128, 32, 128], BF16) for i in range(2)]
        w2s = [sb(f"w2s{i}", [128, 128], BF16) for i in range(2)]
        pes = [sb(f"pes{i}", [128, 32], BF16) for i in range(2)]
        for i, (w1, w2, pe) in enumerate([(w1k, w2k, peTk), (w1v, w2v, peTv)]):
            P.dma('pool', w1s[i][:], w1.rearrange("(l p) o -> p l o", p=128), writes=[('w1s', i)])
            P.dma('pool', w2s[i][:], w2[:, :], writes=[('w2s', i)])
            P.dma('pool', pes[i][:], pe[:, :], writes=[('pes', i)])
        xc = sb("xc", [128, S], BF16)
        bz = sb("bz", [128, 1], F32)
        zs = sb("zs", [128, 256], F32); z2 = sb("z2", [128, 256], F32); sg = sb("sg", [128, 256], F32)
        G = sb("G", [128, 256], BF16)
        sq = sb("sq", [128, 256], BF16); rs = sb("rs", [128, 256], F32); kn = sb("kn", [128, 256], BF16)
        r1 = sb("r1", [32, 256], F32); r2 = sb("r2", [32, 256], F32)
        vct = sb("vct", [128, 2, 128], BF16)
        P.op('pool', lambda e: e.memset(G[:], 0.0), writes=[('G',)])
        P.op('pool', lambda e: e.memset(kn[:], 0.0), writes=[('kn',)])
        for g in range(NG):
            for i in range(2):
                P.dma('sp', xc[:], qk_s[6 * NG + NG * i + g, :, :], writes=[('xc',)])
                xv = xc[:].rearrange("p (i r) -> p i r", r=16)
                for l in range(32):
                    a, r = (0, l) if l < 16 else (1, l - 16)
                    P.op('pe', lambda e, l=l, a=a, r=r, i=i: e.matmul(ps[0][:, 0:255], w1s[i][:, l, :], xv[:, a:a + 255, r], start=(l == 0), stop=(l == 31)),
                         reads=[('w1s', i), ('xc',)], writes=[('ps', 0)])
                for l in range(32):
                    P.op('pe', lambda e, l=l, i=i: e.matmul(ps[1][:, 0:1], w1s[i][:, l, :], pes[i][:, l:l + 1], start=(l == 0), stop=(l == 31)),
                         reads=[('w1s', i), ('pes', i)], writes=[('ps', 1)])
                P.op('dve', lambda e: e.tensor_copy(out=bz[:], in_=ps[1][:, 0:1]), reads=[('ps', 1)], writes=[('bz',)])
                P.op('dve', lambda e: e.tensor_scalar(out=zs[:, 0:255], in0=ps[0][:, 0:255], scalar1=bz[:, 0:1], scalar2=None, op0=ALU.add),
                     reads=[('ps', 0), ('bz',)], writes=[('zs',)], strict=True)
                P.op('dve', lambda e: e.tensor_tensor(out=z2[:, 0:255], in0=zs[:, 0:255], in1=zs[:, 0:255], op=ALU.mult), reads=[('zs',)], writes=[('z2',)])
                P.op('dve', lambda e: e.tensor_scalar(out=z2[:, 0:255], in0=z2[:, 0:255], scalar1=0.044715, scalar2=1.0, op0=ALU.mult, op1=ALU.add), reads=[('z2',)], writes=[('z2',)])
                P.op('dve', lambda e: e.tensor_tensor(out=z2[:, 0:255], in0=z2[:, 0:255], in1=zs[:, 0:255], op=ALU.mult), reads=[('z2',), ('zs',)], writes=[('z2',)])
                P.op('act', lambda e: e.activation(out=sg[:, 0:255], in_=z2[:, 0:255], func=AF.Sigmoid, scale=GC), reads=[('z2',)], writes=[('sg',)])
                P.op('dve', lambda e: e.tensor_tensor(out=G[:, 0:255], in0=zs[:, 0:255], in1=sg[:, 0:255], op=ALU.mult), reads=[('zs',), ('sg',)], writes=[('G',)])
                if i == 0:
                    P.op('pe', lambda e: e.matmul(ps[2][:, 0:256], w2s[0][:], G[:], start=True, stop=True), reads=[('w2s', 0), ('G',)], writes=[('ps', 2)])
                    P.op('act', lambda e: e.activation(out=sq[:], in_=ps[2][:, 0:256], func=AF.Square), reads=[('ps', 2)], writes=[('sq',)])
                    P.op('pe', lambda e: e.matmul(ps[3][:, 0:256], ones[:], sq[:], start=True, stop=True), reads=[('sq',), ('ones',)], writes=[('ps', 3)])
                    P.op('act', lambda e: e.activation(out=rs[:], in_=ps[3][:, 0:256], func=AF.Sqrt, scale=1.0 / 128, bias=EPS), reads=[('ps', 3)], writes=[('rs',)])
                    P.op('dve', lambda e: e.reciprocal(out=rs[:], in_=rs[:]), reads=[('rs',)], writes=[('rs',)])
                    P.op('dve', lambda e: e.scalar_tensor_tensor(out=kn[:], in0=ps[2][:, 0:256], scalar=gains[:, 3:4], in1=rs[:], op0=ALU.mult, op1=ALU.mult),
                         reads=[('ps', 2), ('rs',), ('gains',)], writes=[('kn',)])
                    P.op('pe', lambda e: e.matmul(ps[3][0:32, 0:256], Pm[:], kn[0:32, :], start=True, stop=True), reads=[('kn',), ('Pm',)], writes=[('ps', 3)])
                    P.op('dve', lambda e: e.tensor_tensor(out=r1[:], in0=ps[3][0:32, 0:256], in1=Sc[:], op=ALU.mult), reads=[('ps', 3), ('tabc',)], writes=[('r1',)])
                    P.op('dve', lambda e: e.tensor_tensor(out=r2[:], in0=kn[0:32, :], in1=Cc[:], op=ALU.mult), reads=[('kn',), ('tabc',)], writes=[('r2',)])
                    P.op('dve', lambda e: e.tensor_tensor(out=kn[0:32, :], in0=r1[:], in1=r2[:], op=ALU.add), reads=[('r1',), ('r2',)], writes=[('kn',)])
                    P.dma('sp', kc_s[g, :, :], kn[:], reads=[('kn',)], writes=[('kc_s', g)])
                else:
                    for nb in range(2):
                        P.op('pe', lambda e, nb=nb: e.matmul(ps[2][:, nb * 128:(nb + 1) * 128], G[:, nb * 128:(nb + 1) * 128], w2s[1][:], start=True, stop=True),
                             reads=[('w2s', 1), ('G',)], writes=[('ps', 2)])
                    P.op('act', lambda e: e.activation(out=vct[:].rearrange("p a b -> p (a b)"), in_=ps[2][:, 0:256], func=AF.Copy), reads=[('ps', 2)], writes=[('vct',)])
                    P.dma('sp', vc_s[g].rearrange("(nb p) d -> p nb d", p=128), vct[:], reads=[('vct',)], writes=[('vc_s', g)])
        P.emit_stage()

    with ExitStack() as es:
        def sb(name, shape, dt):
            return es.enter_context(nc.sbuf_tensor("b2_" + name, shape, dt))
        st = dict(srr=0, prr=0, err=0, onrr=0, trr=0, ogrr=0, scl=0)
        ps = [es.enter_context(nc.psum_tensor(f"b2ps{i}", [128, 512], F32)) for i in range(7)]
        pst = es.enter_context(nc.psum_tensor("b2pst", [128, 1024], BF16))
        ident = sb("ident", [128, 128], BF16)
        trib = sb("trib", [128, 128], BF16)
        cmpM = sb("cmpM", [128, 2, S], BF16)
        ExpT = sb("ExpT", [64, 32, 128], BF16)
        WB = sb("WB", [128, 8, 512], BF16)
        P.dma('pool', ident[:], ident_in[:, :], writes=[('ident',)])
        P.dma('pool', trib[:], trib_in[:, :], writes=[('trib',)])
        for nb in range(2):
            for c in range(8):
                P.dma('pool', cmpM[:, nb, c * 512:(c + 1) * 512], cmpM_in[nb * 128:(nb + 1) * 128, c * 512:(c + 1) * 512], writes=[('cmpM', nb, c)])
        P.dma('pool', ExpT[:], ExpT_in.rearrange("j (k p) -> j k p", k=32), writes=[('ExpT',)])
        P.dma('pool', WB[:], WB_in.rearrange("p (o c) -> p o c", o=8), writes=[('WB',)])
        ksT = sb("ksT", [128, S], BF16); kwT = sb("kwT", [128, S], BF16)
        VS = sb("VS", [128, 32, 136], BF16); VW = sb("VW", [128, 32, 136], BF16)
        kcT = sb("kcT", [128, 256], BF16)
        VC = sb("VC", [128, 2, 200], BF16)
        gsg = sb("gsg", [128, 32, 12], F32)
        gtmp = sb("gtmp", [128, 32, 12], BF16)
        qT = [sb(f"qT{i}", [128, S], BF16) for i in range(4)]
        oacc = [sb(f"oacc{i}", [128, 4, 128], F32) for i in range(4)]
        impa = sb("impa", [128, 4, 64], F32)
        FTt = sb("FTt", [128, 4, 64], F32); CTt = sb("CTt", [128, 4, 64], F32)
        m8 = sb("m8", [128, 8], F32); m8b = sb("m8b", [128, 8], F32)
        wk = sb("wk", [128, 64], F32); s1 = sb("s1", [128, 64], F32); s2 = sb("s2", [128, 64], F32)
        selb = sb("selb", [128, 64], BF16)
        selT = sb("selT", [64, 512], BF16)
        et = [sb(f"et{i}", [128, 512], BF16) for i in range(2)]
        pT = [sb(f"pT{i}", [128, 512], BF16) for i in range(3)]
        og = [sb(f"og{i}", [128, 512], F32) for i in range(2)]
        on = [sb(f"on{i}", [128, 128], BF16) for i in range(2)]
        rden = [sb(f"rden{i}", [128, 1], F32) for i in range(4)]
        scl = [sb(f"scl{i}", [128, 1], F32) for i in range(4)]
        P.op('pool', lambda e: e.memset(VS[:, :, 128:136], 1.0), writes=[('VS1',)])
        P.op('pool', lambda e: e.memset(VW[:, :, 128:136], 1.0), writes=[('VW1',)])
        P.op('pool', lambda e: e.memset(VC[:, :, 128:136], 1.0), writes=[('VC1',)])
        for nb in range(2):
            P.dma('pool', VC[:, nb, 136:200], ovl_in[nb * 128:(nb + 1) * 128, :], writes=[('VCo', nb)])
        NQC = int(os.environ.get('NQC', '8'))
        dbt = sb('dbt', [128, 2048], F32)
        P.op('pool', lambda e: e.memset(dbt[:], 0.0), writes=[('dbt',)])
        dstate = {'done': os.environ.get('DBGD', '0') != '1'}
        P.same = os.environ.get('SAME', '0') == '1'

        def finish_branch(hh, Qc, br, first, ncol_den=128, imp=False, imp_first=False):
            SK = os.environ.get("SKIP", "")
            if "F" in SK:
                return
            if "I" in SK:
                imp = False
            for qb4 in range(4):
                Ops = ps[2 + qb4]
                blk = Qc * 4 + qb4
                ri = st['scl'] % 4
                st['scl'] += 1
                P.op('dve', lambda e, Ops=Ops, ri=ri: e.tensor_scalar(out=rden[ri][:], in0=Ops[:, 128:129], scalar1=1e-30, scalar2=None, op0=ALU.max),
                     reads=[('O', qb4)], writes=[('rden', ri)])
                P.op('dve', lambda e, ri=ri: e.reciprocal(out=rden[ri][:], in_=rden[ri][:]), reads=[('rden', ri)], writes=[('rden', ri)], strict=True)
                P.op('dve', lambda e, ri=ri, blk=blk, hh=hh, br=br: e.tensor_tensor(out=scl[ri][:], in0=rden[ri][:], in1=gsg[:, blk, hh * 3 + br:hh * 3 + br + 1], op=ALU.mult),
                     reads=[('rden', ri), ('gsg',)], writes=[('scl', ri)], strict=True)
                if imp:
                    if imp_first:
                        P.op('dve', lambda e, Ops=Ops, ri=ri, qb4=qb4: e.tensor_scalar(out=impa[:, qb4, :], in0=Ops[:, 136:200], scalar1=rden[ri][:, 0:1], scalar2=None, op0=ALU.mult),
                             reads=[('O', qb4), ('rden', ri)], writes=[('impa', qb4)], strict=True)
                    else:
                        P.op('dve', lambda e, Ops=Ops, ri=ri, qb4=qb4: e.scalar_tensor_tensor(out=impa[:, qb4, :], in0=Ops[:, 136:200], scalar=rden[ri][:, 0:1], in1=impa[:, qb4, :], op0=ALU.mult, op1=ALU.add),
                             reads=[('O', qb4), ('rden', ri), ('impa', qb4)], writes=[('impa', qb4)], strict=True)
                if first:
                    P.op('dve', lambda e, Ops=Ops, ri=ri, qb4=qb4, hh=hh: e.tensor_scalar(out=oacc[hh][:, qb4, :], in0=Ops[:, 0:128], scalar1=scl[ri][:, 0:1], scalar2=None, op0=ALU.mult),
                         reads=[('O', qb4), ('scl', ri)], writes=[('oacc', hh, qb4)], strict=True)
                else:
                    P.op('dve', lambda e, Ops=Ops, ri=ri, qb4=qb4, hh=hh: e.scalar_tensor_tensor(out=oacc[hh][:, qb4, :], in0=Ops[:, 0:128], scalar=scl[ri][:, 0:1], in1=oacc[hh][:, qb4, :], op0=ALU.mult, op1=ALU.add),
                         reads=[('O', qb4), ('scl', ri), ('oacc', hh, qb4)], writes=[('oacc', hh, qb4)], strict=True)

        S3 = [0, 1, 6]

        def group_loads(g):
            P.dma('sp', ksT[:], qk_s[4 * NG + g, :, :], writes=[('ksT',)])
            P.dma('sp', kwT[:], qk_s[5 * NG + g, :, :], writes=[('kwT',)])
            P.dma('sp', VS[:, :, 0:128], v_s[:, g * 128:(g + 1) * 128].rearrange("(blk p) c -> p blk c", p=128), reads=[('VS1',)], writes=[('VS',)])
            P.dma('sp', VW[:, :, 0:128], v_s[:, NG * 128 + g * 128:NG * 128 + (g + 1) * 128].rearrange("(blk p) c -> p blk c", p=128), reads=[('VW1',)], writes=[('VW',)])
            P.dma('sp', kcT[:], kc_s[g, :, :], writes=[('kcT',)])
            P.dma('sp', VC[:, :, 0:128], vc_s[g].rearrange("(nb p) d -> p nb d", p=128), reads=[('VC1',)], writes=[('VC',)])
            P.dma('sp', gtmp[:], v_s[:, 2 * NG * 128 + g * 12:2 * NG * 128 + (g + 1) * 12].rearrange("(blk p) c -> p blk c", p=128), writes=[('gtmp',)])
            P.op('act', lambda e: e.activation(out=gsg[:], in_=gtmp[:], func=AF.Sigmoid), reads=[('gtmp',)], writes=[('gsg',)])
            for hh in range(4):
                P.dma('sp', qT[hh][:], qk_s[g * 4 + hh, :, :], writes=[('qT', hh)])

        def chunk_tables(Qc):
            q0 = Qc * 512
            P.dma('sp', FTt[:], FT_in[q0:q0 + 512, :].rearrange("(b p) j -> p b j", p=128), writes=[('FTt',)])
            P.dma('sp', CTt[:], CT_in[q0:q0 + 512, :].rearrange("(b p) j -> p b j", p=128), writes=[('CTt',)])

        def make_cmp_tile(g, Qc, hh, nb, nbs, pre):
            q0 = Qc * 512
            t = ProjCtx()

            def a():
                si = S3[st['srr'] % 3]; st['srr'] += 1
                t.si = si
                P.op('pe', lambda e: e.matmul(ps[si][:], kcT[:, nb * 128:(nb + 1) * 128], qT[hh][:, q0:q0 + 512], start=True, stop=True),
                     reads=[('kcT',), ('qT', hh)], writes=[('ps', si)])

            def b():
                si = t.si
                ei = st['err'] % 2; st['err'] += 1
                pi = st['prr'] % 3; st['prr'] += 1
                t.pi = pi
                P.op('act', lambda e: e.activation(out=et[ei][:], in_=ps[si][:], func=AF.Exp), reads=[('ps', si)], writes=[('et', ei)])
                P.op('dve', lambda e: e.tensor_tensor(out=pT[pi][:], in0=et[ei][:], in1=cmpM[:, nb, q0:q0 + 512], op=ALU.mult),
                     reads=[('et', ei), ('cmpM', nb, Qc)], writes=[('pT', pi)])

            def c():
                pi = t.pi
                for qb4 in range(4):
                    P.op('pe', lambda e, qb4=qb4: e.matmul(ps[2 + qb4][:, 0:200], pT[pi][:, qb4 * 128:(qb4 + 1) * 128], VC[:, nb, :], start=(nb == 0), stop=(nb == nbs[-1])),
                         reads=[('pT', pi), ('VC',), ('VCo', 0), ('VCo', 1), ('VC1',)], writes=[('O', qb4)])

            def post():
                if nb == nbs[-1]:
                    finish_branch(hh, Qc, 0, True, imp=True, imp_first=(hh == 0))

            t.pre = pre if pre is not None else (lambda: None)
            t.a, t.b, t.c, t.post = a, b, c, post
            return t

        def cmp_tiles(g, Qc):
            out = []
            pre = (lambda: chunk_tables(Qc))
            for hh in range(4):
                nbs = [0] + ([1] if Qc >= 4 else [])
                for nb in nbs:
                    out.append(make_cmp_tile(g, Qc, hh, nb, nbs, pre))
                    pre = None
            return out

        def do_B(Qc):
            for qb4 in range(4):
                P.op('dve', lambda e, qb4=qb4: e.tensor_tensor(out=wk[:], in0=impa[:, qb4, :], in1=FTt[:, qb4, :], op=ALU.max), reads=[('impa', qb4), ('FTt',)], writes=[('wk',)])
                P.op('dve', lambda e, qb4=qb4: e.tensor_tensor(out=wk[:], in0=wk[:], in1=CTt[:, qb4, :], op=ALU.min), reads=[('wk',), ('CTt',)], writes=[('wk',)])
                P.op('dve', lambda e: e.max(out=m8[:], in_=wk[:]), reads=[('wk',)], writes=[('m8',)], strict=True)
                P.op('dve', lambda e: e.match_replace(out=s1[:], in_to_replace=m8[:], in_values=wk[:], imm_value=-3.0e38), reads=[('wk',), ('m8',)], writes=[('s1',)], strict=True)
                P.op('dve', lambda e: e.max(out=m8b[:], in_=s1[:]), reads=[('s1',)], writes=[('m8b',)], strict=True)
                P.op('dve', lambda e: e.tensor_scalar(out=s1[:], in0=wk[:], scalar1=m8b[:, 7:8], scalar2=None, op0=ALU.is_ge), reads=[('wk',), ('m8b',)], writes=[('s1',)], strict=True)
                P.op('dve', lambda e: e.tensor_scalar(out=s2[:], in0=wk[:], scalar1=-5.0e29, scalar2=None, op0=ALU.is_gt), reads=[('wk',)], writes=[('s2',)])
                P.op('dve', lambda e: e.tensor_tensor(out=s1[:], in0=s1[:], in1=s2[:], op=ALU.mult), reads=[('s1',), ('s2',)], writes=[('s1',)])
                P.op('dve', lambda e: e.tensor_scalar(out=selb[:], in0=s1[:], scalar1=-NB, scalar2=NB, op0=ALU.mult, op1=ALU.add), reads=[('s1',)], writes=[('selb',)])
                ti = st['trr'] % 4; st['trr'] += 1
                P.op('pe', lambda e, ti=ti: e.transpose(pst[0:64, ti * 128:(ti + 1) * 128], selb[:], ident[:]), reads=[('selb',), ('ident',)], writes=[('pst', ti)])
                P.op('dve', lambda e, ti=ti, qb4=qb4: e.tensor_copy(out=selT[:, qb4 * 128:(qb4 + 1) * 128], in_=pst[0:64, ti * 128:(ti + 1) * 128]), reads=[('pst', ti)], writes=[('selT',)])

        def make_sw_tile(g, Qc, hh, br, kb, kb_lo, kb_hi):
            q0 = Qc * 512
            KT, VV, kkey, vkey, v1 = (ksT, VS, ('ksT',), ('VS',), ('VS1',)) if br == 1 else (kwT, VW, ('kwT',), ('VW',), ('VW1',))
            j = kb - 4 * Qc
            qlo = max(0, j) * 128
            t = ProjCtx()

            def a():
                si = S3[st['srr'] % 3]; st['srr'] += 1
                t.si = si
                P.op('pe', lambda e: e.matmul(ps[si][:, qlo:512], KT[:, kb * 128:(kb + 1) * 128], qT[hh][:, q0 + qlo:q0 + 512], start=True, stop=False),
                     reads=[kkey, ('qT', hh)], writes=[('ps', si)])
                if br == 1:
                    P.op('pe', lambda e: e.matmul(ps[si][:, qlo:512], ExpT[:, kb, :], selT[:, qlo:512], start=False, stop=(j < 0)),
                         reads=[('ExpT',), ('selT',)], writes=[('ps', si)])
                    if j >= 0:
                        P.op('pe', lambda e: e.matmul(ps[si][:, qlo:qlo + 128], ident[:], trib[:], start=False, stop=True),
                             reads=[('ident',), ('trib',)], writes=[('ps', si)])
                else:
                    oi_ = 4 * Qc - kb + 3
                    P.op('pe', lambda e: e.matmul(ps[si][:, qlo:512], ident[:], WB[:, oi_, qlo:512], start=False, stop=True),
                         reads=[('ident',), ('WB',)], writes=[('ps', si)])

            def b():
                si = t.si
                pi = st['prr'] % 3; st['prr'] += 1
                t.pi = pi
                P.op('act', lambda e: e.activation(out=pT[pi][:, qlo:512], in_=ps[si][:, qlo:512], func=AF.Exp), reads=[('ps', si)], writes=[('pT', pi)])

            def c():
                pi = t.pi
                for qb4 in range(max(0, j), 4):
                    last_kb = 4 * Qc + qb4
                    P.op('pe', lambda e, qb4=qb4, last_kb=last_kb: e.matmul(ps[2 + qb4][:, 0:130], pT[pi][:, qb4 * 128:(qb4 + 1) * 128], VV[:, kb, 0:130], start=(kb == kb_lo), stop=(kb == last_kb)),
                         reads=[('pT', pi), vkey, v1], writes=[('O', qb4)])

            def post():
                if kb != kb_hi:
                    return
                finish_branch(hh, Qc, br, False)
                if br != 2:
                    return
                ogi = st['ogrr'] % 2; st['ogrr'] += 1
                for qb4 in range(4):
                    oi = st['onrr'] % 2; st['onrr'] += 1
                    P.op('dve', lambda e, oi=oi, qb4=qb4: e.tensor_copy(out=on[oi][:], in_=oacc[hh][:, qb4, :]), reads=[('oacc', hh, qb4)], writes=[('on', oi)])
                    ti = st['trr'] % 4; st['trr'] += 1
                    P.op('pe', lambda e, oi=oi, ti=ti: e.transpose(pst[:, ti * 128:(ti + 1) * 128], on[oi][:], ident[:]), reads=[('on', oi), ('ident',)], writes=[('pst', ti)])
                    P.op('dve', lambda e, ti=ti, qb4=qb4: e.tensor_copy(out=og[ogi][:, qb4 * 128:(qb4 + 1) * 128], in_=pst[:, ti * 128:(ti + 1) * 128]), reads=[('pst', ti)], writes=[('og', ogi)])
                orow = (g * 4 + hh) * 128
                P.dma('sp', oT[orow:orow + 128, q0:q0 + 512], og[ogi][:], reads=[('og', ogi)], writes=[('oT', g, hh, Qc)])

            t.pre = (lambda: None)
            t.a, t.b, t.c, t.post = a, b, c, post
            return t

        def sw_tiles(g, Qc):
            out = []
            for hh in range(4):
                for br in (1, 2):
                    kb_lo = 0 if br == 1 else max(0, 4 * Qc - 4)
                    kb_hi = 4 * Qc + 3
                    for kb in range(kb_lo, kb_hi + 1):
                        out.append(make_sw_tile(g, Qc, hh, br, kb, kb_lo, kb_hi))
            return out

        for g in range(NG):
            group_loads(g)
            run_pipeline(cmp_tiles(g, 0), depth=2)
            for Qc in range(NQC):
                do_B(Qc)
                tasks = sw_tiles(g, Qc) + (cmp_tiles(g, Qc + 1) if Qc + 1 < NQC else [])
                run_pipeline(tasks, depth=2)
        P.emit_stage()


def host_consts1():
    c0 = host_consts()
    out = dict(ropeC=c0['ropeC'], ropeS=c0['ropeS'], Pm=c0['Pm'], ident=c0['ident'])
    inv = 1.0 / (500000.0 ** (np.arange(0, 32, 2, dtype=np.float32) / 32))
    posc = (np.arange(256) * 16 + 31).astype(np.float32)
    ang = posc[None, :] * inv[:, None]
    cos = np.cos(ang).astype(np.float32); sin = np.sin(ang).astype(np.float32)
    out['ropeCc'] = np.concatenate([cos, cos], 0); out['ropeSc'] = np.concatenate([-sin, sin], 0)
    n = np.arange(256)
    start = n * 16
    js = np.arange(64) * 64
    ov = ((start[:, None] < js[None, :] + 64) & (start[:, None] + 32 > js[None, :])).astype(np.float32)
    ov[255] = 0
    out['ovl'] = ov
    q = np.arange(S)
    cm = ((16 * n + 31)[:, None] <= q[None, :]).astype(np.float32)
    cm[255] = 0
    out['cmpM'] = cm
    cur = (q // 64)[:, None]
    jj = np.arange(64)[None, :]
    forced = (jj == 0) | (jj == cur) | (jj == cur - 1)
    out['FT'] = np.where(forced, 1e9, 0.0).astype(np.float32)
    out['CT'] = np.where(jj <= cur, 3.0e38, -1e30).astype(np.float32)
    E = np.zeros((64, 32, 128), np.float32)
    for kb in range(32):
        for p in range(128):
            E[2 * kb + p // 64, kb, p] = 1.0
    out['ExpT'] = E.reshape(64, 32 * 128)
    WBt = np.zeros((128, 8, 512), np.float32)
    for oi in range(8):
        o = 128 * (oi - 3)
        d = o + np.arange(512)[None, :] - np.arange(128)[:, None]
        WBt[:, oi, :] = np.where((d >= 0) & (d < 512), 0.0, NB)
    out['WB'] = WBt.reshape(128, 8 * 512)
    p = np.arange(128)[:, None]; c = np.arange(128)[None, :]
    out['trib'] = np.where(p > c, NB, 0.0).astype(np.float32)
    return out


def host_inputs_mix1(inp, xT_b, half):
    w = inp['odd_w_in'][0]
    q = w[:, 0:2048][:, half * 1024:(half + 1) * 1024]
    def grp(i):
        blk = w[:, 2048 + i * 512:2048 + (i + 1) * 512]
        return blk[:, half * 256:(half + 1) * 256]
    kc, vc, ks, vs, kw, vw = [grp(i) for i in range(6)]
    gt = w[:, 2048 + 3072:2048 + 3072 + 48][:, half * 24:(half + 1) * 24]
    gpad = np.zeros((D, 256), np.float32); gpad[:, :24] = gt
    m = dict(
        xT=xT_b,
        g=np.ascontiguousarray(inp['ln_mix_g'][1].reshape(16, 128).T),
        Wfm=np.ascontiguousarray(np.concatenate([q, ks, kw, kc, vc], 1)),
        Wtm=np.ascontiguousarray(np.concatenate([vs, vw, gpad], 1)),
        gains=np.ascontiguousarray(np.stack([inp['odd_g_q'][0], inp['odd_g_ks'][0], inp['odd_g_kw'][0], inp['odd_g_kc'][0]], 1)),
        w1k=inp['odd_phi_k_w1'][0], w2k=inp['odd_phi_k_w2'][0], peTk=np.ascontiguousarray(inp['odd_phi_k_pe'][0].T),
        w1v=inp['odd_phi_v_w1'][0], w2v=inp['odd_phi_v_w2'][0], peTv=np.ascontiguousarray(inp['odd_phi_v_pe'][0].T),
    )
    m.update(host_consts1())
    return m


F = 8192


def emit_phaseC(nc, P, tag, xT, oT, w_out, g_in, w_up, w_down, hT, T, TT=512):
    NT = T // TT
    with ExitStack() as es:
        def sb(name, shape, dt):
            return es.enter_context(nc.sbuf_tensor(tag + name, shape, dt))
        wb = [sb(f"wb{i}", [128, 8192], BF16) for i in range(3)]
        ot = sb("ot", [128, 16, TT], BF16)
        ht = sb("ht", [128, 16, TT], F32)
        hn = sb("hn", [128, 16, TT], BF16)
        ut = sb("ut", [128, 32, TT], BF16)
        sq = [sb(f"sq{i}", [128, TT], BF16) for i in range(2)]
        rt = [sb(f"rt{i}", [128, TT], F32) for i in range(2)]
        rstd = sb("rstd", [128, TT], F32)
        ones = sb("ones", [128, 128], BF16)
        gt = sb("gt", [128, 16], F32)
        ps = [es.enter_context(nc.psum_tensor(tag + f"ps{i}", [128, 512], F32)) for i in range(8)]

        P.op('pool', lambda e: e.memset(ones[:], 1.0), writes=[('ones',)])
        P.dma('sp', gt[:], g_in[:, :], writes=[('g',)])

        jobs = []
        state = {'psrr': 0, 'sqrr': 0, 'rtrr': 0}

        def nextps():
            i = state['psrr'] % 7
            state['psrr'] += 1
            return i

        for t in range(NT):
            tsl = slice(t * TT, (t + 1) * TT)

            def tile_begin(t=t, tsl=tsl):
                P.dma('pool', ot[:], oT[:, tsl].rearrange("(kc p) n -> p kc n", p=128),
                      writes=[('ot',)])
                P.dma('sp', ht[:], xT[:, tsl].rearrange("(kc p) n -> p kc n", p=128),
                      writes=[('ht', dc) for dc in range(16)])

            for blk in range(4):
                def load(bi, blk=blk):
                    v = wb[bi][:].rearrange("p (kc c) -> p kc c", kc=16)
                    P.dma('pool', v, w_out[:, blk * 512:(blk + 1) * 512].rearrange("(kc p) c -> p kc c", p=128),
                          writes=[('wb', bi)])

                def comp(bi, blk=blk, t=t, first=(blk == 0), tb=tile_begin):
                    if first:
                        tb()
                    v = wb[bi][:].rearrange("p (kc c) -> p kc c", kc=16)
                    for dcl in range(4):
                        dc = blk * 4 + dcl
                        pi = nextps()
                        for kc in range(16):
                            P.op('pe', lambda e, kc=kc, pi=pi, dcl=dcl: e.matmul(ps[pi][:], v[:, kc, dcl * 128:(dcl + 1) * 128], ot[:, kc, :], start=(kc == 0), stop=(kc == 15)),
                                 reads=[('wb', bi), ('ot',)], writes=[('ps', pi)])
                        P.op('dve', lambda e, pi=pi, dc=dc: e.tensor_tensor(out=ht[:, dc, :], in0=ps[pi][:], in1=ht[:, dc, :], op=ALU.add),
                             reads=[('ps', pi), ('ht', dc)], writes=[('ht', dc)])
                jobs.append((load, comp))

            def norm():
                pn = 7
                for dc in range(16):
                    si = state['sqrr'] % 2
                    state['sqrr'] += 1
                    P.op('act', lambda e, dc=dc, si=si: e.activation(out=sq[si][:], in_=ht[:, dc, :], func=AF.Square),
                         reads=[('ht', dc)], writes=[('sq', si)])
                    P.op('pe', lambda e, dc=dc, si=si: e.matmul(ps[pn][:], ones[:], sq[si][:], start=(dc == 0), stop=(dc == 15)),
                         reads=[('sq', si), ('ones',)], writes=[('ps', pn)])
                P.op('act', lambda e: e.activation(out=rstd[:], in_=ps[pn][:], func=AF.Sqrt, scale=1.0 / D, bias=EPS),
                     reads=[('ps', pn)], writes=[('rstd',)])
                P.op('dve', lambda e: e.reciprocal(out=rstd[:], in_=rstd[:]), reads=[('rstd',)], writes=[('rstd',)])
                for dc in range(16):
                    P.op('dve', lambda e, dc=dc: e.scalar_tensor_tensor(out=hn[:, dc, :], in0=ht[:, dc, :], scalar=gt[:, dc:dc + 1], in1=rstd[:], op0=ALU.mult, op1=ALU.mult),
                         reads=[('ht', dc), ('g',), ('rstd',)], writes=[('hn', dc)])

            for hf in range(2):
                for blk in range(8):
                    c0 = hf * 4096 + blk * 512

                    def load(bi, c0=c0):
                        v = wb[bi][:].rearrange("p (kc c) -> p kc c", kc=16)
                        P.dma('pool', v, w_up[:, c0:c0 + 512].rearrange("(kc p) c -> p kc c", p=128), writes=[('wb', bi)])

                    def comp(bi, blk=blk, hf=hf, donorm=(hf == 0 and blk == 0), nf=norm):
                        if donorm:
                            nf()
                        v = wb[bi][:].rearrange("p (kc c) -> p kc c", kc=16)
                        for fl in range(4):
                            fc = blk * 4 + fl
                            pi = nextps()
                            for kc in range(16):
                                P.op('pe', lambda e, kc=kc, pi=pi, fl=fl: e.matmul(ps[pi][:], v[:, kc, fl * 128:(fl + 1) * 128], hn[:, kc, :], start=(kc == 0), stop=(kc == 15)),
                                     reads=[('wb', bi), ('hn', kc)], writes=[('ps', pi)])
                            ri = state['rtrr'] % 2
                            state['rtrr'] += 1
                            P.op('act', lambda e, pi=pi, ri=ri: e.activation(out=rt[ri][:], in_=ps[pi][:], func=AF.Relu),
                                 reads=[('ps', pi)], writes=[('rt', ri)])
                            P.op('dve', lambda e, ri=ri, fc=fc: e.tensor_tensor(out=ut[:, fc, :], in0=rt[ri][:], in1=rt[ri][:], op=ALU.mult),
                                 reads=[('rt', ri)], writes=[('ut', fc)])
                    jobs.append((load, comp))
                for blk in range(8):
                    r0 = hf * 4096

                    def load(bi, blk=blk, r0=r0):
                        v = wb[bi][:].rearrange("p (fc c) -> p fc c", fc=32)
                        P.dma('pool', v, w_down[r0:r0 + 4096, blk * 256:(blk + 1) * 256].rearrange("(fc p) c -> p fc c", p=128), writes=[('wb', bi)])

                    def comp(bi, blk=blk, hf=hf, t=t, tsl=tsl):
                        v = wb[bi][:].rearrange("p (fc c) -> p fc c", fc=32)
                        for dcl in range(2):
                            dc = blk * 2 + dcl
                            pi = nextps()
                            for fc in range(32):
                                P.op('pe', lambda e, fc=fc, pi=pi, dcl=dcl: e.matmul(ps[pi][:], v[:, fc, dcl * 128:(dcl + 1) * 128], ut[:, fc, :], start=(fc == 0), stop=(fc == 31)),
                                     reads=[('wb', bi), ('ut', fc)], writes=[('ps', pi)])
                            P.op('dve', lambda e, pi=pi, dc=dc: e.tensor_tensor(out=ht[:, dc, :], in0=ps[pi][:], in1=ht[:, dc, :], op=ALU.add),
                                 reads=[('ps', pi), ('ht', dc)], writes=[('ht', dc)])
                        if hf == 1 and blk == 7:
                            P.dma('sp', hT[:, tsl].rearrange("(kc p) n -> p kc n", p=128), ht[:],
                                  reads=[('ht', dc) for dc in range(16)], writes=[('hTout', t)])
                    jobs.append((load, comp))

        nj = len(jobs)
        for i in range(min(2, nj)):
            jobs[i][0](i % 3)
        for i in range(nj):
            if i + 2 < nj:
                jobs[i + 2][0]((i + 2) % 3)
            jobs[i][1](i % 3)
        P.emit_stage()


IN_SPECS = [
    ("xT", [D, S]), ("g0", [128, 16]), ("Wfm0", [D, 5120]), ("Wtm0", [D, 2048]), ("Wf0", [D, 8]), ("bf0", [8, 1]),
    ("gains0", [128, 4]), ("ropeC", [32, S]), ("ropeS", [32, S]), ("Pm", [32, 32]), ("trim", [128, 128]),
    ("ident", [128, 128]), ("Mt", [128, 20 * 512]),
    ("w_out0", [D, D]), ("gm0", [128, 16]), ("w_up0", [D, F]), ("w_down0", [F, D]),
    ("g1", [128, 16]), ("Wfm1", [D, 4096]), ("Wtm1", [D, 1280]), ("gains1", [128, 4]),
    ("ropeCc", [32, 256]), ("ropeSc", [32, 256]),
    ("w1k", [4096, 128]), ("w2k", [128, 128]), ("peTk", [128, 32]), ("w1v", [4096, 128]), ("w2v", [128, 128]), ("peTv", [128, 32]),
    ("ovl", [256, 64]), ("cmpM", [256, S]), ("FT", [S, 64]), ("CT", [S, 64]), ("ExpT", [64, 32 * 128]), ("WB", [128, 8 * 512]),
    ("trib", [128, 128]),
    ("w_out1", [D, D]), ("gm1", [128, 16]), ("w_up1", [D, F]), ("w_down1", [F, D]),
]


def build_fused():
    nc = bass.Bass("TRN2", target_bir_lowering=False)
    T_ = {}
    for name, shape in IN_SPECS:
        T_[name] = nc.dram_tensor(name, shape, F32, kind="ExternalInput").ap()
    T_['o0T'] = nc.dram_tensor("o0T", [D, S], F32, kind="Internal").ap()
    T_['h2T'] = nc.dram_tensor("h2T", [D, S], F32, kind="Internal").ap()
    T_['o1T'] = nc.dram_tensor("o1T", [D, S], F32, kind="Internal").ap()
    hT = nc.dram_tensor("hT", [D, S], F32, kind="ExternalOutput").ap()
    P = Prog(nc)
    emit_mix0(nc, P, T_, NH=8)
    emit_phaseC(nc, P, "c0_", T_['xT'], T_['o0T'], T_['w_out0'], T_['gm0'], T_['w_up0'], T_['w_down0'], T_['h2T'], S)
    emit_mix1(nc, P, T_, NG=4)
    emit_phaseC(nc, P, "c1_", T_['h2T'], T_['o1T'], T_['w_out1'], T_['gm1'], T_['w_up1'], T_['w_down1'], hT, S)
    return nc


def host_shared_inputs(inp):
    def gl(v):
        return np.ascontiguousarray(v.reshape(16, 128).T)
    w = inp['even_w_in'][0]
    m = dict(
        g0=gl(inp['ln_mix_g'][0]),
        Wfm0=np.ascontiguousarray(np.concatenate([w[:, 0:1024], w[:, 1024:2048], w[:, 3072:4096], w[:, 4104:5128], w[:, 5128:6152]], 1)),
        Wtm0=np.ascontiguousarray(np.concatenate([w[:, 2048:3072], w[:, 6152:7176]], 1)),
        Wf0=np.ascontiguousarray(w[:, 4096:4104]),
        bf0=np.ascontiguousarray(inp['even_b_f'][0].reshape(8, 1)),
        gains0=np.ascontiguousarray(np.stack([inp['even_g_q_fox'][0], inp['even_g_k_fox'][0], inp['even_g_q_dil'][0], inp['even_g_k_dil'][0]], 1)),
        w_out0=inp['even_w_out'][0], gm0=gl(inp['ln_mlp_g'][0]), w_up0=inp['w_mlp_up'][0], w_down0=inp['w_mlp_down'][0],
    )
    w = inp['odd_w_in'][0]
    def grp(i):
        return w[:, 2048 + i * 512:2048 + (i + 1) * 512]
    kc, vc, ks, vs, kw, vw = [grp(i) for i in range(6)]
    gpad = np.zeros((D, 256), np.float32); gpad[:, :48] = w[:, 5120:5168]
    m.update(
        g1=gl(inp['ln_mix_g'][1]),
        Wfm1=np.ascontiguousarray(np.concatenate([w[:, 0:2048], ks, kw, kc, vc], 1)),
        Wtm1=np.ascontiguousarray(np.concatenate([vs, vw, gpad], 1)),
        gains1=np.ascontiguousarray(np.stack([inp['odd_g_q'][0], inp['odd_g_ks'][0], inp['odd_g_kw'][0], inp['odd_g_kc'][0]], 1)),
        w1k=inp['odd_phi_k_w1'][0], w2k=inp['odd_phi_k_w2'][0], peTk=np.ascontiguousarray(inp['odd_phi_k_pe'][0].T),
        w1v=inp['odd_phi_v_w1'][0], w2v=inp['odd_phi_v_w2'][0], peTv=np.ascontiguousarray(inp['odd_phi_v_pe'][0].T),
        w_out1=inp['odd_w_out'][0], gm1=gl(inp['ln_mlp_g'][1]), w_up1=inp['w_mlp_up'][1], w_down1=inp['w_mlp_down'][1],
    )
    m.update(host_consts())
    m.update(host_consts1())
    return m


def kernel(**inp):
    inp = {k: np.asarray(v) for k, v in inp.items()}
    x = inp['x']
    B = x.shape[0]
    cores = list(range(8))
    shared = host_shared_inputs(inp)
    nc = build_fused()
    maps = []
    for c in cores:
        m = dict(shared)
        m['xT'] = np.ascontiguousarray(x[c // 2].T)
        maps.append({name: np.ascontiguousarray(m[name], dtype=np.float32) for name, _ in IN_SPECS})
    res = run_bass_kernel_spmd(nc, maps, core_ids=cores).results
    return np.stack([np.ascontiguousarray(np.asarray(res[2 * b]["hT"]).T) for b in range(B)], axis=0).astype(np.float32)
```

```python
import numpy as np, time, sys, os
from contextlib import ExitStack
from concourse.bass_utils import run_bass_kernel_spmd
import numpy as np
import concourse.bass as bass
import concourse.mybir as mybir

F32 = mybir.dt.float32
BF16 = mybir.dt.bfloat16
AF = mybir.ActivationFunctionType
ALU = mybir.AluOpType
AX = mybir.AxisListType

ENGS = ['pe', 'act', 'dve', 'pool', 'sp']


class _Op:
    __slots__ = ('eng', 'fn', 'deps', 'dma', 'needed', 'sem', 'val', 'strict')


class Prog:
    def __init__(self, nc, n_slots=8, same_engine_sync=False):
        self.nc = nc
        self.same = same_engine_sync
        self.n_slots = n_slots
        self.sems = {}
        self.cnt = {e: 0 for e in ENGS}
        self.dma_cum = {}
        self.dma_last = {}
        self.dma_rr = {e: 0 for e in ENGS}
        self.waited = {e: {} for e in ENGS}
        self._stack = []
        for e in ['pe', 'act', 'dve', 'pool']:
            self.sems[e] = nc.alloc_semaphore(name=f"s_{e}")
        for q in ['sp', 'pool', 'act']:
            for s in range(n_slots):
                k = (q, s)
                self.sems[k] = nc.alloc_semaphore(name=f"d_{q}{s}")
                self.dma_cum[k] = 0
                self.dma_last[k] = None
        self.reset_stage()

    def reset_stage(self):
        self.ops = []
        self.last_w = {}
        self.readers = {}

    def _eng(self, e):
        nc = self.nc
        return {'pe': nc.tensor, 'act': nc.scalar, 'dve': nc.vector, 'pool': nc.gpsimd, 'sp': nc.sync}[e]

    def op(self, eng, fn, reads=(), writes=(), dma=False, strict=False):
        o = _Op()
        o.eng = eng; o.fn = fn; o.dma = dma; o.needed = False; o.sem = None; o.val = None; o.strict = strict
        deps = []
        for k in reads:
            w = self.last_w.get(k)
            if w is not None:
                deps.append(w)
        for k in writes:
            w = self.last_w.get(k)
            if w is not None:
                deps.append(w)
            deps.extend(self.readers.get(k, ()))
        if dma:
            slot = (eng, self.dma_rr[eng] % self.n_slots)
            self.dma_rr[eng] += 1
            prev = self.dma_last[slot]
            if prev is not None:
                deps.append(prev)
            self.dma_last[slot] = o
            self.dma_cum[slot] += 16
            o.sem = slot
            o.val = self.dma_cum[slot]
        o.deps = deps
        for k in reads:
            self.readers.setdefault(k, []).append(o)
        for k in writes:
            self.last_w[k] = o
            self.readers[k] = []
        self.ops.append(o)
        return o

    def dma(self, q, out, in_, reads=(), writes=(), **kw):
        return self.op(q, lambda e: e.dma_start(out=out, in_=in_, **kw), reads=reads, writes=writes, dma=True)

    def emit_stage(self, block_name=None, final_wait_all_dma=True):
        nc = self.nc
        ops = self.ops
        for o in ops:
            for d in o.deps:
                if d.dma:
                    continue
                if d.eng == o.eng and not o.dma and not self.same and not o.strict:
                    continue
                d.needed = True
        for o in ops:
            if not o.dma and o.needed:
                self.cnt[o.eng] += 1
                o.sem = o.eng
                o.val = self.cnt[o.eng]
        per = {e: [] for e in ENGS}
        for o in ops:
            per[o.eng].append(o)
        sems = self.sems
        waited = self.waited
        same = self.same
        dma_cum = self.dma_cum

        def emit_engine(ename, eng):
            wd = waited[ename]
            for o in per[ename]:
                need = {}
                for d in o.deps:
                    if d.val is None:
                        continue
                    if (not d.dma) and d.eng == ename and (not o.dma) and (not same) and (not o.strict):
                        continue
                    s = d.sem
                    if need.get(s, 0) < d.val:
                        need[s] = d.val
                for s, v in need.items():
                    if wd.get(s, 0) >= v:
                        continue
                    eng.wait_ge(sems[s], v)
                    wd[s] = v
                ins = o.fn(eng)
                if o.dma:
                    ins.then_inc(sems[o.sem], 16)
                elif o.needed:
                    ins.then_inc(sems[o.sem], 1)
            if final_wait_all_dma:
                for s, v in dma_cum.items():
                    if s[0] == ename and v > 0 and wd.get(s, 0) < v:
                        eng.wait_ge(sems[s], v)
                        wd[s] = v

        with nc.Block() as block:
            @block.tensor
            def _(e):
                emit_engine('pe', e)

            @block.scalar
            def _(e):
                emit_engine('act', e)

            @block.vector
            def _(e):
                emit_engine('dve', e)

            @block.gpsimd
            def _(e):
                emit_engine('pool', e)

            @block.sync
            def _(e):
                emit_engine('sp', e)
        for e in ENGS:
            for s in self.sems:
                if isinstance(s, str):
                    self.waited[e][s] = self.cnt[s]
                else:
                    self.waited[e][s] = self.dma_cum[s]
        self.reset_stage()

from contextlib import ExitStack

D = 2048; EPS = 1e-6; S = 4096
SCALE = 128 ** -0.5
NEGM = -1e30


def norm_stage(nc, P, es, sb, ps, xT, g_in, hnT, ones):
    xt = sb("xt", [128, 16, 256], F32)
    gt = sb("gt", [128, 16], F32)
    sq = [sb(f"nsq{i}", [128, 256], BF16) for i in range(2)]
    rstd = sb("nrstd", [128, 256], F32)
    P.dma('sp', gt[:], g_in[:, :], writes=[('g',)])
    pn = 6
    for tt in range(S // 256):
        tsl = slice(tt * 256, (tt + 1) * 256)
        P.dma('sp', xt[:], xT[:, tsl].rearrange("(kc p) n -> p kc n", p=128), writes=[('xt', dc) for dc in range(16)])
        for dc in range(16):
            si = dc % 2
            P.op('act', lambda e, dc=dc, si=si: e.activation(out=sq[si][:], in_=xt[:, dc, :], func=AF.Square),
                 reads=[('xt', dc)], writes=[('nsq', si)])
            P.op('pe', lambda e, dc=dc, si=si: e.matmul(ps[pn][:, 0:256], ones[:], sq[si][:], start=(dc == 0), stop=(dc == 15)),
                 reads=[('nsq', si), ('ones',)], writes=[('ps', pn)])
        P.op('act', lambda e: e.activation(out=rstd[:], in_=ps[pn][:, 0:256], func=AF.Sqrt, scale=1.0 / D, bias=EPS),
             reads=[('ps', pn)], writes=[('nrstd',)])
        P.op('dve', lambda e: e.reciprocal(out=rstd[:], in_=rstd[:]), reads=[('nrstd',)], writes=[('nrstd',)])
        for dc in range(16):
            P.op('dve', lambda e, dc=dc, tsl=tsl: e.scalar_tensor_tensor(out=hnT[:, dc, tsl], in0=xt[:, dc, :], scalar=gt[:, dc:dc + 1], in1=rstd[:], op0=ALU.mult, op1=ALU.mult),
                 reads=[('xt', dc), ('g',), ('nrstd',)], writes=[('hnT', tt // 2)])


class ProjCtx:
    pass


def run_pipeline(tasks, depth=2):
    n = len(tasks)
    for i in range(min(depth, n)):
        tasks[i].pre()
        tasks[i].a()
    for t in range(n):
        if t + depth < n:
            tasks[t + depth].pre()
            tasks[t + depth].a()
        tasks[t].b()
        tasks[t].c()
        tasks[t].post()


def proj_fm(nc, P, C, wsrc, ncols, specs, dst):
    nblk = ncols // 256
    state = C.state

    def load(blk):
        bi = blk % 2
        P.dma('pool', C.wbl[bi][:], wsrc[:, blk * 256:(blk + 1) * 256].rearrange("(kc p) c -> p kc c", p=128), writes=[('wbl', bi)])

    def make_tile(blk, sub, tt, idx):
        bi = blk % 2
        j = blk * 2 + sub
        sp = specs[j]
        tsl = slice(tt * 512, (tt + 1) * 512)
        pi = idx % 4
        nbk = 4 + (idx % 2)
        rbk = 6 + (idx % 2)
        qi = idx % 3
        qn = C.qn[qi]
        si = idx % 2
        ri = idx % 2
        t = ProjCtx()

        def a():
            if sub == 0 and tt == 0:
                if blk == 0:
                    load(0)
                if blk + 1 < nblk:
                    load(blk + 1)
            for kc in range(16):
                P.op('pe', lambda e, kc=kc: e.matmul(C.ps[pi][:], C.wbl[bi][:, kc, sub * 128:(sub + 1) * 128], C.hnT[:, kc, tsl], start=(kc == 0), stop=(kc == 15)),
                     reads=[('wbl', bi), ('hnT', tt)], writes=[('ps', pi)])

        def b():
            if sp['kind'] == 'raw':
                P.op('act', lambda e: e.activation(out=qn[:], in_=C.ps[pi][:], func=AF.Copy), reads=[('ps', pi)], writes=[('qn', qi)])
                return
            P.op('act', lambda e: e.activation(out=C.sq[si][:], in_=C.ps[pi][:], func=AF.Square), reads=[('ps', pi)], writes=[('sq', si)])
            P.op('pe', lambda e: e.matmul(C.ps[nbk][:], C.ones[:], C.sq[si][:], start=True, stop=True), reads=[('sq', si), ('ones',)], writes=[('ps', nbk)])
            P.op('act', lambda e: e.activation(out=C.rs[ri][:], in_=C.ps[nbk][:], func=AF.Ln, scale=1.0 / 128, bias=EPS), reads=[('ps', nbk)], writes=[('rs', ri)])
            P.op('act', lambda e: e.activation(out=C.rs[ri][:], in_=C.rs[ri][:], func=AF.Exp, scale=-0.5), reads=[('rs', ri)], writes=[('rs', ri)])
            gc = sp['gain']
            P.op('dve', lambda e: e.scalar_tensor_tensor(out=qn[:], in0=C.ps[pi][:], scalar=C.gains[:, gc:gc + 1], in1=C.rs[ri][:], op0=ALU.mult, op1=ALU.mult),
                 reads=[('ps', pi), ('rs', ri), ('gains',)], writes=[('qn', qi)])

        def c():
            if sp['kind'] != 'raw' and sp.get('rope'):
                ct, st = sp['tabs']
                tofs = sp.get('tofs', 0)
                tl = slice(tofs + tt * 512, tofs + (tt + 1) * 512)
                P.op('pe', lambda e: e.matmul(C.ps[rbk][0:32, :], C.Pm[:], qn[0:32, :], start=True, stop=True), reads=[('qn', qi), ('Pm',)], writes=[('ps', rbk)])
                P.op('dve', lambda e: e.tensor_tensor(out=C.r1[:], in0=C.ps[rbk][0:32, :], in1=st[:, tl], op=ALU.mult), reads=[('ps', rbk), ('tabs',)], writes=[('r1',)])
                P.op('dve', lambda e: e.tensor_tensor(out=C.r2[:], in0=qn[0:32, :], in1=ct[:, tl], op=ALU.mult), reads=[('qn', qi), ('tabs',)], writes=[('r2',)])
                P.op('dve', lambda e: e.tensor_tensor(out=qn[0:32, :], in0=C.r1[:], in1=C.r2[:], op=ALU.add), reads=[('r1',), ('r2',)], writes=[('qn', qi)])
            P.dma('sp', dst(j, tt), qn[:], reads=[('qn', qi)], writes=[('dst_fm', id(dst), j, tt)])

        t.a, t.b, t.c = a, b, c
        return t

    tiles = []
    for blk in range(nblk):
        for sub in range(2):
            for tt in range(8):
                tiles.append(make_tile(blk, sub, tt, len(tiles)))
    n = len(tiles)
    tiles[0].a()
    for t in range(n):
        if t + 1 < n:
            tiles[t + 1].a()
        tiles[t].b()
        if t >= 1:
            tiles[t - 1].c()
    tiles[n - 1].c()


def proj_tm(nc, P, C, wsrc, ncols, dst):
    nblk = ncols // 256
    state = C.state
    for blk in range(nblk):
        bi = state['wrr'] % 2
        state['wrr'] += 1
        P.dma('pool', C.wbl[bi][:], wsrc[:, blk * 256:(blk + 1) * 256].rearrange("(kc p) c -> p kc c", p=128), writes=[('wbl', bi)])
        for tb4 in range(8):
            vi = state['vrr'] % 2
            state['vrr'] += 1
            for i in range(4):
                tb = tb4 * 4 + i
                pi = state['psrr'] % 4
                state['psrr'] += 1
                for kc in range(16):
                    P.op('pe', lambda e, kc=kc, pi=pi, bi=bi, tb=tb: e.matmul(C.ps[pi][:, 0:256], C.hnT[:, kc, tb * 128:(tb + 1) * 128], C.wbl[bi][:, kc, :], start=(kc == 0), stop=(kc == 15)),
                         reads=[('wbl', bi), ('hnT', tb // 4)], writes=[('ps', pi)])
                P.op('act', lambda e, pi=pi, vi=vi, i=i: e.activation(out=C.vst[vi][:, i, :], in_=C.ps[pi][:, 0:256], func=AF.Copy),
                     reads=[('ps', pi)], writes=[('vst', vi)])
            P.dma('sp', dst(blk, tb4), C.vst[vi][:], reads=[('vst', vi)], writes=[('dst_tm', blk, tb4)])


def attn_out_block(nc, P, C, Ops, qb4, gate_ap, og, key_og, gate_key=None):
    st = C.state
    oi = st['onrr'] % 2
    st['onrr'] += 1
    P.op('dve', lambda e, oi=oi: e.reciprocal(out=C.rden[oi][:], in_=Ops[:, 128:129]),
         reads=[('O', qb4)], writes=[('rden', oi)])
    P.op('dve', lambda e, oi=oi: e.tensor_scalar(out=C.on[oi][:], in0=Ops[:, 0:128], scalar1=C.rden[oi][:, 0:1], scalar2=None, op0=ALU.mult),
         reads=[('O', qb4), ('rden', oi)], writes=[('on', oi)], strict=True)
    ti = st['trr'] % 4
    st['trr'] += 1
    P.op('pe', lambda e, oi=oi, ti=ti: e.transpose(C.pst[:, ti * 128:(ti + 1) * 128], C.on[oi][:], C.ident[:]),
         reads=[('on', oi), ('ident',)], writes=[('pst', ti)])
    if gate_ap is not None:
        P.op('dve', lambda e, ti=ti: e.tensor_tensor(out=og[:, qb4 * 128:(qb4 + 1) * 128], in0=C.pst[:, ti * 128:(ti + 1) * 128], in1=gate_ap, op=ALU.mult),
             reads=[('pst', ti), gate_key], writes=[key_og])
    else:
        P.op('dve', lambda e, ti=ti: e.tensor_copy(out=og[:, qb4 * 128:(qb4 + 1) * 128], in_=C.pst[:, ti * 128:(ti + 1) * 128]),
             reads=[('pst', ti)], writes=[key_og])


def emit_mix0(nc, P, T_, NH=8):
    xT = T_['xT']; g_in = T_['g0']; Wfm = T_['Wfm0']; Wtm = T_['Wtm0']; Wf = T_['Wf0']; bfv = T_['bf0']
    gains_in = T_['gains0']; ropeC = T_['ropeC']; ropeS = T_['ropeS']; Pm_in = T_['Pm']; trim_in = T_['trim']
    ident_in = T_['ident']; Mt_in = T_['Mt']; oT = T_['o0T']
    debug_stage1_only = False
    qk_s = nc.dram_tensor("a_qk_s", [5 * NH, 128, S], BF16, kind="Internal").ap()
    v_s = nc.dram_tensor("a_v_s", [S, 2 * NH * 128], BF16, kind="Internal").ap()
    c_s = nc.dram_tensor("a_c_s", [NH, S], F32, kind="Internal").ap()
    nheads_fox = nheads_dil = NH

    with ExitStack() as es:
        def sb(name, shape, dt):
            return es.enter_context(nc.sbuf_tensor("a1_" + name, shape, dt))
        C = ProjCtx()
        C.state = dict(wrr=0, psrr=0, qrr=0, sqrr=0, rsrr=0, vrr=0)
        C.ps = [es.enter_context(nc.psum_tensor(f"a1ps{i}", [128, 512], F32)) for i in range(8)]
        C.hnT = sb("hnT", [128, 16, S], BF16)
        C.ones = sb("ones", [128, 128], BF16)
        C.wbl = [sb(f"wbl{i}", [128, 16, 256], BF16) for i in range(2)]
        C.Ct = sb("Ct", [32, S], BF16)
        C.St = sb("St", [32, S], BF16)
        C.Pm = sb("Pmt", [32, 32], BF16)
        C.gains = sb("gains", [128, 4], F32)
        C.sq = [sb(f"sq{i}", [128, 512], BF16) for i in range(2)]
        C.rs = [sb(f"rs{i}", [128, 512], F32) for i in range(2)]
        C.qn = [sb(f"qn{i}", [128, 512], BF16) for i in range(3)]
        C.r1 = sb("r1", [32, 512], F32)
        C.r2 = sb("r2", [32, 512], F32)
        C.vst = [sb(f"vst{i}", [128, 4, 256], BF16) for i in range(2)]
        wft = sb("wft", [128, 16, NH], BF16)
        bft = sb("bft", [NH, 1], F32)
        onesr = sb("onesr", [NH, 512], F32)
        fx = sb("fx", [NH, 512], F32)
        fa_ = sb("fa_", [NH, 512], F32)
        fm_ = sb("fm_", [NH, 512], F32)
        ct = [sb(f"ct{i}", [NH, 512], F32) for i in range(2)]

        P.op('pool', lambda e: e.memset(C.ones[:], 1.0), writes=[('ones',)])
        P.op('pool', lambda e: e.memset(onesr[:], 1.0), writes=[('onesr',)])
        P.dma('pool', C.Ct[:], ropeC[:, :], writes=[('tabs',)])
        P.dma('pool', C.St[:], ropeS[:, :], writes=[('tabs',)])
        P.dma('pool', C.Pm[:], Pm_in[:, :], writes=[('Pm',)])
        P.dma('sp', C.gains[:], gains_in[:, :], writes=[('gains',)])
        P.dma('pool', wft[:], Wf.rearrange("(kc p) c -> p kc c", p=128), writes=[('wft',)])
        P.dma('sp', bft[:], bfv[:, :], writes=[('bft',)])
        P.op('act', lambda e: e.mul(C.gains[:, 0:1], C.gains[:, 0:1], SCALE), reads=[('gains',)], writes=[('gains',)])
        P.op('act', lambda e: e.mul(C.gains[:, 2:3], C.gains[:, 2:3], SCALE), reads=[('gains',)], writes=[('gains',)])

        norm_stage(nc, P, es, sb, C.ps, xT, g_in, C.hnT, C.ones)

        for tt in range(8):
            tsl = slice(tt * 512, (tt + 1) * 512)
            for kc in range(16):
                P.op('pe', lambda e, kc=kc, tsl=tsl: e.matmul(C.ps[7][0:NH, :], wft[:, kc, :], C.hnT[:, kc, tsl], start=(kc == 0), stop=(kc == 15)),
                     reads=[('wft',), ('hnT', tt)], writes=[('ps', 7)])
            P.op('dve', lambda e: e.tensor_scalar(out=fx[:], in0=C.ps[7][0:NH, :], scalar1=bft[:, 0:1], scalar2=None, op0=ALU.add),
                 reads=[('ps', 7), ('bft',)], writes=[('fx',)])
            P.op('act', lambda e: e.activation(out=fa_[:], in_=fx[:], func=AF.Abs), reads=[('fx',)], writes=[('fa_',)])
            P.op('act', lambda e: e.activation(out=fa_[:], in_=fa_[:], func=AF.Exp, scale=-1.0), reads=[('fa_',)], writes=[('fa_',)])
            P.op('act', lambda e: e.activation(out=fa_[:], in_=fa_[:], func=AF.Ln, bias=1.0), reads=[('fa_',)], writes=[('fa_',)])
            P.op('dve', lambda e: e.tensor_scalar(out=fm_[:], in0=fx[:], scalar1=0.0, scalar2=None, op0=ALU.min),
                 reads=[('fx',)], writes=[('fm_',)])
            P.op('dve', lambda e: e.tensor_tensor(out=fm_[:], in0=fm_[:], in1=fa_[:], op=ALU.subtract),
                 reads=[('fm_',), ('fa_',)], writes=[('fm_',)])
            ci = tt % 2
            init = 0.0 if tt == 0 else ct[1 - ci][:, 511:512]
            P.op('dve', lambda e, ci=ci, init=init: e.tensor_tensor_scan(out=ct[ci][:], data0=onesr[:], data1=fm_[:], initial=init, op0=ALU.mult, op1=ALU.add),
                 reads=[('fm_',), ('onesr',), ('ct', 1 - ci)], writes=[('ct', ci)], strict=True)
            P.dma('sp', c_s[:, tsl], ct[ci][:], reads=[('ct', ci)], writes=[('c_s', tt)])

        specs = ([dict(kind='norm', gain=0)] * NH + [dict(kind='norm', gain=1)] * NH + [dict(kind='raw')] * NH +
                 [dict(kind='norm', gain=2, rope=True, tabs=(C.Ct, C.St))] * NH + [dict(kind='norm', gain=3, rope=True, tabs=(C.Ct, C.St))] * NH)
        proj_fm(nc, P, C, Wfm, 5 * NH * 128, specs, lambda j, tt: qk_s[j, :, tt * 512:(tt + 1) * 512])
        proj_tm(nc, P, C, Wtm, 2 * NH * 128, lambda blk, tb4: v_s[tb4 * 512:(tb4 + 1) * 512, blk * 256:(blk + 1) * 256].rearrange("(i p) c -> p i c", p=128))
        P.emit_stage()


    with ExitStack() as es:
        def sb(name, shape, dt):
            return es.enter_context(nc.sbuf_tensor("a2_" + name, shape, dt))
        C = ProjCtx()
        C.state = dict(onrr=0, trr=0, srr=0, trr2=0, prr=0, ogrr=0)
        C.ps = [es.enter_context(nc.psum_tensor(f"a2ps{i}", [128, 512], F32)) for i in range(7)]
        C.pst = es.enter_context(nc.psum_tensor("a2pst", [128, 1024], BF16))
        C.ident = sb("ident", [128, 128], BF16)
        trim = sb("trim", [128, 128], F32)
        Mt = sb("Mt", [128, 20, 512], BF16)
        C.rden = [sb(f"rden{i}", [128, 1], F32) for i in range(2)]
        C.on = [sb(f"on{i}", [128, 128], BF16) for i in range(2)]
        tt_ = [sb(f"tt{i}", [128, 512], F32) for i in range(2)]
        et = [sb(f"et{i}", [128, 512], BF16) for i in range(2)]
        pT = [sb(f"pT{i}", [128, 512], BF16) for i in range(3)]
        og = [sb(f"og{i}", [128, 512], F32) for i in range(2)]
        hb = []
        for i in range(2):
            h_ = ProjCtx()
            h_.qT = sb(f"hq{i}", [128, S], BF16)
            h_.kT = sb(f"hk{i}", [128, S], BF16)
            h_.gT = sb(f"hg{i}", [128, S], BF16)
            h_.V = sb(f"hv{i}", [128, 32, 136], BF16)
            h_.cqb = sb(f"hcq{i}", [128, S], F32)
            h_.ck = sb(f"hck{i}", [128, 32], F32)
            hb.append(h_)
        P.dma('pool', C.ident[:], ident_in[:, :], writes=[('ident',)])
        P.dma('sp', trim[:], trim_in[:, :], writes=[('trim',)])
        P.dma('pool', Mt[:], Mt_in.rearrange("p (o c) -> p o c", o=20), writes=[('Mt',)])
        for i in range(2):
            P.op('pool', lambda e, i=i: e.memset(hb[i].V[:, :, 128:136], 1.0), writes=[('hb_V1', i)])

        heads = [('fox', h) for h in range(nheads_fox)] + [('dil', h) for h in range(nheads_dil)]
        S3 = [0, 1, 6]
        NQC0 = int(os.environ.get('NQC', '8'))

        def make_pre(hi, kind, h):
            bi = hi % 2
            H = hb[bi]
            kq, kk, kg, kv, kc_, kcq = [('hb', bi, n) for n in ('q', 'k', 'g', 'v', 'ck', 'cq')]
            if kind == 'fox':
                jq, jk, jg, vc0 = h, NH + h, 2 * NH + h, h * 128
            else:
                jq, jk, jg, vc0 = 3 * NH + h, 4 * NH + h, None, NH * 128 + h * 128

            def pre():
                P.dma('sp', H.qT[:], qk_s[jq, :, :], writes=[kq])
                P.dma('sp', H.kT[:], qk_s[jk, :, :], writes=[kk])
                P.dma('sp', H.V[:, :, 0:128], v_s[:, vc0:vc0 + 128].rearrange("(blk p) c -> p blk c", p=128), reads=[('hb_V1', bi)], writes=[kv])
                if kind == 'fox':
                    P.dma('sp', H.gT[:], qk_s[jg, :, :], writes=[kg])
                    P.op('act', lambda e: e.activation(out=H.gT[:], in_=H.gT[:], func=AF.Sigmoid), reads=[kg], writes=[kg])
                    P.dma('sp', H.ck[:], c_s[h, :].rearrange("(blk p) -> p blk", p=128), reads=[], writes=[kc_], allow_slow_non_contiguous=True)
                    P.op('pool', lambda e: e.tensor_scalar(out=H.ck[:], in0=H.ck[:], scalar1=-1.0, scalar2=None, op0=ALU.mult), reads=[kc_], writes=[kc_])
                    P.dma('sp', H.cqb[:], c_s[h, :].partition_broadcast(128), writes=[kcq])
            return pre

        def make_tile(hi, kind, h, Qc, kb, pre, last_of_chunk, ogi):
            bi = hi % 2
            H = hb[bi]
            kq, kk, kg, kv, kc_, kcq = [('hb', bi, n) for n in ('q', 'k', 'g', 'v', 'ck', 'cq')]
            q0 = Qc * 512
            kb_lo = 0 if kind == 'fox' else max(0, 4 * Qc - 16)
            j = kb - 4 * Qc
            qlo = max(0, j) * 128
            t = ProjCtx()

            def a():
                si = S3[C.state['srr'] % 3]
                C.state['srr'] += 1
                t.si = si
                P.op('pe', lambda e: e.matmul(C.ps[si][:, qlo:512], H.kT[:, kb * 128:(kb + 1) * 128], H.qT[:, q0 + qlo:q0 + 512], start=True, stop=True),
                     reads=[kq, kk], writes=[('ps', si)])

            def b():
                si = t.si
                pi = C.state['prr'] % 3
                C.state['prr'] += 1
                t.pi = pi
                if kind == 'fox':
                    ti = C.state['trr2'] % 2
                    C.state['trr2'] += 1
                    P.op('dve', lambda e: e.tensor_tensor(out=tt_[ti][:, qlo:512], in0=C.ps[si][:, qlo:512], in1=H.cqb[:, q0 + qlo:q0 + 512], op=ALU.add),
                         reads=[('ps', si), kcq], writes=[('tt', ti)])
                    if j >= 0:
                        P.op('dve', lambda e: e.tensor_tensor(out=tt_[ti][:, qlo:qlo + 128], in0=tt_[ti][:, qlo:qlo + 128], in1=trim[:], op=ALU.add),
                             reads=[('tt', ti), ('trim',)], writes=[('tt', ti)])
                    P.op('act', lambda e: e.activation(out=pT[pi][:, qlo:512], in_=tt_[ti][:, qlo:512], func=AF.Exp, bias=H.ck[:, kb:kb + 1]),
                         reads=[('tt', ti), kc_], writes=[('pT', pi)])
                else:
                    ei = C.state['trr2'] % 2
                    C.state['trr2'] += 1
                    oi_ = 4 * Qc - kb + 3
                    P.op('act', lambda e: e.activation(out=et[ei][:, qlo:512], in_=C.ps[si][:, qlo:512], func=AF.Exp),
                         reads=[('ps', si)], writes=[('et', ei)])
                    P.op('dve', lambda e: e.tensor_tensor(out=pT[pi][:, qlo:512], in0=et[ei][:, qlo:512], in1=Mt[:, oi_, qlo:512], op=ALU.mult),
                         reads=[('et', ei), ('Mt',)], writes=[('pT', pi)])

            def c():
                pi = t.pi
                for qb4 in range(max(0, j), 4):
                    last_kb = 4 * Qc + qb4
                    P.op('pe', lambda e, qb4=qb4, last_kb=last_kb: e.matmul(C.ps[2 + qb4][:, 0:130], pT[pi][:, qb4 * 128:(qb4 + 1) * 128], H.V[:, kb, 0:130], start=(kb == kb_lo), stop=(kb == last_kb)),
                         reads=[('pT', pi), kv], writes=[('O', qb4)])

            def post():
                if not last_of_chunk:
                    return
                for qb4 in range(4):
                    gate_ap = H.gT[:, q0 + qb4 * 128:q0 + (qb4 + 1) * 128] if kind == 'fox' else None
                    attn_out_block(nc, P, C, C.ps[2 + qb4], qb4, gate_ap, og[ogi], ('og', ogi), gate_key=kg)
                orow = (h if kind == 'fox' else NH + h) * 128
                P.dma('sp', oT[orow:orow + 128, q0:q0 + 512], og[ogi][:], reads=[('og', ogi)], writes=[('oT', hi, Qc)])

            t.pre = pre if pre is not None else (lambda: None)
            t.a, t.b, t.c, t.post = a, b, c, post
            return t

        tasks = []
        pres = [make_pre(hi, kind, h) for hi, (kind, h) in enumerate(heads)]
        for hi, (kind, h) in enumerate(heads):
            nt_head = 0
            for Qc in range(NQC0):
                kb_lo = 0 if kind == 'fox' else max(0, 4 * Qc - 16)
                kb_hi = 4 * Qc + 3
                ogi = C.state['ogrr'] % 2
                C.state['ogrr'] += 1
                for kb in range(kb_lo, kb_hi + 1):
                    pre = None
                    if hi == 0 and nt_head == 0:
                        pre = pres[0]
                    if nt_head == 2 and hi + 1 < len(heads):
                        pre = pres[hi + 1]
                    tasks.append(make_tile(hi, kind, h, Qc, kb, pre, kb == kb_hi, ogi))
                    nt_head += 1
        run_pipeline(tasks, depth=2)
        P.emit_stage()


def host_consts():
    inv = 1.0 / (500000.0 ** (np.arange(0, 32, 2, dtype=np.float32) / 32))
    ang = np.arange(S, dtype=np.float32)[None, :] * inv[:, None]
    cos = np.cos(ang).astype(np.float32); sin = np.sin(ang).astype(np.float32)
    ropeC = np.concatenate([cos, cos], 0)
    ropeS = np.concatenate([-sin, sin], 0)
    Pm = np.zeros((32, 32), np.float32)
    for m in range(32):
        Pm[(m + 16) % 32, m] = 1.0
    p = np.arange(128)[:, None]; c = np.arange(128)[None, :]
    trim = np.where(p > c, NEGM, 0.0).astype(np.float32)
    ident = np.eye(128, dtype=np.float32)
    Mt = np.zeros((128, 20, 512), np.float32)
    for oi in range(20):
        o = 128 * (oi - 3)
        d = o + np.arange(512)[None, :] - np.arange(128)[:, None]
        m = ((d >= 0) & (d <= 128)).astype(np.float32) + ((d >= 0) & (d <= 512) & (d % 4 == 0)) + ((d >= 0) & (d <= 2048) & (d % 16 == 0))
        Mt[:, oi, :] = m
    return dict(ropeC=ropeC, ropeS=ropeS, Pm=Pm, trim=trim, ident=ident, Mt=Mt.reshape(128, 20 * 512))


def host_inputs_mix0(inp, b, half):
    w = inp['even_w_in'][0]
    hs = slice(half * 512, (half + 1) * 512)
    qa = w[:, 0:1024][:, hs]; ka = w[:, 1024:2048][:, hs]; va = w[:, 2048:3072][:, hs]; ga = w[:, 3072:4096][:, hs]
    fa = w[:, 4096:4104][:, half * 4:(half + 1) * 4]
    qd = w[:, 4104:5128][:, hs]; kd = w[:, 5128:6152][:, hs]; vd = w[:, 6152:7176][:, hs]
    m = dict(
        xT=np.ascontiguousarray(inp['x'][b].T),
        g=np.ascontiguousarray(inp['ln_mix_g'][0].reshape(16, 128).T),
        Wfm=np.ascontiguousarray(np.concatenate([qa, ka, ga, qd, kd], 1)),
        Wtm=np.ascontiguousarray(np.concatenate([va, vd], 1)),
        Wf=np.ascontiguousarray(fa),
        bf=np.ascontiguousarray(inp['even_b_f'][0][half * 4:(half + 1) * 4].reshape(4, 1)),
        gains=np.ascontiguousarray(np.stack([inp['even_g_q_fox'][0], inp['even_g_k_fox'][0], inp['even_g_q_dil'][0], inp['even_g_k_dil'][0]], 1)),
    )
    m.update(host_consts())
    return m


NB = -30000.0
GC = 1.5957691216057308


def emit_mix1(nc, P, T_, NG=4):
    xT = T_['h2T']; g_in = T_['g1']; Wfm = T_['Wfm1']; Wtm = T_['Wtm1']; gains_in = T_['gains1']
    ropeC = T_['ropeC']; ropeS = T_['ropeS']; Pm_in = T_['Pm']; ropeCc = T_['ropeCc']; ropeSc = T_['ropeSc']
    ident_in = T_['ident']
    w1k = T_['w1k']; w2k = T_['w2k']; peTk = T_['peTk']; w1v = T_['w1v']; w2v = T_['w2v']; peTv = T_['peTv']
    ovl_in = T_['ovl']; cmpM_in = T_['cmpM']; FT_in = T_['FT']; CT_in = T_['CT']; ExpT_in = T_['ExpT']; WB_in = T_['WB']
    trib_in = T_['trib']; oT = T_['o1T']
    dbg = False
    dbgo = nc.dram_tensor("b_dbgo", [128, 2048], F32, kind="Internal").ap()
    qk_s = nc.dram_tensor("b_qk_s", [8 * NG, 128, S], BF16, kind="Internal").ap()
    v_s = nc.dram_tensor("b_v_s", [S, 2 * NG * 128 + 256], BF16, kind="Internal").ap()
    kc_s = nc.dram_tensor("b_kc_s", [NG, 128, 256], BF16, kind="Internal").ap()
    vc_s = nc.dram_tensor("b_vc_s", [NG, 256, 128], BF16, kind="Internal").ap()

    with ExitStack() as es:
        def sb(name, shape, dt):
            return es.enter_context(nc.sbuf_tensor("b1_" + name, shape, dt))
        C = ProjCtx()
        C.state = dict(wrr=0, psrr=0, qrr=0, sqrr=0, rsrr=0, vrr=0)
        C.ps = [es.enter_context(nc.psum_tensor(f"b1ps{i}", [128, 512], F32)) for i in range(8)]
        C.hnT = sb("hnT", [128, 16, S], BF16)
        C.ones = sb("ones", [128, 128], BF16)
        C.wbl = [sb(f"wbl{i}", [128, 16, 256], BF16) for i in range(2)]
        C.Ct = sb("Ct", [32, S], BF16)
        C.St = sb("St", [32, S], BF16)
        C.Pm = sb("Pmt", [32, 32], BF16)
        C.gains = sb("gains", [128, 4], F32)
        C.sq = [sb(f"sq{i}", [128, 512], BF16) for i in range(2)]
        C.rs = [sb(f"rs{i}", [128, 512], F32) for i in range(2)]
        C.qn = [sb(f"qn{i}", [128, 512], BF16) for i in range(3)]
        C.r1 = sb("r1", [32, 512], F32)
        C.r2 = sb("r2", [32, 512], F32)
        C.vst = [sb(f"vst{i}", [128, 4, 256], BF16) for i in range(2)]
        P.op('pool', lambda e: e.memset(C.ones[:], 1.0), writes=[('ones',)])
        P.dma('pool', C.Ct[:], ropeC[:, :], writes=[('tabs',)])
        P.dma('pool', C.St[:], ropeS[:, :], writes=[('tabs',)])
        P.dma('pool', C.Pm[:], Pm_in[:, :], writes=[('Pm',)])
        P.dma('sp', C.gains[:], gains_in[:, :], writes=[('gains',)])
        P.op('act', lambda e: e.mul(C.gains[:, 0:1], C.gains[:, 0:1], SCALE), reads=[('gains',)], writes=[('gains',)])
        norm_stage(nc, P, es, sb, C.ps, xT, g_in, C.hnT, C.ones)
        rp = dict(rope=True, tabs=(C.Ct, C.St))
        specs = ([dict(kind='norm', gain=0, **rp)] * (4 * NG) + [dict(kind='norm', gain=1, **rp)] * NG +
                 [dict(kind='norm', gain=2, **rp)] * NG + [dict(kind='raw')] * (2 * NG))
        proj_fm(nc, P, C, Wfm, 8 * NG * 128, specs, lambda j, tt: qk_s[j, :, tt * 512:(tt + 1) * 512])
        proj_tm(nc, P, C, Wtm, 2 * NG * 128 + 256, lambda blk, tb4: v_s[tb4 * 512:(tb4 + 1) * 512, blk * 256:(blk + 1) * 256].rearrange("(i p) c -> p i c", p=128))
        P.emit_stage()

    with ExitStack() as es:
        def sb(name, shape, dt):
            return es.enter_context(nc.sbuf_tensor("bc_" + name, shape, dt))
        ps = [es.enter_context(nc.psum_tensor(f"bcps{i}", [128, 512], F32)) for i in range(4)]
        ones = sb("ones", [128, 128], BF16)
        Pm = sb("Pm", [32, 32], BF16)
        Cc = sb("Cc", [32, 256], BF16); Sc = sb("Sc", [32, 256], BF16)
        gains = sb("gains", [128, 4], F32)
        P.op('pool', lambda e: e.memset(ones[:], 1.0), writes=[('ones',)])
        P.dma('pool', Pm[:], Pm_in[:, :], writes=[('Pm',)])
        P.dma('pool', Cc[:], ropeCc[:, :], writes=[('tabc',)])
        P.dma('pool', Sc[:], ropeSc[:, :], writes=[('tabc',)])
        P.dma('sp', gains[:], gains_in[:, :], writes=[('gains',)])
        w1s = [sb(f"w1s{i}", [128, 32, 128], BF16) for i in range(2)]
        w2s = [sb(f"w2s{i}", [128, 128], BF16) for i in range(2)]
        pes = [sb(f"pes{i}", [128, 32], BF16) for i in range(2)]
        for i, (w1, w2, pe) in enumerate([(w1k, w2k, peTk), (w1v, w2v, peTv)]):
            P.dma('pool', w1s[i][:], w1.rearrange("(l p) o -> p l o", p=128), writes=[('w1s', i)])
            P.dma('pool', w2s[i][:], w2[:, :], writes=[('w2s', i)])
            P.dma('pool', pes[i][:], pe[:, :], writes=[('pes', i)])
        xc = sb("xc", [128, S], BF16)
        bz = sb("bz", [128, 1], F32)
        zs = sb("zs", [128, 256], F32); z2 = sb("z2", [128, 256], F32); sg = sb("sg", [128, 256], F32)
        G = sb("G", [128, 256], BF16)
        sq = sb("sq", [128, 256], BF16); rs = sb("rs", [128, 256], F32); kn = sb("kn", [128, 256], BF16)
        r1 = sb("r1", [32, 256], F32); r2 = sb("r2", [32, 256], F32)
        vct = sb("vct", [128, 2, 128], BF16)
        P.op('pool', lambda e: e.memset(G[:], 0.0), writes=[('G',)])
        P.op('pool', lambda e: e.memset(kn[:], 0.0), writes=[('kn',)])
        for g in range(NG):
            for i in range(2):
                P.dma('sp', xc[:], qk_s[6 * NG + NG * i + g, :, :], writes=[('xc',)])
                xv = xc[:].rearrange("p (i r) -> p i r", r=16)
                for l in range(32):
                    a, r = (0, l) if l < 16 else (1, l - 16)
                    P.op('pe', lambda e, l=l, a=a, r=r, i=i: e.matmul(ps[0][:, 0:255], w1s[i][:, l, :], xv[:, a:a + 255, r], start=(l == 0), stop=(l == 31)),
                         reads=[('w1s', i), ('xc',)], writes=[('ps', 0)])
                for l in range(32):
                    P.op('pe', lambda e, l=l, i=i: e.matmul(ps[1][:, 0:1], w1s[i][:, l, :], pes[i][:, l:l + 1], start=(l == 0), stop=(l == 31)),
                         reads=[('w1s', i), ('pes', i)], writes=[('ps', 1)])
                P.op('dve', lambda e: e.tensor_copy(out=bz[:], in_=ps[1][:, 0:1]), reads=[('ps', 1)], writes=[('bz',)])
                P.op('dve', lambda e: e.tensor_scalar(out=zs[:, 0:255], in0=ps[0][:, 0:255], scalar1=bz[:, 0:1], scalar2=None, op0=ALU.add),
                     reads=[('ps', 0), ('bz',)], writes=[('zs',)], strict=True)
                P.op('dve', lambda e: e.tensor_tensor(out=z2[:, 0:255], in0=zs[:, 0:255], in1=zs[:, 0:255], op=ALU.mult), reads=[('zs',)], writes=[('z2',)])
                P.op('dve', lambda e: e.tensor_scalar(out=z2[:, 0:255], in0=z2[:, 0:255], scalar1=0.044715, scalar2=1.0, op0=ALU.mult, op1=ALU.add), reads=[('z2',)], writes=[('z2',)])
                P.op('dve', lambda e: e.tensor_tensor(out=z2[:, 0:255], in0=z2[:, 0:255], in1=zs[:, 0:255], op=ALU.mult), reads=[('z2',), ('zs',)], writes=[('z2',)])
                P.op('act', lambda e: e.activation(out=sg[:, 0:255], in_=z2[:, 0:255], func=AF.Sigmoid, scale=GC), reads=[('z2',)], writes=[('sg',)])
                P.op('dve', lambda e: e.tensor_tensor(out=G[:, 0:255], in0=zs[:, 0:255], in1=sg[:, 0:255], op=ALU.mult), reads=[('zs',), ('sg',)], writes=[('G',)])
                if i == 0:
                    P.op('pe', lambda e: e.matmul(ps[2][:, 0:256], w2s[0][:], G[:], start=True, stop=True), reads=[('w2s', 0), ('G',)], writes=[('ps', 2)])
                    P.op('act', lambda e: e.activation(out=sq[:], in_=ps[2][:, 0:256], func=AF.Square), reads=[('ps', 2)], writes=[('sq',)])
                    P.op('pe', lambda e: e.matmul(ps[3][:, 0:256], ones[:], sq[:], start=True, stop=True), reads=[('sq',), ('ones',)], writes=[('ps', 3)])
                    P.op('act', lambda e: e.activation(out=rs[:], in_=ps[3][:, 0:256], func=AF.Sqrt, scale=1.0 / 128, bias=EPS), reads=[('ps', 3)], writes=[('rs',)])
                    P.op('dve', lambda e: e.reciprocal(out=rs[:], in_=rs[:]), reads=[('rs',)], writes=[('rs',)])
                    P.op('dve', lambda e: e.scalar_tensor_tensor(out=kn[:], in0=ps[2][:, 0:256], scalar=gains[:, 3:4], in1=rs[:], op0=ALU.mult, op1=ALU.mult),
                         reads=[('ps', 2), ('rs',), ('gains',)], writes=[('kn',)])
                    P.op('pe', lambda e: e.matmul(ps[3][0:32, 0:256], Pm[:], kn[0:32, :], start=True, stop=True), reads=[('kn',), ('Pm',)], writes=[('ps', 3)])
                    P.op('dve', lambda e: e.tensor_tensor(out=r1[:], in0=ps[3][0:32, 0:256], in1=Sc[:], op=ALU.mult), reads=[('ps', 3), ('tabc',)], writes=[('r1',)])
                    P.op('dve', lambda e: e.tensor_tensor(out=r2[:], in0=kn[0:32, :], in1=Cc[:], op=ALU.mult), reads=[('kn',), ('tabc',)], writes=[('r2',)])
                    P.op('dve', lambda e: e.tensor_tensor(out=kn[0:32, :], in0=r1[:], in1=r2[:], op=ALU.add), reads=[('r1',), ('r2',)], writes=[('kn',)])
                    P.dma('sp', kc_s[g, :, :], kn[:], reads=[('kn',)], writes=[('kc_s', g)])
                else:
                    for nb in range(2):
                        P.op('pe', lambda e, nb=nb: e.matmul(ps[2][:, nb * 128:(nb + 1) * 128], G[:, nb * 128:(nb + 1) * 128], w2s[1][:], start=True, stop=True),
                             reads=[('w2s', 1), ('G',)], writes=[('ps', 2)])
                    P.op('act', lambda e: e.activation(out=vct[:].rearrange("p a b -> p (a b)"), in_=ps[2][:, 0:256], func=AF.Copy), reads=[('ps', 2)], writes=[('vct',)])
                    P.dma('sp', vc_s[g].rearrange("(nb p) d -> p nb d", p=128), vct[:], reads=[('vct',)], writes=[('vc_s', g)])
        P.emit_stage()

    with ExitStack() as es:
        def sb(name, shape, dt):
            return es.enter_context(nc.sbuf_tensor("b2_" + name, shape, dt))
        st = dict(srr=0, prr=0, err=0, onrr=0, trr=0, ogrr=0, scl=0)
        ps = [es.enter_context(nc.psum_tensor(f"b2ps{i}", [128, 512], F32)) for i in range(7)]
        pst = es.enter_context(nc.psum_tensor("b2pst", [128, 1024], BF16))
        ident = sb("ident", [128, 128], BF16)
        trib = sb("trib", [128, 128], BF16)
        cmpM = sb("cmpM", [128, 2, S], BF16)
        ExpT = sb("ExpT", [64, 32, 128], BF16)
        WB = sb("WB", [128, 8, 512], BF16)
        P.dma('pool', ident[:], ident_in[:, :], writes=[('ident',)])
        P.dma('pool', trib[:], trib_in[:, :], writes=[('trib',)])
        for nb in range(2):
            for c in range(8):
                P.dma('pool', cmpM[:, nb, c * 512:(c + 1) * 512], cmpM_in[nb * 128:(nb + 1) * 128, c * 512:(c + 1) * 512], writes=[('cmpM', nb, c)])
        P.dma('pool', ExpT[:], ExpT_in.rearrange("j (k p) -> j k p", k=32), writes=[('ExpT',)])
        P.dma('pool', WB[:], WB_in.rearrange("p (o c) -> p o c", o=8), writes=[('WB',)])
        WBm = sb("WBm", [128, 8, 512], BF16)
        P.op('pool', lambda e: e.tensor_scalar(out=WBm[:], in0=WB[:], scalar1=-1.0, scalar2=None, op0=ALU.is_gt), reads=[('WB',)], writes=[('WBm',)])
        ksT = sb("ksT", [128, S], BF16); kwT = sb("kwT", [128, S], BF16)
        VS = sb("VS", [128, 32, 136], BF16); VW = sb("VW", [128, 32, 136], BF16)
        kcT = sb("kcT", [128, 256], BF16)
        VC = sb("VC", [128, 2, 200], BF16)
        gsg = sb("gsg", [128, 32, 12], F32)
        gtmp = sb("gtmp", [128, 32, 12], BF16)
        qT = [sb(f"qT{i}", [128, S], BF16) for i in range(4)]
        oacc = [sb(f"oacc{i}", [128, 4, 128], F32) for i in range(4)]
        impa = sb("impa", [128, 4, 64], F32)
        FTt = sb("FTt", [128, 4, 64], F32); CTt = sb("CTt", [128, 4, 64], F32)
        m8 = sb("m8", [128, 8], F32); m8b = sb("m8b", [128, 8], F32)
        wk = sb("wk", [128, 64], F32); s1 = sb("s1", [128, 64], F32); s2 = sb("s2", [128, 64], F32)
        selb = sb("selb", [128, 64], BF16)
        selT = sb("selT", [64, 512], BF16)
        et = [sb(f"et{i}", [128, 512], BF16) for i in range(2)]
        maskS = sb("maskS", [128, 32, 512], BF16)
        sadd = [sb(f"sadd{i}", [128, 512], F32) for i in range(2)]
        pT = [sb(f"pT{i}", [128, 512], BF16) for i in range(3)]
        og = [sb(f"og{i}", [128, 512], F32) for i in range(2)]
        on = [sb(f"on{i}", [128, 128], BF16) for i in range(2)]
        rden = [sb(f"rden{i}", [128, 1], F32) for i in range(4)]
        scl = [sb(f"scl{i}", [128, 1], F32) for i in range(4)]
        P.op('pool', lambda e: e.memset(VS[:, :, 128:136], 1.0), writes=[('VS1',)])
        P.op('pool', lambda e: e.memset(VW[:, :, 128:136], 1.0), writes=[('VW1',)])
        P.op('pool', lambda e: e.memset(VC[:, :, 128:136], 1.0), writes=[('VC1',)])
        for nb in range(2):
            P.dma('pool', VC[:, nb, 136:200], ovl_in[nb * 128:(nb + 1) * 128, :], writes=[('VCo', nb)])
        NQC = int(os.environ.get('NQC', '8'))
        dbt = sb('dbt', [128, 2048], F32)
        P.op('pool', lambda e: e.memset(dbt[:], 0.0), writes=[('dbt',)])
        dstate = {'done': os.environ.get('DBGD', '0') != '1'}
        P.same = os.environ.get('SAME', '0') == '1'

        def finish_branch(hh, Qc, br, first, ncol_den=128, imp=False, imp_first=False):
            ris = []
            for qb4 in range(4):
                ris.append(st['scl'] % 4)
                st['scl'] += 1
            for qb4 in range(4):
                Ops = ps[2 + qb4]; ri = ris[qb4]
                P.op('dve', lambda e, Ops=Ops, ri=ri: e.tensor_scalar(out=rden[ri][:], in0=Ops[:, 128:129], scalar1=1e-30, scalar2=None, op0=ALU.max),
                     reads=[('O', qb4)], writes=[('rden', ri)])
            for qb4 in range(4):
                ri = ris[qb4]
                P.op('dve', lambda e, ri=ri: e.reciprocal(out=rden[ri][:], in_=rden[ri][:]), reads=[('rden', ri)], writes=[('rden', ri)], strict=True)
            for qb4 in range(4):
                ri = ris[qb4]; blk = Qc * 4 + qb4
                P.op('dve', lambda e, ri=ri, blk=blk: e.tensor_tensor(out=scl[ri][:], in0=rden[ri][:], in1=gsg[:, blk, hh * 3 + br:hh * 3 + br + 1], op=ALU.mult),
                     reads=[('rden', ri), ('gsg',)], writes=[('scl', ri)], strict=True)
            for qb4 in range(4):
                Ops = ps[2 + qb4]; ri = ris[qb4]
                if first:
                    P.op('dve', lambda e, Ops=Ops, ri=ri, qb4=qb4: e.tensor_scalar(out=oacc[hh][:, qb4, :], in0=Ops[:, 0:128], scalar1=scl[ri][:, 0:1], scalar2=None, op0=ALU.mult),
                         reads=[('O', qb4), ('scl', ri)], writes=[('oacc', hh, qb4)], strict=True)
                else:
                    P.op('dve', lambda e, Ops=Ops, ri=ri, qb4=qb4: e.scalar_tensor_tensor(out=oacc[hh][:, qb4, :], in0=Ops[:, 0:128], scalar=scl[ri][:, 0:1], in1=oacc[hh][:, qb4, :], op0=ALU.mult, op1=ALU.add),
                         reads=[('O', qb4), ('scl', ri), ('oacc', hh, qb4)], writes=[('oacc', hh, qb4)], strict=True)
            if imp:
                for qb4 in range(4):
                    Ops = ps[2 + qb4]; ri = ris[qb4]
                    if imp_first:
                        P.op('dve', lambda e, Ops=Ops, ri=ri, qb4=qb4: e.tensor_scalar(out=impa[:, qb4, :], in0=Ops[:, 136:200], scalar1=rden[ri][:, 0:1], scalar2=None, op0=ALU.mult),
                             reads=[('O', qb4), ('rden', ri)], writes=[('impa', qb4)], strict=True)
                    else:
                        P.op('dve', lambda e, Ops=Ops, ri=ri, qb4=qb4: e.scalar_tensor_tensor(out=impa[:, qb4, :], in0=Ops[:, 136:200], scalar=rden[ri][:, 0:1], in1=impa[:, qb4, :], op0=ALU.mult, op1=ALU.add),
                             reads=[('O', qb4), ('rden', ri), ('impa', qb4)], writes=[('impa', qb4)], strict=True)

        S3 = [0, 1, 6]

        def group_loads(g):
            P.dma('sp', ksT[:], qk_s[4 * NG + g, :, :], writes=[('ksT',)])
            P.dma('sp', kwT[:], qk_s[5 * NG + g, :, :], writes=[('kwT',)])
            P.dma('sp', VS[:, :, 0:128], v_s[:, g * 128:(g + 1) * 128].rearrange("(blk p) c -> p blk c", p=128), reads=[('VS1',)], writes=[('VS',)])
            P.dma('sp', VW[:, :, 0:128], v_s[:, NG * 128 + g * 128:NG * 128 + (g + 1) * 128].rearrange("(blk p) c -> p blk c", p=128), reads=[('VW1',)], writes=[('VW',)])
            P.dma('sp', kcT[:], kc_s[g, :, :], writes=[('kcT',)])
            P.dma('sp', VC[:, :, 0:128], vc_s[g].rearrange("(nb p) d -> p nb d", p=128), reads=[('VC1',)], writes=[('VC',)])
            P.dma('sp', gtmp[:], v_s[:, 2 * NG * 128 + g * 12:2 * NG * 128 + (g + 1) * 12].rearrange("(blk p) c -> p blk c", p=128), writes=[('gtmp',)])
            P.op('act', lambda e: e.activation(out=gsg[:], in_=gtmp[:], func=AF.Sigmoid), reads=[('gtmp',)], writes=[('gsg',)])
            for hh in range(4):
                P.dma('sp', qT[hh][:], qk_s[g * 4 + hh, :, :], writes=[('qT', hh)])

        def chunk_tables(Qc):
            q0 = Qc * 512
            P.dma('sp', FTt[:], FT_in[q0:q0 + 512, :].rearrange("(b p) j -> p b j", p=128), writes=[('FTt',)])
            P.dma('sp', CTt[:], CT_in[q0:q0 + 512, :].rearrange("(b p) j -> p b j", p=128), writes=[('CTt',)])

        def make_cmp_tile(g, Qc, hh, nb, nbs, pre):
            q0 = Qc * 512
            t = ProjCtx()

            def a():
                si = S3[st['srr'] % 3]; st['srr'] += 1
                t.si = si
                P.op('pe', lambda e: e.matmul(ps[si][:], kcT[:, nb * 128:(nb + 1) * 128], qT[hh][:, q0:q0 + 512], start=True, stop=True),
                     reads=[('kcT',), ('qT', hh)], writes=[('ps', si)])

            def b():
                si = t.si
                ei = st['err'] % 2; st['err'] += 1
                pi = st['prr'] % 3; st['prr'] += 1
                t.pi = pi
                P.op('act', lambda e: e.activation(out=et[ei][:], in_=ps[si][:], func=AF.Exp), reads=[('ps', si)], writes=[('et', ei)])
                P.op('dve', lambda e: e.tensor_tensor(out=pT[pi][:], in0=et[ei][:], in1=cmpM[:, nb, q0:q0 + 512], op=ALU.mult),
                     reads=[('et', ei), ('cmpM', nb, Qc)], writes=[('pT', pi)])

            def c():
                pi = t.pi
                for qb4 in range(4):
                    P.op('pe', lambda e, qb4=qb4: e.matmul(ps[2 + qb4][:, 0:200], pT[pi][:, qb4 * 128:(qb4 + 1) * 128], VC[:, nb, :], start=(nb == 0), stop=(nb == nbs[-1])),
                         reads=[('pT', pi), ('VC',), ('VCo', 0), ('VCo', 1), ('VC1',)], writes=[('O', qb4)])

            def post():
                if nb == nbs[-1]:
                    finish_branch(hh, Qc, 0, True, imp=True, imp_first=(hh == 0))

            t.pre = pre if pre is not None else (lambda: None)
            t.a, t.b, t.c, t.post = a, b, c, post
            return t

        def cmp_tiles(g, Qc):
            out = []
            pre = (lambda: chunk_tables(Qc))
            for hh in range(4):
                nbs = [0] + ([1] if Qc >= 4 else [])
                for nb in nbs:
                    out.append(make_cmp_tile(g, Qc, hh, nb, nbs, pre))
                    pre = None
            return out

        def do_B(Qc):
            for qb4 in range(4):
                P.op('dve', lambda e, qb4=qb4: e.tensor_tensor(out=wk[:], in0=impa[:, qb4, :], in1=FTt[:, qb4, :], op=ALU.max), reads=[('impa', qb4), ('FTt',)], writes=[('wk',)])
                P.op('dve', lambda e, qb4=qb4: e.tensor_tensor(out=wk[:], in0=wk[:], in1=CTt[:, qb4, :], op=ALU.min), reads=[('wk',), ('CTt',)], writes=[('wk',)])
                P.op('dve', lambda e: e.max(out=m8[:], in_=wk[:]), reads=[('wk',)], writes=[('m8',)], strict=True)
                P.op('dve', lambda e: e.match_replace(out=s1[:], in_to_replace=m8[:], in_values=wk[:], imm_value=-3.0e38), reads=[('wk',), ('m8',)], writes=[('s1',)], strict=True)
                P.op('dve', lambda e: e.max(out=m8b[:], in_=s1[:]), reads=[('s1',)], writes=[('m8b',)], strict=True)
                P.op('dve', lambda e: e.tensor_scalar(out=s1[:], in0=wk[:], scalar1=m8b[:, 7:8], scalar2=None, op0=ALU.is_ge), reads=[('wk',), ('m8b',)], writes=[('s1',)], strict=True)
                P.op('dve', lambda e: e.tensor_scalar(out=s2[:], in0=wk[:], scalar1=-5.0e29, scalar2=None, op0=ALU.is_gt), reads=[('wk',)], writes=[('s2',)])
                P.op('dve', lambda e: e.tensor_tensor(out=s1[:], in0=s1[:], in1=s2[:], op=ALU.mult), reads=[('s1',), ('s2',)], writes=[('s1',)])
                P.op('dve', lambda e: e.tensor_scalar(out=selb[:], in0=s1[:], scalar1=-NB, scalar2=NB, op0=ALU.mult, op1=ALU.add), reads=[('s1',)], writes=[('selb',)])
                ti = st['trr'] % 4; st['trr'] += 1
                P.op('pe', lambda e, ti=ti: e.transpose(pst[0:64, ti * 128:(ti + 1) * 128], selb[:], ident[:]), reads=[('selb',), ('ident',)], writes=[('pst', ti)])
                P.op('dve', lambda e, ti=ti, qb4=qb4: e.tensor_copy(out=selT[:, qb4 * 128:(qb4 + 1) * 128], in_=pst[0:64, ti * 128:(ti + 1) * 128]), reads=[('pst', ti)], writes=[('selT',)])
            for kb in range(0, 4 * Qc + 4):
                j = kb - 4 * Qc
                qlo = max(0, j) * 128
                si = S3[st['srr'] % 3]; st['srr'] += 1
                P.op('pe', lambda e, kb=kb, si=si, j=j: e.matmul(ps[si][:, 0:512], ExpT[:, kb, :], selT[:, 0:512], start=True, stop=(j < 0)),
                     reads=[('ExpT',), ('selT',)], writes=[('ps', si)])
                if j >= 0:
                    P.op('pe', lambda e, si=si, qlo=qlo: e.matmul(ps[si][:, qlo:qlo + 128], ident[:], trib[:], start=False, stop=True),
                         reads=[('ident',), ('trib',)], writes=[('ps', si)])
                if True:
                    P.op('dve', lambda e, kb=kb, si=si: e.tensor_copy(out=maskS[:, kb, :], in_=ps[si][:, 0:512]), reads=[('ps', si)], writes=[('maskS', kb)])
                else:
                    P.op('dve', lambda e, kb=kb, si=si: e.tensor_scalar(out=maskS[:, kb, :], in0=ps[si][:, 0:512], scalar1=-1.0, scalar2=None, op0=ALU.is_gt), reads=[('ps', si)], writes=[('maskS', kb)])

        def make_sw_tile(g, Qc, hh, br, kb, kb_lo, kb_hi):
            q0 = Qc * 512
            KT, VV, kkey, vkey, v1 = (ksT, VS, ('ksT',), ('VS',), ('VS1',)) if br == 1 else (kwT, VW, ('kwT',), ('VW',), ('VW1',))
            j = kb - 4 * Qc
            qlo = max(0, j) * 128
            t = ProjCtx()

            def a():
                si = S3[st['srr'] % 3]; st['srr'] += 1
                t.si = si
                P.op('pe', lambda e: e.matmul(ps[si][:, qlo:512], KT[:, kb * 128:(kb + 1) * 128], qT[hh][:, q0 + qlo:q0 + 512], start=True, stop=True),
                     reads=[kkey, ('qT', hh)], writes=[('ps', si)])

            def b():
                si = t.si
                pi = st['prr'] % 3; st['prr'] += 1
                t.pi = pi
                ai = st['err'] % 2; st['err'] += 1
                if True:
                    if br == 1:
                        mk_ap, mk_key = maskS[:, kb, qlo:512], ('maskS', kb)
                    else:
                        mk_ap, mk_key = WB[:, 4 * Qc - kb + 3, qlo:512], ('WB',)
                    P.op('dve', lambda e: e.tensor_tensor(out=sadd[ai][:, qlo:512], in0=ps[si][:, qlo:512], in1=mk_ap, op=ALU.add),
                         reads=[('ps', si), mk_key], writes=[('sadd', ai)])
                    P.op('act', lambda e: e.activation(out=pT[pi][:, qlo:512], in_=sadd[ai][:, qlo:512], func=AF.Exp), reads=[('sadd', ai)], writes=[('pT', pi)])
                else:
                    if br == 1:
                        mk_ap, mk_key = maskS[:, kb, qlo:512], ('maskS', kb)
                    else:
                        mk_ap, mk_key = WBm[:, 4 * Qc - kb + 3, qlo:512], ('WBm',)
                    P.op('act', lambda e: e.activation(out=et[ai][:, qlo:512], in_=ps[si][:, qlo:512], func=AF.Exp), reads=[('ps', si)], writes=[('et', ai)])
                    P.op('pool', lambda e: e.tensor_tensor(out=pT[pi][:, qlo:512], in0=et[ai][:, qlo:512], in1=mk_ap, op=ALU.mult),
                         reads=[('et', ai), mk_key], writes=[('pT', pi)])

            def c():
                pi = t.pi
                for qb4 in range(max(0, j), 4):
                    last_kb = 4 * Qc + qb4
                    P.op('pe', lambda e, qb4=qb4, last_kb=last_kb: e.matmul(ps[2 + qb4][:, 0:130], pT[pi][:, qb4 * 128:(qb4 + 1) * 128], VV[:, kb, 0:130], start=(kb == kb_lo), stop=(kb == last_kb)),
                         reads=[('pT', pi), vkey, v1], writes=[('O', qb4)])

            def post():
                if kb != kb_hi:
                    return
                finish_branch(hh, Qc, br, False)
                if br != 2:
                    return
                ogi = st['ogrr'] % 2; st['ogrr'] += 1
                for qb4 in range(4):
                    oi = st['onrr'] % 2; st['onrr'] += 1
                    P.op('dve', lambda e, oi=oi, qb4=qb4: e.tensor_copy(out=on[oi][:], in_=oacc[hh][:, qb4, :]), reads=[('oacc', hh, qb4)], writes=[('on', oi)])
                    ti = st['trr'] % 4; st['trr'] += 1
                    P.op('pe', lambda e, oi=oi, ti=ti: e.transpose(pst[:, ti * 128:(ti + 1) * 128], on[oi][:], ident[:]), reads=[('on', oi), ('ident',)], writes=[('pst', ti)])
                    P.op('dve', lambda e, ti=ti, qb4=qb4: e.tensor_copy(out=og[ogi][:, qb4 * 128:(qb4 + 1) * 128], in_=pst[:, ti * 128:(ti + 1) * 128]), reads=[('pst', ti)], writes=[('og', ogi)])
                orow = (g * 4 + hh) * 128
                P.dma('sp', oT[orow:orow + 128, q0:q0 + 512], og[ogi][:], reads=[('og', ogi)], writes=[('oT', g, hh, Qc)])

            t.pre = (lambda: None)
            t.a, t.b, t.c, t.post = a, b, c, post
            return t

        def sw_tiles(g, Qc):
            out = []
            for hh in range(4):
                for br in (1, 2):
                    kb_lo = 0 if br == 1 else max(0, 4 * Qc - 4)
                    kb_hi = 4 * Qc + 3
                    for kb in range(kb_lo, kb_hi + 1):
                        out.append(make_sw_tile(g, Qc, hh, br, kb, kb_lo, kb_hi))
            return out

        for g in range(NG):
            group_loads(g)
            run_pipeline(cmp_tiles(g, 0), depth=2)
            for Qc in range(NQC):
                do_B(Qc)
                tasks = sw_tiles(g, Qc) + (cmp_tiles(g, Qc + 1) if Qc + 1 < NQC else [])
                run_pipeline(tasks, depth=2)
        P.emit_stage()


def host_consts1():
    c0 = host_consts()
    out = dict(ropeC=c0['ropeC'], ropeS=c0['ropeS'], Pm=c0['Pm'], ident=c0['ident'])
    inv = 1.0 / (500000.0 ** (np.arange(0, 32, 2, dtype=np.float32) / 32))
    posc = (np.arange(256) * 16 + 31).astype(np.float32)
    ang = posc[None, :] * inv[:, None]
    cos = np.cos(ang).astype(np.float32); sin = np.sin(ang).astype(np.float32)
    out['ropeCc'] = np.concatenate([cos, cos], 0); out['ropeSc'] = np.concatenate([-sin, sin], 0)
    n = np.arange(256)
    start = n * 16
    js = np.arange(64) * 64
    ov = ((start[:, None] < js[None, :] + 64) & (start[:, None] + 32 > js[None, :])).astype(np.float32)
    ov[255] = 0
    out['ovl'] = ov
    q = np.arange(S)
    cm = ((16 * n + 31)[:, None] <= q[None, :]).astype(np.float32)
    cm[255] = 0
    out['cmpM'] = cm
    cur = (q // 64)[:, None]
    jj = np.arange(64)[None, :]
    forced = (jj == 0) | (jj == cur) | (jj == cur - 1)
    out['FT'] = np.where(forced, 1e9, 0.0).astype(np.float32)
    out['CT'] = np.where(jj <= cur, 3.0e38, -1e30).astype(np.float32)
    E = np.zeros((64, 32, 128), np.float32)
    for kb in range(32):
        for p in range(128):
            E[2 * kb + p // 64, kb, p] = 1.0
    out['ExpT'] = E.reshape(64, 32 * 128)
    WBt = np.zeros((128, 8, 512), np.float32)
    for oi in range(8):
        o = 128 * (oi - 3)
        d = o + np.arange(512)[None, :] - np.arange(128)[:, None]
        WBt[:, oi, :] = np.where((d >= 0) & (d < 512), 0.0, NB)
    out['WB'] = WBt.reshape(128, 8 * 512)
    p = np.arange(128)[:, None]; c = np.arange(128)[None, :]
    out['trib'] = np.where(p > c, NB, 0.0).astype(np.float32)
    return out


def host_inputs_mix1(inp, xT_b, half):
    w = inp['odd_w_in'][0]
    q = w[:, 0:2048][:, half * 1024:(half + 1) * 1024]
    def grp(i):
        blk = w[:, 2048 + i * 512:2048 + (i + 1) * 512]
        return blk[:, half * 256:(half + 1) * 256]
    kc, vc, ks, vs, kw, vw = [grp(i) for i in range(6)]
    gt = w[:, 2048 + 3072:2048 + 3072 + 48][:, half * 24:(half + 1) * 24]
    gpad = np.zeros((D, 256), np.float32); gpad[:, :24] = gt
    m = dict(
        xT=xT_b,
        g=np.ascontiguousarray(inp['ln_mix_g'][1].reshape(16, 128).T),
        Wfm=np.ascontiguousarray(np.concatenate([q, ks, kw, kc, vc], 1)),
        Wtm=np.ascontiguousarray(np.concatenate([vs, vw, gpad], 1)),
        gains=np.ascontiguousarray(np.stack([inp['odd_g_q'][0], inp['odd_g_ks'][0], inp['odd_g_kw'][0], inp['odd_g_kc'][0]], 1)),
        w1k=inp['odd_phi_k_w1'][0], w2k=inp['odd_phi_k_w2'][0], peTk=np.ascontiguousarray(inp['odd_phi_k_pe'][0].T),
        w1v=inp['odd_phi_v_w1'][0], w2v=inp['odd_phi_v_w2'][0], peTv=np.ascontiguousarray(inp['odd_phi_v_pe'][0].T),
    )
    m.update(host_consts1())
    return m


F = 8192


def emit_phaseC(nc, P, tag, xT, oT, w_out, g_in, w_up, w_down, hT, T, TT=512):
    NT = T // TT
    with ExitStack() as es:
        def sb(name, shape, dt):
            return es.enter_context(nc.sbuf_tensor(tag + name, shape, dt))
        wb = [sb(f"wb{i}", [128, 8192], BF16) for i in range(3)]
        ot = sb("ot", [128, 16, TT], BF16)
        ht = sb("ht", [128, 16, TT], F32)
        hn = sb("hn", [128, 16, TT], BF16)
        ut = sb("ut", [128, 32, TT], BF16)
        sq = [sb(f"sq{i}", [128, TT], BF16) for i in range(2)]
        rt = [sb(f"rt{i}", [128, TT], F32) for i in range(2)]
        rstd = sb("rstd", [128, TT], F32)
        ones = sb("ones", [128, 128], BF16)
        gt = sb("gt", [128, 16], F32)
        ps = [es.enter_context(nc.psum_tensor(tag + f"ps{i}", [128, 512], F32)) for i in range(8)]

        P.op('pool', lambda e: e.memset(ones[:], 1.0), writes=[('ones',)])
        P.dma('sp', gt[:], g_in[:, :], writes=[('g',)])

        jobs = []
        state = {'psrr': 0, 'sqrr': 0, 'rtrr': 0}

        def nextps():
            i = state['psrr'] % 7
            state['psrr'] += 1
            return i

        for t in range(NT):
            tsl = slice(t * TT, (t + 1) * TT)

            def tile_begin(t=t, tsl=tsl):
                P.dma('pool', ot[:], oT[:, tsl].rearrange("(kc p) n -> p kc n", p=128),
                      writes=[('ot',)])
                P.dma('sp', ht[:], xT[:, tsl].rearrange("(kc p) n -> p kc n", p=128),
                      writes=[('ht', dc) for dc in range(16)])

            for blk in range(4):
                def load(bi, blk=blk):
                    v = wb[bi][:].rearrange("p (kc c) -> p kc c", kc=16)
                    P.dma('pool', v, w_out[:, blk * 512:(blk + 1) * 512].rearrange("(kc p) c -> p kc c", p=128),
                          writes=[('wb', bi)])

                def comp(bi, blk=blk, t=t, first=(blk == 0), tb=tile_begin):
                    if first:
                        tb()
                    v = wb[bi][:].rearrange("p (kc c) -> p kc c", kc=16)
                    for dcl in range(4):
                        dc = blk * 4 + dcl
                        pi = nextps()
                        for kc in range(16):
                            P.op('pe', lambda e, kc=kc, pi=pi, dcl=dcl: e.matmul(ps[pi][:], v[:, kc, dcl * 128:(dcl + 1) * 128], ot[:, kc, :], start=(kc == 0), stop=(kc == 15)),
                                 reads=[('wb', bi), ('ot',)], writes=[('ps', pi)])
                        P.op('dve', lambda e, pi=pi, dc=dc: e.tensor_tensor(out=ht[:, dc, :], in0=ps[pi][:], in1=ht[:, dc, :], op=ALU.add),
                             reads=[('ps', pi), ('ht', dc)], writes=[('ht', dc)])
                jobs.append((load, comp))

            def norm():
                pn = 7
                for dc in range(16):
                    si = state['sqrr'] % 2
                    state['sqrr'] += 1
                    P.op('act', lambda e, dc=dc, si=si: e.activation(out=sq[si][:], in_=ht[:, dc, :], func=AF.Square),
                         reads=[('ht', dc)], writes=[('sq', si)])
                    P.op('pe', lambda e, dc=dc, si=si: e.matmul(ps[pn][:], ones[:], sq[si][:], start=(dc == 0), stop=(dc == 15)),
                         reads=[('sq', si), ('ones',)], writes=[('ps', pn)])
                P.op('act', lambda e: e.activation(out=rstd[:], in_=ps[pn][:], func=AF.Sqrt, scale=1.0 / D, bias=EPS),
                     reads=[('ps', pn)], writes=[('rstd',)])
                P.op('dve', lambda e: e.reciprocal(out=rstd[:], in_=rstd[:]), reads=[('rstd',)], writes=[('rstd',)])
                for dc in range(16):
                    P.op('dve', lambda e, dc=dc: e.scalar_tensor_tensor(out=hn[:, dc, :], in0=ht[:, dc, :], scalar=gt[:, dc:dc + 1], in1=rstd[:], op0=ALU.mult, op1=ALU.mult),
                         reads=[('ht', dc), ('g',), ('rstd',)], writes=[('hn', dc)])

            for hf in range(2):
                for blk in range(8):
                    c0 = hf * 4096 + blk * 512

                    def load(bi, c0=c0):
                        v = wb[bi][:].rearrange("p (kc c) -> p kc c", kc=16)
                        P.dma('pool', v, w_up[:, c0:c0 + 512].rearrange("(kc p) c -> p kc c", p=128), writes=[('wb', bi)])

                    def comp(bi, blk=blk, hf=hf, donorm=(hf == 0 and blk == 0), nf=norm):
                        if donorm:
                            nf()
                        v = wb[bi][:].rearrange("p (kc c) -> p kc c", kc=16)
                        for fl in range(4):
                            fc = blk * 4 + fl
                            pi = nextps()
                            for kc in range(16):
                                P.op('pe', lambda e, kc=kc, pi=pi, fl=fl: e.matmul(ps[pi][:], v[:, kc, fl * 128:(fl + 1) * 128], hn[:, kc, :], start=(kc == 0), stop=(kc == 15)),
                                     reads=[('wb', bi), ('hn', kc)], writes=[('ps', pi)])
                            ri = state['rtrr'] % 2
                            state['rtrr'] += 1
                            P.op('act', lambda e, pi=pi, ri=ri: e.activation(out=rt[ri][:], in_=ps[pi][:], func=AF.Relu),
                                 reads=[('ps', pi)], writes=[('rt', ri)])
                            P.op('dve', lambda e, ri=ri, fc=fc: e.tensor_tensor(out=ut[:, fc, :], in0=rt[ri][:], in1=rt[ri][:], op=ALU.mult),
                                 reads=[('rt', ri)], writes=[('ut', fc)])
                    jobs.append((load, comp))
                for blk in range(8):
                    r0 = hf * 4096

                    def load(bi, blk=blk, r0=r0):
                        v = wb[bi][:].rearrange("p (fc c) -> p fc c", fc=32)
                        P.dma('pool', v, w_down[r0:r0 + 4096, blk * 256:(blk + 1) * 256].rearrange("(fc p) c -> p fc c", p=128), writes=[('wb', bi)])

                    def comp(bi, blk=blk, hf=hf, t=t, tsl=tsl):
                        v = wb[bi][:].rearrange("p (fc c) -> p fc c", fc=32)
                        for dcl in range(2):
                            dc = blk * 2 + dcl
                            pi = nextps()
                            for fc in range(32):
                                P.op('pe', lambda e, fc=fc, pi=pi, dcl=dcl: e.matmul(ps[pi][:], v[:, fc, dcl * 128:(dcl + 1) * 128], ut[:, fc, :], start=(fc == 0), stop=(fc == 31)),
                                     reads=[('wb', bi), ('ut', fc)], writes=[('ps', pi)])
                            P.op('dve', lambda e, pi=pi, dc=dc: e.tensor_tensor(out=ht[:, dc, :], in0=ps[pi][:], in1=ht[:, dc, :], op=ALU.add),
                                 reads=[('ps', pi), ('ht', dc)], writes=[('ht', dc)])
                        if hf == 1 and blk == 7:
                            P.dma('sp', hT[:, tsl].rearrange("(kc p) n -> p kc n", p=128), ht[:],
                                  reads=[('ht', dc) for dc in range(16)], writes=[('hTout', t)])
                    jobs.append((load, comp))

        nj = len(jobs)
        for i in range(min(2, nj)):
            jobs[i][0](i % 3)
        for i in range(nj):
            if i + 2 < nj:
                jobs[i + 2][0]((i + 2) % 3)
            jobs[i][1](i % 3)
        P.emit_stage()


IN_SPECS = [
    ("xT", [D, S]), ("g0", [128, 16]), ("Wfm0", [D, 5120]), ("Wtm0", [D, 2048]), ("Wf0", [D, 8]), ("bf0", [8, 1]),
    ("gains0", [128, 4]), ("ropeC", [32, S]), ("ropeS", [32, S]), ("Pm", [32, 32]), ("trim", [128, 128]),
    ("ident", [128, 128]), ("Mt", [128, 20 * 512]),
    ("w_out0", [D, D]), ("gm0", [128, 16]), ("w_up0", [D, F]), ("w_down0", [F, D]),
    ("g1", [128, 16]), ("Wfm1", [D, 4096]), ("Wtm1", [D, 1280]), ("gains1", [128, 4]),
    ("ropeCc", [32, 256]), ("ropeSc", [32, 256]),
    ("w1k", [4096, 128]), ("w2k", [128, 128]), ("peTk", [128, 32]), ("w1v", [4096, 128]), ("w2v", [128, 128]), ("peTv", [128, 32]),
    ("ovl", [256, 64]), ("cmpM", [256, S]), ("FT", [S, 64]), ("CT", [S, 64]), ("ExpT", [64, 32 * 128]), ("WB", [128, 8 * 512]),
    ("trib", [128, 128]),
    ("w_out1", [D, D]), ("gm1", [128, 16]), ("w_up1", [D, F]), ("w_down1", [F, D]),
]


def build_fused():
    nc = bass.Bass("TRN2", target_bir_lowering=False)
    T_ = {}
    for name, shape in IN_SPECS:
        T_[name] = nc.dram_tensor(name, shape, F32, kind="ExternalInput").ap()
    T_['o0T'] = nc.dram_tensor("o0T", [D, S], F32, kind="Internal").ap()
    T_['h2T'] = nc.dram_tensor("h2T", [D, S], F32, kind="Internal").ap()
    T_['o1T'] = nc.dram_tensor("o1T", [D, S], F32, kind="Internal").ap()
    hT = nc.dram_tensor("hT", [D, S], F32, kind="ExternalOutput").ap()
    P = Prog(nc)
    emit_mix0(nc, P, T_, NH=8)
    emit_phaseC(nc, P, "c0_", T_['xT'], T_['o0T'], T_['w_out0'], T_['gm0'], T_['w_up0'], T_['w_down0'], T_['h2T'], S)
    emit_mix1(nc, P, T_, NG=4)
    emit_phaseC(nc, P, "c1_", T_['h2T'], T_['o1T'], T_['w_out1'], T_['gm1'], T_['w_up1'], T_['w_down1'], hT, S)
    return nc


def host_shared_inputs(inp):
    def gl(v):
        return np.ascontiguousarray(v.reshape(16, 128).T)
    w = inp['even_w_in'][0]
    m = dict(
        g0=gl(inp['ln_mix_g'][0]),
        Wfm0=np.ascontiguousarray(np.concatenate([w[:, 0:1024], w[:, 1024:2048], w[:, 3072:4096], w[:, 4104:5128], w[:, 5128:6152]], 1)),
        Wtm0=np.ascontiguousarray(np.concatenate([w[:, 2048:3072], w[:, 6152:7176]], 1)),
        Wf0=np.ascontiguousarray(w[:, 4096:4104]),
        bf0=np.ascontiguousarray(inp['even_b_f'][0].reshape(8, 1)),
        gains0=np.ascontiguousarray(np.stack([inp['even_g_q_fox'][0], inp['even_g_k_fox'][0], inp['even_g_q_dil'][0], inp['even_g_k_dil'][0]], 1)),
        w_out0=inp['even_w_out'][0], gm0=gl(inp['ln_mlp_g'][0]), w_up0=inp['w_mlp_up'][0], w_down0=inp['w_mlp_down'][0],
    )
    w = inp['odd_w_in'][0]
    def grp(i):
        return w[:, 2048 + i * 512:2048 + (i + 1) * 512]
    kc, vc, ks, vs, kw, vw = [grp(i) for i in range(6)]
    gpad = np.zeros((D, 256), np.float32); gpad[:, :48] = w[:, 5120:5168]
    m.update(
        g1=gl(inp['ln_mix_g'][1]),
        Wfm1=np.ascontiguousarray(np.concatenate([w[:, 0:2048], ks, kw, kc, vc], 1)),
        Wtm1=np.ascontiguousarray(np.concatenate([vs, vw, gpad], 1)),
        gains1=np.ascontiguousarray(np.stack([inp['odd_g_q'][0], inp['odd_g_ks'][0], inp['odd_g_kw'][0], inp['odd_g_kc'][0]], 1)),
        w1k=inp['odd_phi_k_w1'][0], w2k=inp['odd_phi_k_w2'][0], peTk=np.ascontiguousarray(inp['odd_phi_k_pe'][0].T),
        w1v=inp['odd_phi_v_w1'][0], w2v=inp['odd_phi_v_w2'][0], peTv=np.ascontiguousarray(inp['odd_phi_v_pe'][0].T),
        w_out1=inp['odd_w_out'][0], gm1=gl(inp['ln_mlp_g'][1]), w_up1=inp['w_mlp_up'][1], w_down1=inp['w_mlp_down'][1],
    )
    m.update(host_consts())
    m.update(host_consts1())
    return m


def kernel(**inp):
    inp = {k: np.asarray(v) for k, v in inp.items()}
    x = inp['x']
    B = x.shape[0]
    cores = list(range(8))
    shared = host_shared_inputs(inp)
    nc = build_fused()
    maps = []
    for c in cores:
        m = dict(shared)
        m['xT'] = np.ascontiguousarray(x[c // 2].T)
        maps.append({name: np.ascontiguousarray(m[name], dtype=np.float32) for name, _ in IN_SPECS})
    res = run_bass_kernel_spmd(nc, maps, core_ids=cores).results
    return np.stack([np.ascontiguousarray(np.asarray(res[2 * b]["hT"]).T) for b in range(B)], axis=0).astype(np.float32)
```

```python
import numpy as np, time, sys, os
from contextlib import ExitStack
from concourse.bass_utils import run_bass_kernel_spmd
import numpy as np
import concourse.bass as bass
import concourse.mybir as mybir

F32 = mybir.dt.float32
BF16 = mybir.dt.bfloat16
AF = mybir.ActivationFunctionType
ALU = mybir.AluOpType
AX = mybir.AxisListType

ENGS = ['pe', 'act', 'dve', 'pool', 'sp']


class _Op:
    __slots__ = ('eng', 'fn', 'deps', 'dma', 'needed', 'sem', 'val', 'strict')


class Prog:
    def __init__(self, nc, n_slots=8, same_engine_sync=False):
        self.nc = nc
        self.same = same_engine_sync
        self.n_slots = n_slots
        self.sems = {}
        self.cnt = {e: 0 for e in ENGS}
        self.dma_cum = {}
        self.dma_last = {}
        self.dma_rr = {e: 0 for e in ENGS}
        self.waited = {e: {} for e in ENGS}
        self._stack = []
        for e in ['pe', 'act', 'dve', 'pool']:
            self.sems[e] = nc.alloc_semaphore(name=f"s_{e}")
        for q in ['sp', 'pool', 'act']:
            for s in range(n_slots):
                k = (q, s)
                self.sems[k] = nc.alloc_semaphore(name=f"d_{q}{s}")
                self.dma_cum[k] = 0
                self.dma_last[k] = None
        self.reset_stage()

    def reset_stage(self):
        self.ops = []
        self.last_w = {}
        self.readers = {}

    def _eng(self, e):
        nc = self.nc
        return {'pe': nc.tensor, 'act': nc.scalar, 'dve': nc.vector, 'pool': nc.gpsimd, 'sp': nc.sync}[e]

    def op(self, eng, fn, reads=(), writes=(), dma=False, strict=False):
        o = _Op()
        o.eng = eng; o.fn = fn; o.dma = dma; o.needed = False; o.sem = None; o.val = None; o.strict = strict
        deps = []
        for k in reads:
            w = self.last_w.get(k)
            if w is not None:
                deps.append(w)
        for k in writes:
            w = self.last_w.get(k)
            if w is not None:
                deps.append(w)
            deps.extend(self.readers.get(k, ()))
        if dma:
            slot = (eng, self.dma_rr[eng] % self.n_slots)
            self.dma_rr[eng] += 1
            prev = self.dma_last[slot]
            if prev is not None:
                deps.append(prev)
            self.dma_last[slot] = o
            self.dma_cum[slot] += 16
            o.sem = slot
            o.val = self.dma_cum[slot]
        o.deps = deps
        for k in reads:
            self.readers.setdefault(k, []).append(o)
        for k in writes:
            self.last_w[k] = o
            self.readers[k] = []
        self.ops.append(o)
        return o

    def dma(self, q, out, in_, reads=(), writes=(), **kw):
        return self.op(q, lambda e: e.dma_start(out=out, in_=in_, **kw), reads=reads, writes=writes, dma=True)

    def emit_stage(self, block_name=None, final_wait_all_dma=True):
        nc = self.nc
        ops = self.ops
        for o in ops:
            for d in o.deps:
                if d.dma:
                    continue
                if d.eng == o.eng and not o.dma and not self.same and not o.strict:
                    continue
                d.needed = True
        for o in ops:
            if not o.dma and o.needed:
                self.cnt[o.eng] += 1
                o.sem = o.eng
                o.val = self.cnt[o.eng]
        per = {e: [] for e in ENGS}
        for o in ops:
            per[o.eng].append(o)
        sems = self.sems
        waited = self.waited
        same = self.same
        dma_cum = self.dma_cum

        def emit_engine(ename, eng):
            wd = waited[ename]
            for o in per[ename]:
                need = {}
                for d in o.deps:
                    if d.val is None:
                        continue
                    if (not d.dma) and d.eng == ename and (not o.dma) and (not same) and (not o.strict):
                        continue
                    s = d.sem
                    if need.get(s, 0) < d.val:
                        need[s] = d.val
                for s, v in need.items():
                    if wd.get(s, 0) >= v:
                        continue
                    eng.wait_ge(sems[s], v)
                    wd[s] = v
                ins = o.fn(eng)
                if o.dma:
                    ins.then_inc(sems[o.sem], 16)
                elif o.needed:
                    ins.then_inc(sems[o.sem], 1)
            if final_wait_all_dma:
                for s, v in dma_cum.items():
                    if s[0] == ename and v > 0 and wd.get(s, 0) < v:
                        eng.wait_ge(sems[s], v)
                        wd[s] = v

        with nc.Block() as block:
            @block.tensor
            def _(e):
                emit_engine('pe', e)

            @block.scalar
            def _(e):
                emit_engine('act', e)

            @block.vector
            def _(e):
                emit_engine('dve', e)

            @block.gpsimd
            def _(e):
                emit_engine('pool', e)

            @block.sync
            def _(e):
                emit_engine('sp', e)
        for e in ENGS:
            for s in self.sems:
                if isinstance(s, str):
                    self.waited[e][s] = self.cnt[s]
                else:
                    self.waited[e][s] = self.dma_cum[s]
        self.reset_stage()

from contextlib import ExitStack

D = 2048; EPS = 1e-6; S = 4096
SCALE = 128 ** -0.5
NEGM = -1e30


def norm_stage(nc, P, es, sb, ps, xT, g_in, hnT, ones):
    NTK = 128
    xts = [sb(f"xt{i}", [128, 16, NTK], F32) for i in range(2)]
    gt = sb("gt", [128, 16], F32)
    sq = [sb(f"nsq{i}", [128, NTK], BF16) for i in range(2)]
    rstd = sb("nrstd", [128, NTK], F32)
    P.dma('sp', gt[:], g_in[:, :], writes=[('g',)])
    pns = [6, 7]

    def tile(tt):
        xt = xts[tt % 2]
        pn = pns[tt % 2]
        tsl = slice(tt * NTK, (tt + 1) * NTK)
        xk = ('xt', tt % 2)
        for dc in range(16):
            si = dc % 2
            P.op('act', lambda e, dc=dc, si=si: e.activation(out=sq[si][:], in_=xt[:, dc, :], func=AF.Square),
                 reads=[xk], writes=[('nsq', si)])
            P.op('pe', lambda e, dc=dc, si=si: e.matmul(ps[pn][:, 0:NTK], ones[:], sq[si][:], start=(dc == 0), stop=(dc == 15)),
                 reads=[('nsq', si), ('ones',)], writes=[('ps', pn)])
        P.op('act', lambda e: e.activation(out=rstd[:], in_=ps[pn][:, 0:NTK], func=AF.Sqrt, scale=1.0 / D, bias=EPS),
             reads=[('ps', pn)], writes=[('nrstd',)])
        P.op('dve', lambda e: e.reciprocal(out=rstd[:], in_=rstd[:]), reads=[('nrstd',)], writes=[('nrstd',)])
        for dc in range(16):
            P.op('dve', lambda e, dc=dc: e.scalar_tensor_tensor(out=hnT[:, dc, tsl], in0=xt[:, dc, :], scalar=gt[:, dc:dc + 1], in1=rstd[:], op0=ALU.mult, op1=ALU.mult),
                 reads=[xk, ('g',), ('nrstd',)], writes=[('hnT', tt // 4)])

    def load(tt):
        tsl = slice(tt * NTK, (tt + 1) * NTK)
        P.dma('sp', xts[tt % 2][:], xT[:, tsl].rearrange("(kc p) n -> p kc n", p=128), writes=[('xt', tt % 2)])

    n = S // NTK
    load(0)
    for tt in range(n):
        if tt + 1 < n:
            load(tt + 1)
        tile(tt)


class ProjCtx:
    pass


def run_pipeline(tasks, depth=2):
    n = len(tasks)
    for i in range(min(depth, n)):
        tasks[i].pre()
        tasks[i].a()
    for t in range(n):
        if t + depth < n:
            tasks[t + depth].pre()
            tasks[t + depth].a()
        tasks[t].b()
        tasks[t].c()
        tasks[t].post()


def proj_fm(nc, P, C, wsrc, ncols, specs, dst):
    nblk = ncols // 256
    state = C.state

    def load(blk):
        bi = blk % 2
        P.dma('pool', C.wbl[bi][:], wsrc[:, blk * 256:(blk + 1) * 256].rearrange("(kc p) c -> p kc c", p=128), writes=[('wbl', bi)])

    def make_tile(blk, sub, tt, idx):
        bi = blk % 2
        j = blk * 2 + sub
        sp = specs[j]
        tsl = slice(tt * 512, (tt + 1) * 512)
        pi = idx % 4
        nbk = 4 + (idx % 2)
        rbk = 6 + (idx % 2)
        qi = idx % 3
        qn = C.qn[qi]
        si = idx % 2
        ri = idx % 2
        t = ProjCtx()

        def a():
            if sub == 0 and tt == 0:
                if blk == 0:
                    load(0)
                if blk + 1 < nblk:
                    load(blk + 1)
            for kc in range(16):
                P.op('pe', lambda e, kc=kc: e.matmul(C.ps[pi][:], C.wbl[bi][:, kc, sub * 128:(sub + 1) * 128], C.hnT[:, kc, tsl], start=(kc == 0), stop=(kc == 15)),
                     reads=[('wbl', bi), ('hnT', tt)], writes=[('ps', pi)])

        def b():
            if sp['kind'] == 'raw':
                P.op('act', lambda e: e.activation(out=qn[:], in_=C.ps[pi][:], func=AF.Copy), reads=[('ps', pi)], writes=[('qn', qi)])
                return
            P.op('act', lambda e: e.activation(out=C.sq[si][:], in_=C.ps[pi][:], func=AF.Square), reads=[('ps', pi)], writes=[('sq', si)])
            P.op('pe', lambda e: e.matmul(C.ps[nbk][:], C.ones[:], C.sq[si][:], start=True, stop=True), reads=[('sq', si), ('ones',)], writes=[('ps', nbk)])
            P.op('act', lambda e: e.activation(out=C.rs[ri][:], in_=C.ps[nbk][:], func=AF.Ln, scale=1.0 / 128, bias=EPS), reads=[('ps', nbk)], writes=[('rs', ri)])
            P.op('act', lambda e: e.activation(out=C.rs[ri][:], in_=C.rs[ri][:], func=AF.Exp, scale=-0.5), reads=[('rs', ri)], writes=[('rs', ri)])
            gc = sp['gain']
            P.op('dve', lambda e: e.scalar_tensor_tensor(out=qn[:], in0=C.ps[pi][:], scalar=C.gains[:, gc:gc + 1], in1=C.rs[ri][:], op0=ALU.mult, op1=ALU.mult),
                 reads=[('ps', pi), ('rs', ri), ('gains',)], writes=[('qn', qi)])

        def c():
            if sp['kind'] != 'raw' and sp.get('rope'):
                ct, st = sp['tabs']
                tofs = sp.get('tofs', 0)
                tl = slice(tofs + tt * 512, tofs + (tt + 1) * 512)
                P.op('pe', lambda e: e.matmul(C.ps[rbk][0:32, :], C.Pm[:], qn[0:32, :], start=True, stop=True), reads=[('qn', qi), ('Pm',)], writes=[('ps', rbk)])
                P.op('dve', lambda e: e.tensor_tensor(out=C.r1[:], in0=C.ps[rbk][0:32, :], in1=st[:, tl], op=ALU.mult), reads=[('ps', rbk), ('tabs',)], writes=[('r1',)])
                P.op('dve', lambda e: e.tensor_tensor(out=C.r2[:], in0=qn[0:32, :], in1=ct[:, tl], op=ALU.mult), reads=[('qn', qi), ('tabs',)], writes=[('r2',)])
                P.op('dve', lambda e: e.tensor_tensor(out=qn[0:32, :], in0=C.r1[:], in1=C.r2[:], op=ALU.add), reads=[('r1',), ('r2',)], writes=[('qn', qi)])
            P.dma('sp', dst(j, tt), qn[:], reads=[('qn', qi)], writes=[('dst_fm', id(dst), j, tt)])

        t.a, t.b, t.c = a, b, c
        return t

    tiles = []
    for blk in range(nblk):
        for sub in range(2):
            for tt in range(8):
                tiles.append(make_tile(blk, sub, tt, len(tiles)))
    n = len(tiles)
    tiles[0].a()
    for t in range(n):
        if t + 1 < n:
            tiles[t + 1].a()
        tiles[t].b()
        if t >= 1:
            tiles[t - 1].c()
    tiles[n - 1].c()


def proj_tm(nc, P, C, wsrc, ncols, dst):
    nblk = ncols // 256
    state = C.state
    for blk in range(nblk):
        bi = state['wrr'] % 2
        state['wrr'] += 1
        P.dma('pool', C.wbl[bi][:], wsrc[:, blk * 256:(blk + 1) * 256].rearrange("(kc p) c -> p kc c", p=128), writes=[('wbl', bi)])
        for tb4 in range(8):
            vi = state['vrr'] % 2
            state['vrr'] += 1
            for i in range(4):
                tb = tb4 * 4 + i
                pi = state['psrr'] % 4
                state['psrr'] += 1
                for kc in range(16):
                    P.op('pe', lambda e, kc=kc, pi=pi, bi=bi, tb=tb: e.matmul(C.ps[pi][:, 0:256], C.hnT[:, kc, tb * 128:(tb + 1) * 128], C.wbl[bi][:, kc, :], start=(kc == 0), stop=(kc == 15)),
                         reads=[('wbl', bi), ('hnT', tb // 4)], writes=[('ps', pi)])
                P.op('act', lambda e, pi=pi, vi=vi, i=i: e.activation(out=C.vst[vi][:, i, :], in_=C.ps[pi][:, 0:256], func=AF.Copy),
                     reads=[('ps', pi)], writes=[('vst', vi)])
            P.dma('sp', dst(blk, tb4), C.vst[vi][:], reads=[('vst', vi)], writes=[('dst_tm', blk, tb4)])


def attn_out_block(nc, P, C, Ops, qb4, gate_ap, og, key_og, gate_key=None):
    st = C.state
    oi = st['onrr'] % 2
    st['onrr'] += 1
    P.op('dve', lambda e, oi=oi: e.reciprocal(out=C.rden[oi][:], in_=Ops[:, 128:129]),
         reads=[('O', qb4)], writes=[('rden', oi)])
    P.op('dve', lambda e, oi=oi: e.tensor_scalar(out=C.on[oi][:], in0=Ops[:, 0:128], scalar1=C.rden[oi][:, 0:1], scalar2=None, op0=ALU.mult),
         reads=[('O', qb4), ('rden', oi)], writes=[('on', oi)], strict=True)
    ti = st['trr'] % 4
    st['trr'] += 1
    P.op('pe', lambda e, oi=oi, ti=ti: e.transpose(C.pst[:, ti * 128:(ti + 1) * 128], C.on[oi][:], C.ident[:]),
         reads=[('on', oi), ('ident',)], writes=[('pst', ti)])
    if gate_ap is not None:
        P.op('dve', lambda e, ti=ti: e.tensor_tensor(out=og[:, qb4 * 128:(qb4 + 1) * 128], in0=C.pst[:, ti * 128:(ti + 1) * 128], in1=gate_ap, op=ALU.mult),
             reads=[('pst', ti), gate_key], writes=[key_og])
    else:
        P.op('dve', lambda e, ti=ti: e.tensor_copy(out=og[:, qb4 * 128:(qb4 + 1) * 128], in_=C.pst[:, ti * 128:(ti + 1) * 128]),
             reads=[('pst', ti)], writes=[key_og])


def emit_mix0(nc, P, T_, NH=8):
    xT = T_['xT']; g_in = T_['g0']; Wfm = T_['Wfm0']; Wtm = T_['Wtm0']; Wf = T_['Wf0']; bfv = T_['bf0']
    gains_in = T_['gains0']; ropeC = T_['ropeC']; ropeS = T_['ropeS']; Pm_in = T_['Pm']; trim_in = T_['trim']
    ident_in = T_['ident']; Mt_in = T_['Mt']; oT = T_['o0T']
    debug_stage1_only = False
    qk_s = nc.dram_tensor("a_qk_s", [5 * NH, 128, S], BF16, kind="Internal").ap()
    v_s = nc.dram_tensor("a_v_s", [S, 2 * NH * 128], BF16, kind="Internal").ap()
    c_s = nc.dram_tensor("a_c_s", [NH, S], F32, kind="Internal").ap()
    nheads_fox = nheads_dil = NH

    with ExitStack() as es:
        def sb(name, shape, dt):
            return es.enter_context(nc.sbuf_tensor("a1_" + name, shape, dt))
        C = ProjCtx()
        C.state = dict(wrr=0, psrr=0, qrr=0, sqrr=0, rsrr=0, vrr=0)
        C.ps = [es.enter_context(nc.psum_tensor(f"a1ps{i}", [128, 512], F32)) for i in range(8)]
        C.hnT = sb("hnT", [128, 16, S], BF16)
        C.ones = sb("ones", [128, 128], BF16)
        C.wbl = [sb(f"wbl{i}", [128, 16, 256], BF16) for i in range(2)]
        C.Ct = sb("Ct", [32, S], BF16)
        C.St = sb("St", [32, S], BF16)
        C.Pm = sb("Pmt", [32, 32], BF16)
        C.gains = sb("gains", [128, 4], F32)
        C.sq = [sb(f"sq{i}", [128, 512], BF16) for i in range(2)]
        C.rs = [sb(f"rs{i}", [128, 512], F32) for i in range(2)]
        C.qn = [sb(f"qn{i}", [128, 512], BF16) for i in range(3)]
        C.r1 = sb("r1", [32, 512], F32)
        C.r2 = sb("r2", [32, 512], F32)
        C.vst = [sb(f"vst{i}", [128, 4, 256], BF16) for i in range(2)]
        wft = sb("wft", [128, 16, NH], BF16)
        bft = sb("bft", [NH, 1], F32)
        onesr = sb("onesr", [NH, 512], F32)
        fx = sb("fx", [NH, 512], F32)
        fa_ = sb("fa_", [NH, 512], F32)
        fm_ = sb("fm_", [NH, 512], F32)
        ct = [sb(f"ct{i}", [NH, 512], F32) for i in range(2)]

        P.op('pool', lambda e: e.memset(C.ones[:], 1.0), writes=[('ones',)])
        P.op('pool', lambda e: e.memset(onesr[:], 1.0), writes=[('onesr',)])
        P.dma('pool', C.Ct[:], ropeC[:, :], writes=[('tabs',)])
        P.dma('pool', C.St[:], ropeS[:, :], writes=[('tabs',)])
        P.dma('pool', C.Pm[:], Pm_in[:, :], writes=[('Pm',)])
        P.dma('sp', C.gains[:], gains_in[:, :], writes=[('gains',)])
        P.dma('pool', wft[:], Wf.rearrange("(kc p) c -> p kc c", p=128), writes=[('wft',)])
        P.dma('sp', bft[:], bfv[:, :], writes=[('bft',)])
        P.op('act', lambda e: e.mul(C.gains[:, 0:1], C.gains[:, 0:1], SCALE), reads=[('gains',)], writes=[('gains',)])
        P.op('act', lambda e: e.mul(C.gains[:, 2:3], C.gains[:, 2:3], SCALE), reads=[('gains',)], writes=[('gains',)])

        norm_stage(nc, P, es, sb, C.ps, xT, g_in, C.hnT, C.ones)

        for tt in range(8):
            tsl = slice(tt * 512, (tt + 1) * 512)
            for kc in range(16):
                P.op('pe', lambda e, kc=kc, tsl=tsl: e.matmul(C.ps[7][0:NH, :], wft[:, kc, :], C.hnT[:, kc, tsl], start=(kc == 0), stop=(kc == 15)),
                     reads=[('wft',), ('hnT', tt)], writes=[('ps', 7)])
            P.op('dve', lambda e: e.tensor_scalar(out=fx[:], in0=C.ps[7][0:NH, :], scalar1=bft[:, 0:1], scalar2=None, op0=ALU.add),
                 reads=[('ps', 7), ('bft',)], writes=[('fx',)])
            P.op('act', lambda e: e.activation(out=fa_[:], in_=fx[:], func=AF.Abs), reads=[('fx',)], writes=[('fa_',)])
            P.op('act', lambda e: e.activation(out=fa_[:], in_=fa_[:], func=AF.Exp, scale=-1.0), reads=[('fa_',)], writes=[('fa_',)])
            P.op('act', lambda e: e.activation(out=fa_[:], in_=fa_[:], func=AF.Ln, bias=1.0), reads=[('fa_',)], writes=[('fa_',)])
            P.op('dve', lambda e: e.tensor_scalar(out=fm_[:], in0=fx[:], scalar1=0.0, scalar2=None, op0=ALU.min),
                 reads=[('fx',)], writes=[('fm_',)])
            P.op('dve', lambda e: e.tensor_tensor(out=fm_[:], in0=fm_[:], in1=fa_[:], op=ALU.subtract),
                 reads=[('fm_',), ('fa_',)], writes=[('fm_',)])
            ci = tt % 2
            init = 0.0 if tt == 0 else ct[1 - ci][:, 511:512]
            P.op('dve', lambda e, ci=ci, init=init: e.tensor_tensor_scan(out=ct[ci][:], data0=onesr[:], data1=fm_[:], initial=init, op0=ALU.mult, op1=ALU.add),
                 reads=[('fm_',), ('onesr',), ('ct', 1 - ci)], writes=[('ct', ci)], strict=True)
            P.dma('sp', c_s[:, tsl], ct[ci][:], reads=[('ct', ci)], writes=[('c_s', tt)])

        specs = ([dict(kind='norm', gain=0)] * NH + [dict(kind='norm', gain=1)] * NH + [dict(kind='raw')] * NH +
                 [dict(kind='norm', gain=2, rope=True, tabs=(C.Ct, C.St))] * NH + [dict(kind='norm', gain=3, rope=True, tabs=(C.Ct, C.St))] * NH)
        proj_fm(nc, P, C, Wfm, 5 * NH * 128, specs, lambda j, tt: qk_s[j, :, tt * 512:(tt + 1) * 512])
        proj_tm(nc, P, C, Wtm, 2 * NH * 128, lambda blk, tb4: v_s[tb4 * 512:(tb4 + 1) * 512, blk * 256:(blk + 1) * 256].rearrange("(i p) c -> p i c", p=128))
        P.emit_stage()


    with ExitStack() as es:
        def sb(name, shape, dt):
            return es.enter_context(nc.sbuf_tensor("a2_" + name, shape, dt))
        C = ProjCtx()
        C.state = dict(onrr=0, trr=0, srr=0, trr2=0, prr=0, ogrr=0)
        C.ps = [es.enter_context(nc.psum_tensor(f"a2ps{i}", [128, 512], F32)) for i in range(7)]
        C.pst = es.enter_context(nc.psum_tensor("a2pst", [128, 1024], BF16))
        C.ident = sb("ident", [128, 128], BF16)
        trim = sb("trim", [128, 128], F32)
        Mt = sb("Mt", [128, 20, 512], BF16)
        C.rden = [sb(f"rden{i}", [128, 1], F32) for i in range(4)]
        C.on = [sb(f"on{i}", [128, 128], BF16) for i in range(4)]
        tt_ = [sb(f"tt{i}", [128, 512], F32) for i in range(2)]
        et = [sb(f"et{i}", [128, 512], BF16) for i in range(2)]
        pT = [sb(f"pT{i}", [128, 512], BF16) for i in range(3)]
        og = [sb(f"og{i}", [128, 512], F32) for i in range(2)]
        hb = []
        for i in range(2):
            h_ = ProjCtx()
            h_.qT = sb(f"hq{i}", [128, S], BF16)
            h_.kT = sb(f"hk{i}", [128, S], BF16)
            h_.gT = sb(f"hg{i}", [128, S], BF16)
            h_.V = sb(f"hv{i}", [128, 32, 136], BF16)
            h_.cqb = sb(f"hcq{i}", [128, S], F32)
            h_.ck = sb(f"hck{i}", [128, 32], F32)
            hb.append(h_)
        P.dma('pool', C.ident[:], ident_in[:, :], writes=[('ident',)])
        P.dma('sp', trim[:], trim_in[:, :], writes=[('trim',)])
        P.dma('pool', Mt[:], Mt_in.rearrange("p (o c) -> p o c", o=20), writes=[('Mt',)])
        for i in range(2):
            P.op('pool', lambda e, i=i: e.memset(hb[i].V[:, :, 128:136], 1.0), writes=[('hb_V1', i)])

        heads = [('fox', h) for h in range(nheads_fox)] + [('dil', h) for h in range(nheads_dil)]
        S3 = [0, 1, 6]
        NQC0 = int(os.environ.get('NQC', '8'))

        def make_pre(hi, kind, h):
            bi = hi % 2
            H = hb[bi]
            kq, kk, kg, kv, kc_, kcq = [('hb', bi, n) for n in ('q', 'k', 'g', 'v', 'ck', 'cq')]
            if kind == 'fox':
                jq, jk, jg, vc0 = h, NH + h, 2 * NH + h, h * 128
            else:
                jq, jk, jg, vc0 = 3 * NH + h, 4 * NH + h, None, NH * 128 + h * 128

            def pre():
                P.dma('sp', H.qT[:], qk_s[jq, :, :], writes=[kq])
                P.dma('sp', H.kT[:], qk_s[jk, :, :], writes=[kk])
                P.dma('sp', H.V[:, :, 0:128], v_s[:, vc0:vc0 + 128].rearrange("(blk p) c -> p blk c", p=128), reads=[('hb_V1', bi)], writes=[kv])
                if kind == 'fox':
                    P.dma('sp', H.gT[:], qk_s[jg, :, :], writes=[kg])
                    P.op('act', lambda e: e.activation(out=H.gT[:], in_=H.gT[:], func=AF.Sigmoid), reads=[kg], writes=[kg])
                    P.dma('sp', H.ck[:], c_s[h, :].rearrange("(blk p) -> p blk", p=128), reads=[], writes=[kc_], allow_slow_non_contiguous=True)
                    P.op('pool', lambda e: e.tensor_scalar(out=H.ck[:], in0=H.ck[:], scalar1=-1.0, scalar2=None, op0=ALU.mult), reads=[kc_], writes=[kc_])
                    P.dma('sp', H.cqb[:], c_s[h, :].partition_broadcast(128), writes=[kcq])
            return pre

        def make_tile(hi, kind, h, Qc, kb, pre, last_of_chunk, ogi):
            bi = hi % 2
            H = hb[bi]
            kq, kk, kg, kv, kc_, kcq = [('hb', bi, n) for n in ('q', 'k', 'g', 'v', 'ck', 'cq')]
            q0 = Qc * 512
            kb_lo = 0 if kind == 'fox' else max(0, 4 * Qc - 16)
            j = kb - 4 * Qc
            qlo = max(0, j) * 128
            t = ProjCtx()

            def a():
                si = S3[C.state['srr'] % 3]
                C.state['srr'] += 1
                t.si = si
                P.op('pe', lambda e: e.matmul(C.ps[si][:, qlo:512], H.kT[:, kb * 128:(kb + 1) * 128], H.qT[:, q0 + qlo:q0 + 512], start=True, stop=True),
                     reads=[kq, kk], writes=[('ps', si)])

            def b():
                si = t.si
                pi = C.state['prr'] % 3
                C.state['prr'] += 1
                t.pi = pi
                if kind == 'fox':
                    ti = C.state['trr2'] % 2
                    C.state['trr2'] += 1
                    P.op('dve', lambda e: e.tensor_tensor(out=tt_[ti][:, qlo:512], in0=C.ps[si][:, qlo:512], in1=H.cqb[:, q0 + qlo:q0 + 512], op=ALU.add),
                         reads=[('ps', si), kcq], writes=[('tt', ti)])
                    if j >= 0:
                        P.op('dve', lambda e: e.tensor_tensor(out=tt_[ti][:, qlo:qlo + 128], in0=tt_[ti][:, qlo:qlo + 128], in1=trim[:], op=ALU.add),
                             reads=[('tt', ti), ('trim',)], writes=[('tt', ti)])
                    P.op('act', lambda e: e.activation(out=pT[pi][:, qlo:512], in_=tt_[ti][:, qlo:512], func=AF.Exp, bias=H.ck[:, kb:kb + 1]),
                         reads=[('tt', ti), kc_], writes=[('pT', pi)])
                else:
                    ei = C.state['trr2'] % 2
                    C.state['trr2'] += 1
                    oi_ = 4 * Qc - kb + 3
                    P.op('act', lambda e: e.activation(out=et[ei][:, qlo:512], in_=C.ps[si][:, qlo:512], func=AF.Exp),
                         reads=[('ps', si)], writes=[('et', ei)])
                    P.op('dve', lambda e: e.tensor_tensor(out=pT[pi][:, qlo:512], in0=et[ei][:, qlo:512], in1=Mt[:, oi_, qlo:512], op=ALU.mult),
                         reads=[('et', ei), ('Mt',)], writes=[('pT', pi)])

            def c():
                pi = t.pi
                for qb4 in range(max(0, j), 4):
                    last_kb = 4 * Qc + qb4
                    P.op('pe', lambda e, qb4=qb4, last_kb=last_kb: e.matmul(C.ps[2 + qb4][:, 0:130], pT[pi][:, qb4 * 128:(qb4 + 1) * 128], H.V[:, kb, 0:130], start=(kb == kb_lo), stop=(kb == last_kb)),
                         reads=[('pT', pi), kv], writes=[('O', qb4)])

            def post():
                if not last_of_chunk:
                    return
                for qb4 in range(4):
                    P.op('dve', lambda e, qb4=qb4: e.reciprocal(out=C.rden[qb4][:], in_=C.ps[2 + qb4][:, 128:129]), reads=[('O', qb4)], writes=[('rden', qb4)])
                for qb4 in range(4):
                    P.op('dve', lambda e, qb4=qb4: e.tensor_scalar(out=C.on[qb4][:], in0=C.ps[2 + qb4][:, 0:128], scalar1=C.rden[qb4][:, 0:1], scalar2=None, op0=ALU.mult),
                         reads=[('O', qb4), ('rden', qb4)], writes=[('on', qb4)], strict=True)
                for qb4 in range(4):
                    P.op('pe', lambda e, qb4=qb4: e.transpose(C.pst[:, qb4 * 128:(qb4 + 1) * 128], C.on[qb4][:], C.ident[:]), reads=[('on', qb4), ('ident',)], writes=[('pst', qb4)])
                if kind == 'fox':
                    P.op('dve', lambda e: e.tensor_tensor(out=og[ogi][:, 0:512], in0=C.pst[:, 0:512], in1=H.gT[:, q0:q0 + 512], op=ALU.mult),
                         reads=[('pst', q_) for q_ in range(4)] + [kg], writes=[('og', ogi)])
                else:
                    P.op('dve', lambda e: e.tensor_copy(out=og[ogi][:, 0:512], in_=C.pst[:, 0:512]), reads=[('pst', q_) for q_ in range(4)], writes=[('og', ogi)])
                orow = (h if kind == 'fox' else NH + h) * 128
                P.dma('sp', oT[orow:orow + 128, q0:q0 + 512], og[ogi][:], reads=[('og', ogi)], writes=[('oT', hi, Qc)])

            t.pre = pre if pre is not None else (lambda: None)
            t.a, t.b, t.c, t.post = a, b, c, post
            return t

        tasks = []
        pres = [make_pre(hi, kind, h) for hi, (kind, h) in enumerate(heads)]
        for hi, (kind, h) in enumerate(heads):
            nt_head = 0
            for Qc in range(NQC0):
                kb_lo = 0 if kind == 'fox' else max(0, 4 * Qc - 16)
                kb_hi = 4 * Qc + 3
                ogi = C.state['ogrr'] % 2
                C.state['ogrr'] += 1
                for kb in range(kb_lo, kb_hi + 1):
                    pre = None
                    if hi == 0 and nt_head == 0:
                        pre = pres[0]
                    if nt_head == 2 and hi + 1 < len(heads):
                        pre = pres[hi + 1]
                    tasks.append(make_tile(hi, kind, h, Qc, kb, pre, kb == kb_hi, ogi))
                    nt_head += 1
        run_pipeline(tasks, depth=2)
        P.emit_stage()


def host_consts():
    inv = 1.0 / (500000.0 ** (np.arange(0, 32, 2, dtype=np.float32) / 32))
    ang = np.arange(S, dtype=np.float32)[None, :] * inv[:, None]
    cos = np.cos(ang).astype(np.float32); sin = np.sin(ang).astype(np.float32)
    ropeC = np.concatenate([cos, cos], 0)
    ropeS = np.concatenate([-sin, sin], 0)
    Pm = np.zeros((32, 32), np.float32)
    for m in range(32):
        Pm[(m + 16) % 32, m] = 1.0
    p = np.arange(128)[:, None]; c = np.arange(128)[None, :]
    trim = np.where(p > c, NEGM, 0.0).astype(np.float32)
    ident = np.eye(128, dtype=np.float32)
    Mt = np.zeros((128, 20, 512), np.float32)
    for oi in range(20):
        o = 128 * (oi - 3)
        d = o + np.arange(512)[None, :] - np.arange(128)[:, None]
        m = ((d >= 0) & (d <= 128)).astype(np.float32) + ((d >= 0) & (d <= 512) & (d % 4 == 0)) + ((d >= 0) & (d <= 2048) & (d % 16 == 0))
        Mt[:, oi, :] = m
    return dict(ropeC=ropeC, ropeS=ropeS, Pm=Pm, trim=trim, ident=ident, Mt=Mt.reshape(128, 20 * 512))


def host_inputs_mix0(inp, b, half):
    w = inp['even_w_in'][0]
    hs = slice(half * 512, (half + 1) * 512)
    qa = w[:, 0:1024][:, hs]; ka = w[:, 1024:2048][:, hs]; va = w[:, 2048:3072][:, hs]; ga = w[:, 3072:4096][:, hs]
    fa = w[:, 4096:4104][:, half * 4:(half + 1) * 4]
    qd = w[:, 4104:5128][:, hs]; kd = w[:, 5128:6152][:, hs]; vd = w[:, 6152:7176][:, hs]
    m = dict(
        xT=np.ascontiguousarray(inp['x'][b].T),
        g=np.ascontiguousarray(inp['ln_mix_g'][0].reshape(16, 128).T),
        Wfm=np.ascontiguousarray(np.concatenate([qa, ka, ga, qd, kd], 1)),
        Wtm=np.ascontiguousarray(np.concatenate([va, vd], 1)),
        Wf=np.ascontiguousarray(fa),
        bf=np.ascontiguousarray(inp['even_b_f'][0][half * 4:(half + 1) * 4].reshape(4, 1)),
        gains=np.ascontiguousarray(np.stack([inp['even_g_q_fox'][0], inp['even_g_k_fox'][0], inp['even_g_q_dil'][0], inp['even_g_k_dil'][0]], 1)),
    )
    m.update(host_consts())
    return m


NB = -30000.0
GC = 1.5957691216057308


def emit_mix1(nc, P, T_, NG=4):
    xT = T_['h2T']; g_in = T_['g1']; Wfm = T_['Wfm1']; Wtm = T_['Wtm1']; gains_in = T_['gains1']
    ropeC = T_['ropeC']; ropeS = T_['ropeS']; Pm_in = T_['Pm']; ropeCc = T_['ropeCc']; ropeSc = T_['ropeSc']
    ident_in = T_['ident']
    w1k = T_['w1k']; w2k = T_['w2k']; peTk = T_['peTk']; w1v = T_['w1v']; w2v = T_['w2v']; peTv = T_['peTv']
    ovl_in = T_['ovl']; cmpM_in = T_['cmpM']; FT_in = T_['FT']; CT_in = T_['CT']; ExpT_in = T_['ExpT']; WB_in = T_['WB']
    trib_in = T_['trib']; oT = T_['o1T']
    dbg = False
    dbgo = nc.dram_tensor("b_dbgo", [128, 2048], F32, kind="Internal").ap()
    qk_s = nc.dram_tensor("b_qk_s", [8 * NG, 128, S], BF16, kind="Internal").ap()
    v_s = nc.dram_tensor("b_v_s", [S, 2 * NG * 128 + 256], BF16, kind="Internal").ap()
    kc_s = nc.dram_tensor("b_kc_s", [NG, 128, 256], BF16, kind="Internal").ap()
    vc_s = nc.dram_tensor("b_vc_s", [NG, 256, 128], BF16, kind="Internal").ap()

    with ExitStack() as es:
        def sb(name, shape, dt):
            return es.enter_context(nc.sbuf_tensor("b1_" + name, shape, dt))
        C = ProjCtx()
        C.state = dict(wrr=0, psrr=0, qrr=0, sqrr=0, rsrr=0, vrr=0)
        C.ps = [es.enter_context(nc.psum_tensor(f"b1ps{i}", [128, 512], F32)) for i in range(8)]
        C.hnT = sb("hnT", [128, 16, S], BF16)
        C.ones = sb("ones", [128, 128], BF16)
        C.wbl = [sb(f"wbl{i}", [128, 16, 256], BF16) for i in range(2)]
        C.Ct = sb("Ct", [32, S], BF16)
        C.St = sb("St", [32, S], BF16)
        C.Pm = sb("Pmt", [32, 32], BF16)
        C.gains = sb("gains", [128, 4], F32)
        C.sq = [sb(f"sq{i}", [128, 512], BF16) for i in range(2)]
        C.rs = [sb(f"rs{i}", [128, 512], F32) for i in range(2)]
        C.qn = [sb(f"qn{i}", [128, 512], BF16) for i in range(3)]
        C.r1 = sb("r1", [32, 512], F32)
        C.r2 = sb("r2", [32, 512], F32)
        C.vst = [sb(f"vst{i}", [128, 4, 256], BF16) for i in range(2)]
        P.op('pool', lambda e: e.memset(C.ones[:], 1.0), writes=[('ones',)])
        P.dma('pool', C.Ct[:], ropeC[:, :], writes=[('tabs',)])
        P.dma('pool', C.St[:], ropeS[:, :], writes=[('tabs',)])
        P.dma('pool', C.Pm[:], Pm_in[:, :], writes=[('Pm',)])
        P.dma('sp', C.gains[:], gains_in[:, :], writes=[('gains',)])
        P.op('act', lambda e: e.mul(C.gains[:, 0:1], C.gains[:, 0:1], SCALE), reads=[('gains',)], writes=[('gains',)])
        norm_stage(nc, P, es, sb, C.ps, xT, g_in, C.hnT, C.ones)
        rp = dict(rope=True, tabs=(C.Ct, C.St))
        specs = ([dict(kind='norm', gain=0, **rp)] * (4 * NG) + [dict(kind='norm', gain=1, **rp)] * NG +
                 [dict(kind='norm', gain=2, **rp)] * NG + [dict(kind='raw')] * (2 * NG))
        proj_fm(nc, P, C, Wfm, 8 * NG * 128, specs, lambda j, tt: qk_s[j, :, tt * 512:(tt + 1) * 512])
        proj_tm(nc, P, C, Wtm, 2 * NG * 128 + 256, lambda blk, tb4: v_s[tb4 * 512:(tb4 + 1) * 512, blk * 256:(blk + 1) * 256].rearrange("(i p) c -> p i c", p=128))
        P.emit_stage()

    with ExitStack() as es:
        def sb(name, shape, dt):
            return es.enter_context(nc.sbuf_tensor("bc_" + name, shape, dt))
        ps = [es.enter_context(nc.psum_tensor(f"bcps{i}", [128, 512], F32)) for i in range(4)]
        ones = sb("ones", [128, 128], BF16)
        Pm = sb("Pm", [32, 32], BF16)
        Cc = sb("Cc", [32, 256], BF16); Sc = sb("Sc", [32, 256], BF16)
        gains = sb("gains", [128, 4], F32)
        P.op('pool', lambda e: e.memset(ones[:], 1.0), writes=[('ones',)])
        P.dma('pool', Pm[:], Pm_in[:, :], writes=[('Pm',)])
        P.dma('pool', Cc[:], ropeCc[:, :], writes=[('tabc',)])
        P.dma('pool', Sc[:], ropeSc[:, :], writes=[('tabc',)])
        P.dma('sp', gains[:], gains_in[:, :], writes=[('gains',)])
        w1s = [sb(f"w1s{i}", [128, 32, 128], BF16) for i in range(2)]
        w2s = [sb(f"w2s{i}", [128, 128], BF16) for i in range(2)]
        pes = [sb(f"pes{i}", [128, 32], BF16) for i in range(2)]
        for i, (w1, w2, pe) in enumerate([(w1k, w2k, peTk), (w1v, w2v, peTv)]):
            P.dma('pool', w1s[i][:], w1.rearrange("(l p) o -> p l o", p=128), writes=[('w1s', i)])
            P.dma('pool', w2s[i][:], w2[:, :], writes=[('w2s', i)])
            P.dma('pool', pes[i][:], pe[:, :], writes=[('pes', i)])
        xc = sb("xc", [128, S], BF16)
        bz = sb("bz", [128, 1], F32)
        zs = sb("zs", [128, 256], F32); z2 = sb("z2", [128, 256], F32); sg = sb("sg", [128, 256], F32)
        G = sb("G", [128, 256], BF16)
        sq = sb("sq", [128, 256], BF16); rs = sb("rs", [128, 256], F32); kn = sb("kn", [128, 256], BF16)
        r1 = sb("r1", [32, 256], F32); r2 = sb("r2", [32, 256], F32)
        vct = sb("vct", [128, 2, 128], BF16)
        P.op('pool', lambda e: e.memset(G[:], 0.0), writes=[('G',)])
        P.op('pool', lambda e: e.memset(kn[:], 0.0), writes=[('kn',)])
        for g in range(NG):
            for i in range(2):
                P.dma('sp', xc[:], qk_s[6 * NG + NG * i + g, :, :], writes=[('xc',)])
                xv = xc[:].rearrange("p (i r) -> p i r", r=16)
                for l in range(32):
                    a, r = (0, l) if l < 16 else (1, l - 16)
                    P.op('pe', lambda e, l=l, a=a, r=r, i=i: e.matmul(ps[0][:, 0:255], w1s[i][:, l, :], xv[:, a:a + 255, r], start=(l == 0), stop=(l == 31)),
                         reads=[('w1s', i), ('xc',)], writes=[('ps', 0)])
                for l in range(32):
                    P.op('pe', lambda e, l=l, i=i: e.matmul(ps[1][:, 0:1], w1s[i][:, l, :], pes[i][:, l:l + 1], start=(l == 0), stop=(l == 31)),
                         reads=[('w1s', i), ('pes', i)], writes=[('ps', 1)])
                P.op('dve', lambda e: e.tensor_copy(out=bz[:], in_=ps[1][:, 0:1]), reads=[('ps', 1)], writes=[('bz',)])
                P.op('dve', lambda e: e.tensor_scalar(out=zs[:, 0:255], in0=ps[0][:, 0:255], scalar1=bz[:, 0:1], scalar2=None, op0=ALU.add),
                     reads=[('ps', 0), ('bz',)], writes=[('zs',)], strict=True)
                P.op('dve', lambda e: e.tensor_tensor(out=z2[:, 0:255], in0=zs[:, 0:255], in1=zs[:, 0:255], op=ALU.mult), reads=[('zs',)], writes=[('z2',)])
                P.op('dve', lambda e: e.tensor_scalar(out=z2[:, 0:255], in0=z2[:, 0:255], scalar1=0.044715, scalar2=1.0, op0=ALU.mult, op1=ALU.add), reads=[('z2',)], writes=[('z2',)])
                P.op('dve', lambda e: e.tensor_tensor(out=z2[:, 0:255], in0=z2[:, 0:255], in1=zs[:, 0:255], op=ALU.mult), reads=[('z2',), ('zs',)], writes=[('z2',)])
                P.op('act', lambda e: e.activation(out=sg[:, 0:255], in_=z2[:, 0:255], func=AF.Sigmoid, scale=GC), reads=[('z2',)], writes=[('sg',)])
                P.op('dve', lambda e: e.tensor_tensor(out=G[:, 0:255], in0=zs[:, 0:255], in1=sg[:, 0:255], op=ALU.mult), reads=[('zs',), ('sg',)], writes=[('G',)])
                if i == 0:
                    P.op('pe', lambda e: e.matmul(ps[2][:, 0:256], w2s[0][:], G[:], start=True, stop=True), reads=[('w2s', 0), ('G',)], writes=[('ps', 2)])
                    P.op('act', lambda e: e.activation(out=sq[:], in_=ps[2][:, 0:256], func=AF.Square), reads=[('ps', 2)], writes=[('sq',)])
                    P.op('pe', lambda e: e.matmul(ps[3][:, 0:256], ones[:], sq[:], start=True, stop=True), reads=[('sq',), ('ones',)], writes=[('ps', 3)])
                    P.op('act', lambda e: e.activation(out=rs[:], in_=ps[3][:, 0:256], func=AF.Sqrt, scale=1.0 / 128, bias=EPS), reads=[('ps', 3)], writes=[('rs',)])
                    P.op('dve', lambda e: e.reciprocal(out=rs[:], in_=rs[:]), reads=[('rs',)], writes=[('rs',)])
                    P.op('dve', lambda e: e.scalar_tensor_tensor(out=kn[:], in0=ps[2][:, 0:256], scalar=gains[:, 3:4], in1=rs[:], op0=ALU.mult, op1=ALU.mult),
                         reads=[('ps', 2), ('rs',), ('gains',)], writes=[('kn',)])
                    P.op('pe', lambda e: e.matmul(ps[3][0:32, 0:256], Pm[:], kn[0:32, :], start=True, stop=True), reads=[('kn',), ('Pm',)], writes=[('ps', 3)])
                    P.op('dve', lambda e: e.tensor_tensor(out=r1[:], in0=ps[3][0:32, 0:256], in1=Sc[:], op=ALU.mult), reads=[('ps', 3), ('tabc',)], writes=[('r1',)])
                    P.op('dve', lambda e: e.tensor_tensor(out=r2[:], in0=kn[0:32, :], in1=Cc[:], op=ALU.mult), reads=[('kn',), ('tabc',)], writes=[('r2',)])
                    P.op('dve', lambda e: e.tensor_tensor(out=kn[0:32, :], in0=r1[:], in1=r2[:], op=ALU.add), reads=[('r1',), ('r2',)], writes=[('kn',)])
                    P.dma('sp', kc_s[g, :, :], kn[:], reads=[('kn',)], writes=[('kc_s', g)])
                else:
                    for nb in range(2):
                        P.op('pe', lambda e, nb=nb: e.matmul(ps[2][:, nb * 128:(nb + 1) * 128], G[:, nb * 128:(nb + 1) * 128], w2s[1][:], start=True, stop=True),
                             reads=[('w2s', 1), ('G',)], writes=[('ps', 2)])
                    P.op('act', lambda e: e.activation(out=vct[:].rearrange("p a b -> p (a b)"), in_=ps[2][:, 0:256], func=AF.Copy), reads=[('ps', 2)], writes=[('vct',)])
                    P.dma('sp', vc_s[g].rearrange("(nb p) d -> p nb d", p=128), vct[:], reads=[('vct',)], writes=[('vc_s', g)])
        P.emit_stage()

    with ExitStack() as es:
        def sb(name, shape, dt):
            return es.enter_context(nc.sbuf_tensor("b2_" + name, shape, dt))
        st = dict(srr=0, prr=0, err=0, onrr=0, trr=0, ogrr=0, scl=0)
        _ps3 = [es.enter_context(nc.psum_tensor(f"b2ps{i}", [128, 512], F32)) for i in range(3)]
        Oall = es.enter_context(nc.psum_tensor("b2O", [128, 2048], F32))
        Oden = Oall[:].rearrange("p (b c) -> p b c", c=512)[:, :, 128:129]
        ps = [_ps3[0], _ps3[1]] + [Oall[:, q_ * 512:(q_ + 1) * 512] for q_ in range(4)] + [_ps3[2]]
        pst = es.enter_context(nc.psum_tensor("b2pst", [128, 1024], BF16))
        ident = sb("ident", [128, 128], BF16)
        trib = sb("trib", [128, 128], BF16)
        cmpM = sb("cmpM", [128, 2, S], BF16)
        ExpT = sb("ExpT", [64, 32, 128], BF16)
        WB = sb("WB", [128, 8, 512], BF16)
        P.dma('pool', ident[:], ident_in[:, :], writes=[('ident',)])
        P.dma('pool', trib[:], trib_in[:, :], writes=[('trib',)])
        for nb in range(2):
            for c in range(8):
                P.dma('pool', cmpM[:, nb, c * 512:(c + 1) * 512], cmpM_in[nb * 128:(nb + 1) * 128, c * 512:(c + 1) * 512], writes=[('cmpM', nb, c)])
        P.dma('pool', ExpT[:], ExpT_in.rearrange("j (k p) -> j k p", k=32), writes=[('ExpT',)])
        P.dma('pool', WB[:], WB_in.rearrange("p (o c) -> p o c", o=8), writes=[('WB',)])
        WBm = sb("WBm", [128, 8, 512], BF16)
        P.op('pool', lambda e: e.tensor_scalar(out=WBm[:], in0=WB[:], scalar1=-1.0, scalar2=None, op0=ALU.is_gt), reads=[('WB',)], writes=[('WBm',)])
        ksT = sb("ksT", [128, S], BF16); kwT = sb("kwT", [128, S], BF16)
        VS = sb("VS", [128, 32, 136], BF16); VW = sb("VW", [128, 32, 136], BF16)
        kcT = sb("kcT", [128, 256], BF16)
        VC = sb("VC", [128, 2, 200], BF16)
        gsg = sb("gsg", [128, 32, 12], F32)
        gtmp = sb("gtmp", [128, 32, 12], BF16)
        qT = [sb(f"qT{i}", [128, S], BF16) for i in range(4)]
        oacc = [sb(f"oacc{i}", [128, 4, 128], F32) for i in range(4)]
        impa = sb("impa", [128, 4, 64], F32)
        FTt = sb("FTt", [128, 4, 64], F32); CTt = sb("CTt", [128, 4, 64], F32)
        m8 = sb("m8", [128, 8], F32); m8b = sb("m8b", [128, 8], F32)
        wk = sb("wk", [128, 64], F32); s1 = sb("s1", [128, 64], F32); s2 = sb("s2", [128, 64], F32)
        selb = sb("selb", [128, 64], BF16)
        selT = sb("selT", [64, 512], BF16)
        et = [sb(f"et{i}", [128, 512], BF16) for i in range(2)]
        maskS = sb("maskS", [128, 32, 512], BF16)
        sadd = [sb(f"sadd{i}", [128, 512], F32) for i in range(2)]
        pT = [sb(f"pT{i}", [128, 512], BF16) for i in range(3)]
        og = [sb(f"og{i}", [128, 512], F32) for i in range(2)]
        on = [sb(f"on{i}", [128, 128], BF16) for i in range(2)]
        on4 = [sb(f"on4_{i}", [128, 128], BF16) for i in range(4)]
        rden = [sb(f"rden{i}", [128, 4], F32) for i in range(2)]
        scl = [sb(f"scl{i}", [128, 4], F32) for i in range(2)]
        P.op('pool', lambda e: e.memset(VS[:, :, 128:136], 1.0), writes=[('VS1',)])
        P.op('pool', lambda e: e.memset(VW[:, :, 128:136], 1.0), writes=[('VW1',)])
        P.op('pool', lambda e: e.memset(VC[:, :, 128:136], 1.0), writes=[('VC1',)])
        for nb in range(2):
            P.dma('pool', VC[:, nb, 136:200], ovl_in[nb * 128:(nb + 1) * 128, :], writes=[('VCo', nb)])
        NQC = int(os.environ.get('NQC', '8'))
        dbt = sb('dbt', [128, 2048], F32)
        P.op('pool', lambda e: e.memset(dbt[:], 0.0), writes=[('dbt',)])
        dstate = {'done': os.environ.get('DBGD', '0') != '1'}
        P.same = os.environ.get('SAME', '0') == '1'

        def finish_branch(hh, Qc, br, first, ncol_den=128, imp=False, imp_first=False):
            ri = st['scl'] % 2
            st['scl'] += 1
            rd = rden[ri]; sc = scl[ri]
            rd3 = rd[:].rearrange("p (b o) -> p b o", o=1)
            sc3 = sc[:].rearrange("p (b o) -> p b o", o=1)
            okeys = [('O', q_) for q_ in range(4)]
            P.op('dve', lambda e: e.tensor_scalar(out=rd3, in0=Oden, scalar1=1e-30, scalar2=None, op0=ALU.max), reads=okeys, writes=[('rden', ri)])
            P.op('dve', lambda e: e.reciprocal(out=rd[:], in_=rd[:]), reads=[('rden', ri)], writes=[('rden', ri)], strict=True)
            col = hh * 3 + br
            P.op('dve', lambda e: e.tensor_tensor(out=sc3, in0=rd3, in1=gsg[:, Qc * 4:(Qc + 1) * 4, col:col + 1], op=ALU.mult),
                 reads=[('rden', ri), ('gsg',)], writes=[('scl', ri)], strict=True)
            for qb4 in range(4):
                Ops = ps[2 + qb4]
                if first:
                    P.op('dve', lambda e, Ops=Ops, qb4=qb4: e.tensor_scalar(out=oacc[hh][:, qb4, :], in0=Ops[:, 0:128], scalar1=sc[:, qb4:qb4 + 1], scalar2=None, op0=ALU.mult),
                         reads=[('O', qb4), ('scl', ri)], writes=[('oacc', hh, qb4)], strict=True)
                else:
                    P.op('dve', lambda e, Ops=Ops, qb4=qb4: e.scalar_tensor_tensor(out=oacc[hh][:, qb4, :], in0=Ops[:, 0:128], scalar=sc[:, qb4:qb4 + 1], in1=oacc[hh][:, qb4, :], op0=ALU.mult, op1=ALU.add),
                         reads=[('O', qb4), ('scl', ri), ('oacc', hh, qb4)], writes=[('oacc', hh, qb4)], strict=True)
            if imp:
                for qb4 in range(4):
                    Ops = ps[2 + qb4]
                    if imp_first:
                        P.op('dve', lambda e, Ops=Ops, qb4=qb4: e.tensor_scalar(out=impa[:, qb4, :], in0=Ops[:, 136:200], scalar1=rd[:, qb4:qb4 + 1], scalar2=None, op0=ALU.mult),
                             reads=[('O', qb4), ('rden', ri)], writes=[('impa', qb4)], strict=True)
                    else:
                        P.op('dve', lambda e, Ops=Ops, qb4=qb4: e.scalar_tensor_tensor(out=impa[:, qb4, :], in0=Ops[:, 136:200], scalar=rd[:, qb4:qb4 + 1], in1=impa[:, qb4, :], op0=ALU.mult, op1=ALU.add),
                             reads=[('O', qb4), ('rden', ri), ('impa', qb4)], writes=[('impa', qb4)], strict=True)

        S3 = [0, 1, 6]

        def group_loads(g):
            P.dma('sp', ksT[:], qk_s[4 * NG + g, :, :], writes=[('ksT',)])
            P.dma('sp', kwT[:], qk_s[5 * NG + g, :, :], writes=[('kwT',)])
            P.dma('sp', VS[:, :, 0:128], v_s[:, g * 128:(g + 1) * 128].rearrange("(blk p) c -> p blk c", p=128), reads=[('VS1',)], writes=[('VS',)])
            P.dma('sp', VW[:, :, 0:128], v_s[:, NG * 128 + g * 128:NG * 128 + (g + 1) * 128].rearrange("(blk p) c -> p blk c", p=128), reads=[('VW1',)], writes=[('VW',)])
            P.dma('sp', kcT[:], kc_s[g, :, :], writes=[('kcT',)])
            P.dma('sp', VC[:, :, 0:128], vc_s[g].rearrange("(nb p) d -> p nb d", p=128), reads=[('VC1',)], writes=[('VC',)])
            P.dma('sp', gtmp[:], v_s[:, 2 * NG * 128 + g * 12:2 * NG * 128 + (g + 1) * 12].rearrange("(blk p) c -> p blk c", p=128), writes=[('gtmp',)])
            P.op('act', lambda e: e.activation(out=gsg[:], in_=gtmp[:], func=AF.Sigmoid), reads=[('gtmp',)], writes=[('gsg',)])
            for hh in range(4):
                P.dma('sp', qT[hh][:], qk_s[g * 4 + hh, :, :], writes=[('qT', hh)])

        def chunk_tables(Qc):
            q0 = Qc * 512
            P.dma('sp', FTt[:], FT_in[q0:q0 + 512, :].rearrange("(b p) j -> p b j", p=128), writes=[('FTt',)])
            P.dma('sp', CTt[:], CT_in[q0:q0 + 512, :].rearrange("(b p) j -> p b j", p=128), writes=[('CTt',)])

        def make_cmp_tile(g, Qc, hh, nb, nbs, pre):
            q0 = Qc * 512
            t = ProjCtx()

            def a():
                si = S3[st['srr'] % 3]; st['srr'] += 1
                t.si = si
                P.op('pe', lambda e: e.matmul(ps[si][:], kcT[:, nb * 128:(nb + 1) * 128], qT[hh][:, q0:q0 + 512], start=True, stop=True),
                     reads=[('kcT',), ('qT', hh)], writes=[('ps', si)])

            def b():
                si = t.si
                ei = st['err'] % 2; st['err'] += 1
                pi = st['prr'] % 3; st['prr'] += 1
                t.pi = pi
                P.op('act', lambda e: e.activation(out=et[ei][:], in_=ps[si][:], func=AF.Exp), reads=[('ps', si)], writes=[('et', ei)])
                P.op('dve', lambda e: e.tensor_tensor(out=pT[pi][:], in0=et[ei][:], in1=cmpM[:, nb, q0:q0 + 512], op=ALU.mult),
                     reads=[('et', ei), ('cmpM', nb, Qc)], writes=[('pT', pi)])

            def c():
                pi = t.pi
                for qb4 in range(4):
                    P.op('pe', lambda e, qb4=qb4: e.matmul(ps[2 + qb4][:, 0:200], pT[pi][:, qb4 * 128:(qb4 + 1) * 128], VC[:, nb, :], start=(nb == 0), stop=(nb == nbs[-1])),
                         reads=[('pT', pi), ('VC',), ('VCo', 0), ('VCo', 1), ('VC1',)], writes=[('O', qb4)])

            def post():
                if nb == nbs[-1]:
                    finish_branch(hh, Qc, 0, True, imp=True, imp_first=(hh == 0))

            t.pre = pre if pre is not None else (lambda: None)
            t.a, t.b, t.c, t.post = a, b, c, post
            return t

        def cmp_tiles(g, Qc):
            out = []
            pre = (lambda: chunk_tables(Qc))
            for hh in range(4):
                nbs = [0] + ([1] if Qc >= 4 else [])
                for nb in nbs:
                    out.append(make_cmp_tile(g, Qc, hh, nb, nbs, pre))
                    pre = None
            return out

        def do_B(Qc):
            for qb4 in range(4):
                P.op('dve', lambda e, qb4=qb4: e.tensor_tensor(out=wk[:], in0=impa[:, qb4, :], in1=FTt[:, qb4, :], op=ALU.max), reads=[('impa', qb4), ('FTt',)], writes=[('wk',)])
                P.op('dve', lambda e, qb4=qb4: e.tensor_tensor(out=wk[:], in0=wk[:], in1=CTt[:, qb4, :], op=ALU.min), reads=[('wk',), ('CTt',)], writes=[('wk',)])
                P.op('dve', lambda e: e.max(out=m8[:], in_=wk[:]), reads=[('wk',)], writes=[('m8',)], strict=True)
                P.op('dve', lambda e: e.match_replace(out=s1[:], in_to_replace=m8[:], in_values=wk[:], imm_value=-3.0e38), reads=[('wk',), ('m8',)], writes=[('s1',)], strict=True)
                P.op('dve', lambda e: e.max(out=m8b[:], in_=s1[:]), reads=[('s1',)], writes=[('m8b',)], strict=True)
                P.op('dve', lambda e: e.tensor_scalar(out=s1[:], in0=wk[:], scalar1=m8b[:, 7:8], scalar2=None, op0=ALU.is_ge), reads=[('wk',), ('m8b',)], writes=[('s1',)], strict=True)
                P.op('dve', lambda e: e.tensor_scalar(out=s2[:], in0=wk[:], scalar1=-5.0e29, scalar2=None, op0=ALU.is_gt), reads=[('wk',)], writes=[('s2',)])
                P.op('dve', lambda e: e.tensor_tensor(out=s1[:], in0=s1[:], in1=s2[:], op=ALU.mult), reads=[('s1',), ('s2',)], writes=[('s1',)])
                P.op('dve', lambda e: e.tensor_scalar(out=selb[:], in0=s1[:], scalar1=-NB, scalar2=NB, op0=ALU.mult, op1=ALU.add), reads=[('s1',)], writes=[('selb',)])
                ti = st['trr'] % 4; st['trr'] += 1
                P.op('pe', lambda e, ti=ti: e.transpose(pst[0:64, ti * 128:(ti + 1) * 128], selb[:], ident[:]), reads=[('selb',), ('ident',)], writes=[('pst', ti)])
                P.op('dve', lambda e, ti=ti, qb4=qb4: e.tensor_copy(out=selT[:, qb4 * 128:(qb4 + 1) * 128], in_=pst[0:64, ti * 128:(ti + 1) * 128]), reads=[('pst', ti)], writes=[('selT',)])
            for kb in range(0, 4 * Qc + 4):
                j = kb - 4 * Qc
                qlo = max(0, j) * 128
                si = S3[st['srr'] % 3]; st['srr'] += 1
                P.op('pe', lambda e, kb=kb, si=si, j=j: e.matmul(ps[si][:, 0:512], ExpT[:, kb, :], selT[:, 0:512], start=True, stop=(j < 0)),
                     reads=[('ExpT',), ('selT',)], writes=[('ps', si)])
                if j >= 0:
                    P.op('pe', lambda e, si=si, qlo=qlo: e.matmul(ps[si][:, qlo:qlo + 128], ident[:], trib[:], start=False, stop=True),
                         reads=[('ident',), ('trib',)], writes=[('ps', si)])
                P.op('dve', lambda e, kb=kb, si=si: e.tensor_scalar(out=maskS[:, kb, :], in0=ps[si][:, 0:512], scalar1=-1.0, scalar2=None, op0=ALU.is_gt), reads=[('ps', si)], writes=[('maskS', kb)])

        def make_sw_tile(g, Qc, hh, br, kb, kb_lo, kb_hi):
            q0 = Qc * 512
            KT, VV, kkey, vkey, v1 = (ksT, VS, ('ksT',), ('VS',), ('VS1',)) if br == 1 else (kwT, VW, ('kwT',), ('VW',), ('VW1',))
            j = kb - 4 * Qc
            qlo = max(0, j) * 128
            t = ProjCtx()

            def a():
                si = S3[st['srr'] % 3]; st['srr'] += 1
                t.si = si
                P.op('pe', lambda e: e.matmul(ps[si][:, qlo:512], KT[:, kb * 128:(kb + 1) * 128], qT[hh][:, q0 + qlo:q0 + 512], start=True, stop=True),
                     reads=[kkey, ('qT', hh)], writes=[('ps', si)])

            def b():
                si = t.si
                pi = st['prr'] % 3; st['prr'] += 1
                t.pi = pi
                ai = st['err'] % 2; st['err'] += 1
                if br == 1:
                    mk_ap, mk_key = maskS[:, kb, qlo:512], ('maskS', kb)
                else:
                    mk_ap, mk_key = WBm[:, 4 * Qc - kb + 3, qlo:512], ('WBm',)
                P.op('act', lambda e: e.activation(out=et[ai][:, qlo:512], in_=ps[si][:, qlo:512], func=AF.Exp), reads=[('ps', si)], writes=[('et', ai)])
                P.op('dve', lambda e: e.tensor_tensor(out=pT[pi][:, qlo:512], in0=et[ai][:, qlo:512], in1=mk_ap, op=ALU.mult),
                     reads=[('et', ai), mk_key], writes=[('pT', pi)])

            def c():
                pi = t.pi
                for qb4 in range(max(0, j), 4):
                    last_kb = 4 * Qc + qb4
                    P.op('pe', lambda e, qb4=qb4, last_kb=last_kb: e.matmul(ps[2 + qb4][:, 0:130], pT[pi][:, qb4 * 128:(qb4 + 1) * 128], VV[:, kb, 0:130], start=(kb == kb_lo), stop=(kb == last_kb)),
                         reads=[('pT', pi), vkey, v1], writes=[('O', qb4)])

            def post():
                if kb != kb_hi:
                    return
                finish_branch(hh, Qc, br, False)
                if br != 2:
                    return
                ogi = st['ogrr'] % 2; st['ogrr'] += 1
                for qb4 in range(4):
                    P.op('dve', lambda e, qb4=qb4: e.tensor_copy(out=on4[qb4][:], in_=oacc[hh][:, qb4, :]), reads=[('oacc', hh, qb4)], writes=[('on4', qb4)])
                for qb4 in range(4):
                    P.op('pe', lambda e, qb4=qb4: e.transpose(pst[:, qb4 * 128:(qb4 + 1) * 128], on4[qb4][:], ident[:]), reads=[('on4', qb4), ('ident',)], writes=[('pst', qb4)])
                P.op('dve', lambda e: e.tensor_copy(out=og[ogi][:, 0:512], in_=pst[:, 0:512]), reads=[('pst', q_) for q_ in range(4)], writes=[('og', ogi)])
                orow = (g * 4 + hh) * 128
                P.dma('sp', oT[orow:orow + 128, q0:q0 + 512], og[ogi][:], reads=[('og', ogi)], writes=[('oT', g, hh, Qc)])

            t.pre = (lambda: None)
            t.a, t.b, t.c, t.post = a, b, c, post
            return t

        def sw_tiles(g, Qc):
            out = []
            for hh in range(4):
                for br in (1, 2):
                    kb_lo = 0 if br == 1 else max(0, 4 * Qc - 4)
                    kb_hi = 4 * Qc + 3
                    for kb in range(kb_lo, kb_hi + 1):
                        out.append(make_sw_tile(g, Qc, hh, br, kb, kb_lo, kb_hi))
            return out

        for g in range(NG):
            group_loads(g)
            run_pipeline(cmp_tiles(g, 0), depth=2)
            for Qc in range(NQC):
                do_B(Qc)
                tasks = sw_tiles(g, Qc) + (cmp_tiles(g, Qc + 1) if Qc + 1 < NQC else [])
                run_pipeline(tasks, depth=2)
        P.emit_stage()


def host_consts1():
    c0 = host_consts()
    out = dict(ropeC=c0['ropeC'], ropeS=c0['ropeS'], Pm=c0['Pm'], ident=c0['ident'])
    inv = 1.0 / (500000.0 ** (np.arange(0, 32, 2, dtype=np.float32) / 32))
    posc = (np.arange(256) * 16 + 31).astype(np.float32)
    ang = posc[None, :] * inv[:, None]
    cos = np.cos(ang).astype(np.float32); sin = np.sin(ang).astype(np.float32)
    out['ropeCc'] = np.concatenate([cos, cos], 0); out['ropeSc'] = np.concatenate([-sin, sin], 0)
    n = np.arange(256)
    start = n * 16
    js = np.arange(64) * 64
    ov = ((start[:, None] < js[None, :] + 64) & (start[:, None] + 32 > js[None, :])).astype(np.float32)
    ov[255] = 0
    out['ovl'] = ov
    q = np.arange(S)
    cm = ((16 * n + 31)[:, None] <= q[None, :]).astype(np.float32)
    cm[255] = 0
    out['cmpM'] = cm
    cur = (q // 64)[:, None]
    jj = np.arange(64)[None, :]
    forced = (jj == 0) | (jj == cur) | (jj == cur - 1)
    out['FT'] = np.where(forced, 1e9, 0.0).astype(np.float32)
    out['CT'] = np.where(jj <= cur, 3.0e38, -1e30).astype(np.float32)
    E = np.zeros((64, 32, 128), np.float32)
    for kb in range(32):
        for p in range(128):
            E[2 * kb + p // 64, kb, p] = 1.0
    out['ExpT'] = E.reshape(64, 32 * 128)
    WBt = np.zeros((128, 8, 512), np.float32)
    for oi in range(8):
        o = 128 * (oi - 3)
        d = o + np.arange(512)[None, :] - np.arange(128)[:, None]
        WBt[:, oi, :] = np.where((d >= 0) & (d < 512), 0.0, NB)
    out['WB'] = WBt.reshape(128, 8 * 512)
    p = np.arange(128)[:, None]; c = np.arange(128)[None, :]
    out['trib'] = np.where(p > c, NB, 0.0).astype(np.float32)
    return out


def host_inputs_mix1(inp, xT_b, half):
    w = inp['odd_w_in'][0]
    q = w[:, 0:2048][:, half * 1024:(half + 1) * 1024]
    def grp(i):
        blk = w[:, 2048 + i * 512:2048 + (i + 1) * 512]
        return blk[:, half * 256:(half + 1) * 256]
    kc, vc, ks, vs, kw, vw = [grp(i) for i in range(6)]
    gt = w[:, 2048 + 3072:2048 + 3072 + 48][:, half * 24:(half + 1) * 24]
    gpad = np.zeros((D, 256), np.float32); gpad[:, :24] = gt
    m = dict(
        xT=xT_b,
        g=np.ascontiguousarray(inp['ln_mix_g'][1].reshape(16, 128).T),
        Wfm=np.ascontiguousarray(np.concatenate([q, ks, kw, kc, vc], 1)),
        Wtm=np.ascontiguousarray(np.concatenate([vs, vw, gpad], 1)),
        gains=np.ascontiguousarray(np.stack([inp['odd_g_q'][0], inp['odd_g_ks'][0], inp['odd_g_kw'][0], inp['odd_g_kc'][0]], 1)),
        w1k=inp['odd_phi_k_w1'][0], w2k=inp['odd_phi_k_w2'][0], peTk=np.ascontiguousarray(inp['odd_phi_k_pe'][0].T),
        w1v=inp['odd_phi_v_w1'][0], w2v=inp['odd_phi_v_w2'][0], peTv=np.ascontiguousarray(inp['odd_phi_v_pe'][0].T),
    )
    m.update(host_consts1())
    return m


F = 8192


def emit_phaseC(nc, P, tag, xT, oT, w_out, g_in, w_up, w_down, hT, T, TT=512):
    NT = T // TT
    with ExitStack() as es:
        def sb(name, shape, dt):
            return es.enter_context(nc.sbuf_tensor(tag + name, shape, dt))
        wb = [sb(f"wb{i}", [128, 8192], BF16) for i in range(3)]
        ots = [sb(f"ot{i}", [128, 16, TT], BF16) for i in range(2)]
        hts = [sb(f"ht{i}", [128, 16, TT], F32) for i in range(2)]
        hn = sb("hn", [128, 16, TT], BF16)
        ut = sb("ut", [128, 32, TT], BF16)
        sq = [sb(f"sq{i}", [128, TT], BF16) for i in range(2)]
        rt = [sb(f"rt{i}", [128, TT], F32) for i in range(2)]
        rstd = sb("rstd", [128, TT], F32)
        ones = sb("ones", [128, 128], BF16)
        gt = sb("gt", [128, 16], F32)
        ps = [es.enter_context(nc.psum_tensor(tag + f"ps{i}", [128, 512], F32)) for i in range(8)]

        P.op('pool', lambda e: e.memset(ones[:], 1.0), writes=[('ones',)])
        P.dma('sp', gt[:], g_in[:, :], writes=[('g',)])

        jobs = []
        state = {'psrr': 0, 'sqrr': 0, 'rtrr': 0}

        def nextps():
            i = state['psrr'] % 7
            state['psrr'] += 1
            return i

        tbs = []
        for t in range(NT):
            tsl = slice(t * TT, (t + 1) * TT)
            par = t % 2
            ot = ots[par]
            ht = hts[par]

            def tile_begin(t=t, tsl=tsl, ot=ot, ht=ht, par=par):
                P.dma('pool', ot[:], oT[:, tsl].rearrange("(kc p) n -> p kc n", p=128),
                      writes=[('ot', par)])
                P.dma('sp', ht[:], xT[:, tsl].rearrange("(kc p) n -> p kc n", p=128),
                      writes=[('ht', par, dc) for dc in range(16)])
            tbs.append(tile_begin)

            for blk in range(4):
                def load(bi, blk=blk):
                    v = wb[bi][:].rearrange("p (kc c) -> p kc c", kc=16)
                    P.dma('pool', v, w_out[:, blk * 512:(blk + 1) * 512].rearrange("(kc p) c -> p kc c", p=128),
                          writes=[('wb', bi)])

                def comp(bi, blk=blk, t=t, first=(blk == 0), tb=tile_begin, ot=ot, ht=ht, par=par):
                    if first and t == 0:
                        tb()
                    v = wb[bi][:].rearrange("p (kc c) -> p kc c", kc=16)
                    for dcl in range(4):
                        dc = blk * 4 + dcl
                        pi = nextps()
                        for kc in range(16):
                            P.op('pe', lambda e, kc=kc, pi=pi, dcl=dcl: e.matmul(ps[pi][:], v[:, kc, dcl * 128:(dcl + 1) * 128], ot[:, kc, :], start=(kc == 0), stop=(kc == 15)),
                                 reads=[('wb', bi), ('ot', par)], writes=[('ps', pi)])
                        P.op('dve', lambda e, pi=pi, dc=dc: e.tensor_tensor(out=ht[:, dc, :], in0=ps[pi][:], in1=ht[:, dc, :], op=ALU.add),
                             reads=[('ps', pi), ('ht', par, dc)], writes=[('ht', par, dc)])
                jobs.append((load, comp))

            def norm(ht=ht, par=par):
                pn = 7
                for dc in range(16):
                    si = state['sqrr'] % 2
                    state['sqrr'] += 1
                    P.op('act', lambda e, dc=dc, si=si: e.activation(out=sq[si][:], in_=ht[:, dc, :], func=AF.Square),
                         reads=[('ht', par, dc)], writes=[('sq', si)])
                    P.op('pe', lambda e, dc=dc, si=si: e.matmul(ps[pn][:], ones[:], sq[si][:], start=(dc == 0), stop=(dc == 15)),
                         reads=[('sq', si), ('ones',)], writes=[('ps', pn)])
                P.op('act', lambda e: e.activation(out=rstd[:], in_=ps[pn][:], func=AF.Sqrt, scale=1.0 / D, bias=EPS),
                     reads=[('ps', pn)], writes=[('rstd',)])
                P.op('dve', lambda e: e.reciprocal(out=rstd[:], in_=rstd[:]), reads=[('rstd',)], writes=[('rstd',)])
                for dc in range(16):
                    P.op('dve', lambda e, dc=dc: e.scalar_tensor_tensor(out=hn[:, dc, :], in0=ht[:, dc, :], scalar=gt[:, dc:dc + 1], in1=rstd[:], op0=ALU.mult, op1=ALU.mult),
                         reads=[('ht', par, dc), ('g',), ('rstd',)], writes=[('hn', dc)])

            for hf in range(2):
                for blk in range(8):
                    c0 = hf * 4096 + blk * 512

                    def load(bi, c0=c0):
                        v = wb[bi][:].rearrange("p (kc c) -> p kc c", kc=16)
                        P.dma('pool', v, w_up[:, c0:c0 + 512].rearrange("(kc p) c -> p kc c", p=128), writes=[('wb', bi)])

                    def comp(bi, blk=blk, hf=hf, donorm=(hf == 0 and blk == 0), nf=norm, t=t):
                        if donorm:
                            nf()
                            if t + 1 < NT:
                                tbs[t + 1]()
                        v = wb[bi][:].rearrange("p (kc c) -> p kc c", kc=16)
                        for fl in range(4):
                            fc = blk * 4 + fl
                            pi = nextps()
                            for kc in range(16):
                                P.op('pe', lambda e, kc=kc, pi=pi, fl=fl: e.matmul(ps[pi][:], v[:, kc, fl * 128:(fl + 1) * 128], hn[:, kc, :], start=(kc == 0), stop=(kc == 15)),
                                     reads=[('wb', bi), ('hn', kc)], writes=[('ps', pi)])
                            ri = state['rtrr'] % 2
                            state['rtrr'] += 1
                            P.op('act', lambda e, pi=pi, ri=ri: e.activation(out=rt[ri][:], in_=ps[pi][:], func=AF.Relu),
                                 reads=[('ps', pi)], writes=[('rt', ri)])
                            P.op('dve', lambda e, ri=ri, fc=fc: e.tensor_tensor(out=ut[:, fc, :], in0=rt[ri][:], in1=rt[ri][:], op=ALU.mult),
                                 reads=[('rt', ri)], writes=[('ut', fc)])
                    jobs.append((load, comp))
                for blk in range(8):
                    r0 = hf * 4096

                    def load(bi, blk=blk, r0=r0):
                        v = wb[bi][:].rearrange("p (fc c) -> p fc c", fc=32)
                        P.dma('pool', v, w_down[r0:r0 + 4096, blk * 256:(blk + 1) * 256].rearrange("(fc p) c -> p fc c", p=128), writes=[('wb', bi)])

                    def comp(bi, blk=blk, hf=hf, t=t, tsl=tsl, ht=ht, par=par):
                        v = wb[bi][:].rearrange("p (fc c) -> p fc c", fc=32)
                        for dcl in range(2):
                            dc = blk * 2 + dcl
                            pi = nextps()
                            for fc in range(32):
                                P.op('pe', lambda e, fc=fc, pi=pi, dcl=dcl: e.matmul(ps[pi][:], v[:, fc, dcl * 128:(dcl + 1) * 128], ut[:, fc, :], start=(fc == 0), stop=(fc == 31)),
                                     reads=[('wb', bi), ('ut', fc)], writes=[('ps', pi)])
                            P.op('dve', lambda e, pi=pi, dc=dc: e.tensor_tensor(out=ht[:, dc, :], in0=ps[pi][:], in1=ht[:, dc, :], op=ALU.add),
                                 reads=[('ps', pi), ('ht', par, dc)], writes=[('ht', par, dc)])
                        if hf == 1 and blk == 7:
                            P.dma('sp', hT[:, tsl].rearrange("(kc p) n -> p kc n", p=128), ht[:],
                                  reads=[('ht', par, dc) for dc in range(16)], writes=[('hTout', t)])
                    jobs.append((load, comp))

        nj = len(jobs)
        for i in range(min(2, nj)):
            jobs[i][0](i % 3)
        for i in range(nj):
            if i + 2 < nj:
                jobs[i + 2][0]((i + 2) % 3)
            jobs[i][1](i % 3)
        P.emit_stage()


IN_SPECS = [
    ("xT", [D, S]), ("g0", [128, 16]), ("Wfm0", [D, 5120]), ("Wtm0", [D, 2048]), ("Wf0", [D, 8]), ("bf0", [8, 1]),
    ("gains0", [128, 4]), ("ropeC", [32, S]), ("ropeS", [32, S]), ("Pm", [32, 32]), ("trim", [128, 128]),
    ("ident", [128, 128]), ("Mt", [128, 20 * 512]),
    ("w_out0", [D, D]), ("gm0", [128, 16]), ("w_up0", [D, F]), ("w_down0", [F, D]),
    ("g1", [128, 16]), ("Wfm1", [D, 4096]), ("Wtm1", [D, 1280]), ("gains1", [128, 4]),
    ("ropeCc", [32, 256]), ("ropeSc", [32, 256]),
    ("w1k", [4096, 128]), ("w2k", [128, 128]), ("peTk", [128, 32]), ("w1v", [4096, 128]), ("w2v", [128, 128]), ("peTv", [128, 32]),
    ("ovl", [256, 64]), ("cmpM", [256, S]), ("FT", [S, 64]), ("CT", [S, 64]), ("ExpT", [64, 32 * 128]), ("WB", [128, 8 * 512]),
    ("trib", [128, 128]),
    ("w_out1", [D, D]), ("gm1", [128, 16]), ("w_up1", [D, F]), ("w_down1", [F, D]),
]


def build_fused():
    nc = bass.Bass("TRN2", target_bir_lowering=False)
    T_ = {}
    for name, shape in IN_SPECS:
        T_[name] = nc.dram_tensor(name, shape, F32, kind="ExternalInput").ap()
    T_['o0T'] = nc.dram_tensor("o0T", [D, S], F32, kind="Internal").ap()
    T_['h2T'] = nc.dram_tensor("h2T", [D, S], F32, kind="Internal").ap()
    T_['o1T'] = nc.dram_tensor("o1T", [D, S], F32, kind="Internal").ap()
    hT = nc.dram_tensor("hT", [D, S], F32, kind="ExternalOutput").ap()
    P = Prog(nc)
    emit_mix0(nc, P, T_, NH=8)
    emit_phaseC(nc, P, "c0_", T_['xT'], T_['o0T'], T_['w_out0'], T_['gm0'], T_['w_up0'], T_['w_down0'], T_['h2T'], S)
    emit_mix1(nc, P, T_, NG=4)
    emit_phaseC(nc, P, "c1_", T_['h2T'], T_['o1T'], T_['w_out1'], T_['gm1'], T_['w_up1'], T_['w_down1'], hT, S)
    return nc


def host_shared_inputs(inp):
    def gl(v):
        return np.ascontiguousarray(v.reshape(16, 128).T)
    w = inp['even_w_in'][0]
    m = dict(
        g0=gl(inp['ln_mix_g'][0]),
        Wfm0=np.ascontiguousarray(np.concatenate([w[:, 0:1024], w[:, 1024:2048], w[:, 3072:4096], w[:, 4104:5128], w[:, 5128:6152]], 1)),
        Wtm0=np.ascontiguousarray(np.concatenate([w[:, 2048:3072], w[:, 6152:7176]], 1)),
        Wf0=np.ascontiguousarray(w[:, 4096:4104]),
        bf0=np.ascontiguousarray(inp['even_b_f'][0].reshape(8, 1)),
        gains0=np.ascontiguousarray(np.stack([inp['even_g_q_fox'][0], inp['even_g_k_fox'][0], inp['even_g_q_dil'][0], inp['even_g_k_dil'][0]], 1)),
        w_out0=inp['even_w_out'][0], gm0=gl(inp['ln_mlp_g'][0]), w_up0=inp['w_mlp_up'][0], w_down0=inp['w_mlp_down'][0],
    )
    w = inp['odd_w_in'][0]
    def grp(i):
        return w[:, 2048 + i * 512:2048 + (i + 1) * 512]
    kc, vc, ks, vs, kw, vw = [grp(i) for i in range(6)]
    gpad = np.zeros((D, 256), np.float32); gpad[:, :48] = w[:, 5120:5168]
    m.update(
        g1=gl(inp['ln_mix_g'][1]),
        Wfm1=np.ascontiguousarray(np.concatenate([w[:, 0:2048], ks, kw, kc, vc], 1)),
        Wtm1=np.ascontiguousarray(np.concatenate([vs, vw, gpad], 1)),
        gains1=np.ascontiguousarray(np.stack([inp['odd_g_q'][0], inp['odd_g_ks'][0], inp['odd_g_kw'][0], inp['odd_g_kc'][0]], 1)),
        w1k=inp['odd_phi_k_w1'][0], w2k=inp['odd_phi_k_w2'][0], peTk=np.ascontiguousarray(inp['odd_phi_k_pe'][0].T),
        w1v=inp['odd_phi_v_w1'][0], w2v=inp['odd_phi_v_w2'][0], peTv=np.ascontiguousarray(inp['odd_phi_v_pe'][0].T),
        w_out1=inp['odd_w_out'][0], gm1=gl(inp['ln_mlp_g'][1]), w_up1=inp['w_mlp_up'][1], w_down1=inp['w_mlp_down'][1],
    )
    m.update(host_consts())
    m.update(host_consts1())
    return m


def kernel(**inp):
    inp = {k: np.asarray(v) for k, v in inp.items()}
    x = inp['x']
    B = x.shape[0]
    cores = list(range(8))
    shared = host_shared_inputs(inp)
    nc = build_fused()
    maps = []
    for c in cores:
        m = dict(shared)
        m['xT'] = np.ascontiguousarray(x[c // 2].T)
        maps.append({name: np.ascontiguousarray(m[name], dtype=np.float32) for name, _ in IN_SPECS})
    res = run_bass_kernel_spmd(nc, maps, core_ids=cores).results
    return np.stack([np.ascontiguousarray(np.asarray(res[2 * b]["hT"]).T) for b in range(B)], axis=0).astype(np.float32)
```
